# Optimizing a Trainium2 kernel written in Bass

```python
import math
import jax, jax.numpy as jnp
from jax import lax
import numpy as np

D_MODEL = 1024
BATCH = 2
SEQ = 8192
DEPTH = 1

CHUNK = 64
BAND_PREV = 8
BAND = (BAND_PREV + 1) * CHUNK
REL_CLIP = 256
N_REL = REL_CLIP + CHUNK

W_A = D_MODEL // 2
HEAD_A = 64
H_A = W_A // HEAD_A
W_B = D_MODEL // 2
HEAD_B = 64
H_B = W_B // HEAD_B
DECAY_LORA = 64
A_LORA = 64
GATE_LORA = 128
D_FF = 4 * D_MODEL

ATT_COLS = 3 * W_A
RWKV_COLS = 3 * W_B + DECAY_LORA + A_LORA + GATE_LORA
GATE_COLS = 2 * D_MODEL
IN_COLS = ATT_COLS + RWKV_COLS + GATE_COLS

RMS_EPS = 1e-6
GN_EPS = HEAD_B * 1e-5
NEG_INF = -1e30

kernel_name = "chunked_attn_rwkv7_gated_hybrid"


def rms_norm(x, g):
    xf = x.astype(jnp.float32)
    y = xf * lax.rsqrt(jnp.mean(xf * xf, axis=-1, keepdims=True) + RMS_EPS)
    return (y * g.astype(jnp.float32)).astype(x.dtype)


def chunk_band_attention(q, k, v, rel_bias):
    b, s, h, d = q.shape
    nc = s // CHUNK
    qc = q.reshape(b, nc, CHUNK, h, d)
    pad = ((0, 0), (BAND_PREV * CHUNK, 0), (0, 0), (0, 0))
    kp = jnp.pad(k, pad).reshape(b, nc + BAND_PREV, CHUNK, h, d)
    vp = jnp.pad(v, pad).reshape(b, nc + BAND_PREV, CHUNK, h, d)
    kb = jnp.concatenate([kp[:, o:o + nc] for o in range(BAND_PREV + 1)], axis=2)
    vb = jnp.concatenate([vp[:, o:o + nc] for o in range(BAND_PREV + 1)], axis=2)
    scores = jnp.einsum('bnqhd,bnkhd->bnhqk', qc, kb).astype(jnp.float32) * (d ** -0.5)
    qi = jnp.arange(CHUNK)[:, None]
    kj = jnp.arange(BAND)[None, :]
    dist = qi + BAND_PREV * CHUNK - kj
    idx = jnp.clip(dist, -(CHUNK - 1), REL_CLIP) + (CHUNK - 1)
    bias = rel_bias.astype(jnp.float32)[:, idx]
    valid = (jnp.arange(nc)[:, None] - BAND_PREV + kj // CHUNK) >= 0
    scores = jnp.where(valid[None, :, None, None, :], scores + bias[None, None], NEG_INF)
    p = jax.nn.softmax(scores, axis=-1)
    out = jnp.einsum('bnhqk,bnkhd->bnqhd', p.astype(vb.dtype), vb)
    return out.reshape(b, s, h * d)


def rwkv7_time_mix(p, shift_mu, w0, w2, a0, a2, g2, k_k, k_a, r_k, ln_x_w, ln_x_b):
    b, s, _ = p.shape
    prev = jnp.pad(p, ((0, 0), (1, 0), (0, 0)))[:, :-1]
    p = p + (prev - p) * shift_mu
    splits = [W_B, 2 * W_B, 3 * W_B, 3 * W_B + DECAY_LORA, 3 * W_B + DECAY_LORA + A_LORA]
    r, k, v, cw, ca, cg = jnp.split(p, splits, axis=-1)
    w_log = -jax.nn.softplus(-(w0 + jnp.tanh(cw) @ w2)) - 0.5
    decay = jnp.exp(-jnp.exp(w_log.astype(jnp.float32)))
    a = jax.nn.sigmoid(a0 + ca @ a2)
    g = jax.nn.sigmoid(cg) @ g2

    def heads(t):
        return t.reshape(b, s, H_B, HEAD_B).astype(jnp.float32)

    kk = heads(k * k_k)
    kk = kk / jnp.maximum(jnp.sqrt(jnp.sum(kk * kk, axis=-1, keepdims=True)), 1e-12)
    k = k * (1.0 + (a - 1.0) * k_a)
    r_h, k_h, v_h, a_h, w_h = heads(r), heads(k), heads(v), heads(a), heads(decay)

    def step(state, inp):
        r_t, w_t, k_t, v_t, aa_t, bb_t = inp
        sa = jnp.einsum('bhij,bhj->bhi', state, aa_t)
        state = (state * w_t[:, :, None, :] + sa[..., None] * bb_t[:, :, None, :]
                 + v_t[..., None] * k_t[:, :, None, :])
        y = jnp.einsum('bhij,bhj->bhi', state, r_t)
        return state, y

    xs = tuple(jnp.moveaxis(t, 1, 0) for t in (r_h, w_h, k_h, v_h, -kk, kk * a_h))
    init = jnp.zeros((b, H_B, HEAD_B, HEAD_B), jnp.float32)
    _, ys = lax.scan(step, init, xs)
    y = jnp.moveaxis(ys, 0, 1)
    mu = jnp.mean(y, axis=-1, keepdims=True)
    var = jnp.mean(jnp.square(y - mu), axis=-1, keepdims=True)
    yn = (y - mu) * lax.rsqrt(var + GN_EPS)
    gn_w = ln_x_w.reshape(H_B, HEAD_B).astype(jnp.float32)
    gn_b = ln_x_b.reshape(H_B, HEAD_B).astype(jnp.float32)
    yn = yn * gn_w + gn_b
    bonus = jnp.sum(r_h * k_h * r_k.astype(jnp.float32), axis=-1, keepdims=True) * v_h
    out = (yn + bonus) * heads(g)
    return out.reshape(b, s, W_B).astype(p.dtype)


def setup_inputs(seed: int = 0) -> dict:
    key = jax.random.key(seed)
    ks = jax.random.split(key, 24)
    L = DEPTH

    def nrm(k, shape, scale):
        return jax.random.normal(k, shape, jnp.float32) * scale

    return {
        "x": nrm(ks[0], (BATCH, SEQ, D_MODEL), 1.0),
        "pre_mix_g": 1.0 + nrm(ks[1], (L, D_MODEL), 0.05),
        "w_in": nrm(ks[2], (L, D_MODEL, IN_COLS), D_MODEL ** -0.5),
        "gate_bias": nrm(ks[3], (L, GATE_COLS), 0.1),
        "rel_bias": nrm(ks[4], (L, H_A, N_REL), 0.5),
        "shift_mu": jax.random.uniform(ks[5], (L, RWKV_COLS), jnp.float32),
        "w0": jax.random.uniform(ks[6], (L, W_B), jnp.float32, -5.0, 0.5),
        "w2": nrm(ks[7], (L, DECAY_LORA, W_B), 0.1),
        "a0": nrm(ks[8], (L, W_B), 0.1),
        "a2": nrm(ks[9], (L, A_LORA, W_B), A_LORA ** -0.5),
        "g2": nrm(ks[10], (L, GATE_LORA, W_B), GATE_LORA ** -0.5),
        "k_k": 0.85 + nrm(ks[11], (L, W_B), 0.05),
        "k_a": 1.0 + nrm(ks[12], (L, W_B), 0.05),
        "r_k": nrm(ks[13], (L, H_B, HEAD_B), 0.1),
        "ln_x_w": 1.0 + nrm(ks[14], (L, W_B), 0.05),
        "ln_x_b": nrm(ks[15], (L, W_B), 0.05),
        "proj_a": nrm(ks[16], (L, W_A, D_MODEL), W_A ** -0.5),
        "proj_b": nrm(ks[17], (L, W_B, D_MODEL), W_B ** -0.5),
        "w_out": nrm(ks[18], (L, D_MODEL, D_MODEL), D_MODEL ** -0.5),
        "post_mix_g": 1.0 + nrm(ks[19], (L, D_MODEL), 0.05),
        "pre_ffn_g": 1.0 + nrm(ks[20], (L, D_MODEL), 0.05),
        "w_up": nrm(ks[21], (L, D_MODEL, D_FF), D_MODEL ** -0.5),
        "w_down": nrm(ks[22], (L, D_FF, D_MODEL), D_FF ** -0.5),
        "post_ffn_g": 1.0 + nrm(ks[23], (L, D_MODEL), 0.05),
    }


def reference(x, pre_mix_g, w_in, gate_bias, rel_bias, shift_mu, w0, w2, a0, a2, g2,
              k_k, k_a, r_k, ln_x_w, ln_x_b, proj_a, proj_b, w_out, post_mix_g,
              pre_ffn_g, w_up, w_down, post_ffn_g):
    b, s, _ = x.shape
    for l in range(DEPTH):
        h = rms_norm(x, pre_mix_g[l])
        proj = h @ w_in[l]
        p_att = proj[..., :ATT_COLS]
        p_rwkv = proj[..., ATT_COLS:ATT_COLS + RWKV_COLS]
        p_gate = proj[..., ATT_COLS + RWKV_COLS:] + gate_bias[l]

        q, k, v = jnp.split(p_att, 3, axis=-1)
        shp = (b, s, H_A, HEAD_A)
        y_a = chunk_band_attention(q.reshape(shp), k.reshape(shp), v.reshape(shp), rel_bias[l])
        y_b = rwkv7_time_mix(p_rwkv, shift_mu[l], w0[l], w2[l], a0[l], a2[l], g2[l],
                             k_k[l], k_a[l], r_k[l], ln_x_w[l], ln_x_b[l])

        gate_a = jax.nn.sigmoid(p_gate[..., :D_MODEL])
        gate_b = jax.nn.sigmoid(p_gate[..., D_MODEL:])
        merged = gate_a * (y_a @ proj_a[l]) + gate_b * (y_b @ proj_b[l])
        x = x + rms_norm(merged @ w_out[l], post_mix_g[l])

        hf = rms_norm(x, pre_ffn_g[l])
        u = jnp.square(jax.nn.relu(hf @ w_up[l])) @ w_down[l]
        x = x + rms_norm(u, post_ffn_g[l])
    return x
```

```python
import numpy as np
import ml_dtypes
import concourse.bass as bass
import concourse.mybir as mybir
from concourse.bass_utils import run_bass_kernel_spmd
from contextlib import ExitStack

F32 = mybir.dt.float32
BF16 = mybir.dt.bfloat16
AF = mybir.ActivationFunctionType
ALU = mybir.AluOpType

D = 1024
N_DMA_SEMS = 24
RMS_EPS = 1e-6
GN_EPS = 64 * 1e-5
DEC_C = float(np.exp(-0.5))


class _Rec:
    def __getattr__(self, name):
        def f(*a, **k):
            self.call = (name, a, k)
        return f


class Sched:
    ENGS = ("pe", "act", "dve", "pool", "sp")

    def __init__(self, nc, stack):
        self.nc = nc
        self.prog = {e: [] for e in self.ENGS}
        self.cnt = {e: 0 for e in self.ENGS}
        self.sems = {}
        for e in self.ENGS:
            self.sems["p_" + e] = stack.enter_context(nc.semaphore("prog_" + e))
        for i in range(N_DMA_SEMS):
            self.sems["d%d" % i] = stack.enter_context(nc.semaphore("dma%d" % i))
        self.dcnt = [0] * N_DMA_SEMS
        self.dnx = {"sp": 0, "pool": 0, "act": 0}
        self.seen = {e: {} for e in self.ENGS}
        self.lastw = {}
        self.reads = {}

    def _deps(self, eng, reads, writes):
        deps = {}
        for b in reads:
            for s, v in self.lastw.get(b, {}).items():
                if deps.get(s, 0) < v:
                    deps[s] = v
        for b in writes:
            for s, v in self.lastw.get(b, {}).items():
                if deps.get(s, 0) < v:
                    deps[s] = v
            for s, v in self.reads.get(b, {}).items():
                if deps.get(s, 0) < v:
                    deps[s] = v
        waits = []
        for s, v in deps.items():
            if s == "p_pe" and eng == "pe":
                continue
            if self.seen[eng].get(s, 0) < v:
                self.seen[eng][s] = v
                waits.append((s, v))
        return waits

    def _commit(self, tok, reads, writes):
        s, v = tok
        for b in writes:
            self.lastw.setdefault(b, {})[s] = v
        for b in reads:
            self.reads.setdefault(b, {})[s] = v

    @staticmethod
    def _rec(fn):
        r = _Rec()
        fn(r)
        return r.call

    def op(self, eng, fn, reads=(), writes=()):
        fn = self._rec(fn)
        waits = self._deps(eng, reads, writes)
        self.cnt[eng] += 1
        tok = ("p_" + eng, self.cnt[eng])
        self._commit(tok, reads, writes)
        self.prog[eng].append((waits, fn, ("p_" + eng, 1)))

    def dma(self, eng, fn, reads=(), writes=()):
        fn = self._rec(fn)
        half = N_DMA_SEMS // 2
        base = 0 if eng == "sp" else half
        i = base + self.dnx[eng]
        self.dnx[eng] = (self.dnx[eng] + 1) % half
        waits = self._deps(eng, reads, writes)
        s = "d%d" % i
        if self.dcnt[i] > 0 and self.seen[eng].get(s, 0) < self.dcnt[i]:
            self.seen[eng][s] = self.dcnt[i]
            waits.append((s, self.dcnt[i]))
        self.dcnt[i] += 16
        self._commit((s, self.dcnt[i]), reads, writes)
        self.prog[eng].append((waits, fn, (s, 16)))

    def barrier(self):
        cur = {"p_" + e: self.cnt[e] for e in self.ENGS}
        for i in range(N_DMA_SEMS):
            cur["d%d" % i] = self.dcnt[i]
        for eng in self.ENGS:
            waits = []
            for s_, v in cur.items():
                if s_ == "p_" + eng and eng == "pe":
                    continue
                if v > 0 and self.seen[eng].get(s_, 0) < v:
                    self.seen[eng][s_] = v
                    waits.append((s_, v))
            self.prog[eng].append((waits, None, None))

    def wait_all(self, eng, bufs):
        waits = self._deps(eng, bufs, ())
        self.prog[eng].append((waits, None, None))

    def emit(self):
        nc = self.nc
        with nc.Block() as block:
            def replay(name, e):
                for waits, fn, inc in self.prog[name]:
                    for s, v in waits:
                        e.wait_ge(self.sems[s], v)
                    if fn is None:
                        continue
                    getattr(e, fn[0])(*fn[1], **fn[2]).then_inc(self.sems[inc[0]], inc[1])

            @block.tensor
            def _(e):
                replay("pe", e)

            @block.scalar
            def _(e):
                replay("act", e)

            @block.vector
            def _(e):
                replay("dve", e)

            @block.gpsimd
            def _(e):
                replay("pool", e)

            @block.sync
            def _(e):
                replay("sp", e)


import os
STAGE = int(os.environ.get("STAGE", "9"))
SUB = int(os.environ.get("SUB", "99"))
NTL = int(os.environ.get("NTL", "9999"))
SUB2 = int(os.environ.get("SUB2", "99"))


def build_nc(WIN, OWN, dbg=False):
    NTW = WIN // 128
    NTO = OWN // 128
    NTA = NTO + 4
    NG = OWN // 512
    nc = bass.Bass("TRN2", target_bir_lowering=False)

    def din(name, shape, dt=F32):
        return nc.dram_tensor(name, list(shape), dt, kind="ExternalInput").ap()

    xw = din("xw", [WIN, D])
    w_in = din("w_in", [D, 5376])
    proj_a = din("proj_a", [512, D])
    proj_b = din("proj_b", [512, D])
    w_out = din("w_out", [D, D])
    w_up = din("w_up", [D, 4096])
    w_down = din("w_down", [4096, D])
    cvec = din("cvec", [128, 64])
    rows = din("rows", [2, D])
    w2d = din("w2", [64, 512])
    a2d = din("a2", [64, 512])
    g2d = din("g2", [128, 512])
    biasd = din("biasT", [128, 5 * 8 * 128])
    validd = din("valid", [128, NTA])
    cmat = din("cmat", [128, 6 * 128])
    out = nc.dram_tensor("out", [OWN, D], F32, kind="ExternalOutput").ap()
    dbg_o = {}

    w_in_b = nc.dram_tensor("w_in_b", [D, 5376], BF16).ap()
    pa_b = nc.dram_tensor("pa_b", [512, D], BF16).ap()
    pb_b = nc.dram_tensor("pb_b", [512, D], BF16).ap()
    wo_b = nc.dram_tensor("wo_b", [D, D], BF16).ap()
    wu_b = nc.dram_tensor("wu_b", [D, 4096], BF16).ap()
    wd_b = nc.dram_tensor("wd_b", [4096, D], BF16).ap()
    x1s = nc.dram_tensor("x1s", [OWN, D], F32).ap()

    C_G1, C_G3, C_GB, C_MU = 0, 8, 16, 32
    C_W0, C_A0, C_KK, C_KA = 46, 50, 54, 58
    cvec2 = din("cvec2", [128, 16])
    C2_RK, C2_LW, C2_LB, C2_OMKA = 0, 4, 8, 12

    with ExitStack() as st:
        S = Sched(nc, st)

        def sb(stack, name, shape, dt):
            return stack.enter_context(nc.sbuf_tensor(name, list(shape), dt))

        def psb(stack, name, shape, dt=F32):
            return stack.enter_context(nc.psum_tensor(name, list(shape), dt))

        cv = sb(st, "cv", [128, 64], F32)
        cv2 = sb(st, "cv2", [128, 16], F32)
        cm = sb(st, "cm", [128, 768], F32)
        idb = sb(st, "idb", [128, 128], BF16)
        mskb = sb(st, "mskb", [64, 3, 64], BF16)
        S.dma("sp", lambda e: e.dma_start(out=cv[:], in_=cvec), writes=["cv"])
        S.dma("sp", lambda e: e.dma_start(out=cv2[:], in_=cvec2), writes=["cv2"])
        S.dma("sp", lambda e: e.dma_start(out=cm[:], in_=cmat), writes=["cm"])
        ident = cm[:, 0:128]
        bones = cm[:, 128:256]
        scanm = cm[:, 640:768]
        S.op("dve", lambda e: e.tensor_copy(idb[:], ident), reads=["cm"], writes=["idb"])
        S.op("dve", lambda e: e.tensor_copy(mskb[:], cm[0:64, 256:640].rearrange("p (a b) -> p a b", b=128)[:, :, 0:64]),
             reads=["cm"], writes=["mskb"])

        def conv(dst, src, nrows, nm):
            for r in range(0, nrows, 128):
                S.dma("pool", lambda e, r=r: e.dma_start(out=dst[r:r + 128, :], in_=src[r:r + 128, :]), writes=[nm])
        conv(w_in_b, w_in, D, "w_in_b")
        conv(pa_b, proj_a, 512, "pa_b")
        conv(pb_b, proj_b, 512, "pb_b")
        conv(wo_b, w_out, D, "wo_b")
        conv(wu_b, w_up, D, "wu_b")
        conv(wd_b, w_down, 4096, "wd_b")

        hT_att = sb(st, "hT_att", [128, 8, NTA * 128], BF16)
        y_bT = sb(st, "y_bT", [128, 4, OWN], BF16)
        xt = [sb(st, "xt%d" % i, [128, D], F32) for i in range(2)]
        junk = sb(st, "junk", [128, D], BF16)
        xs = sb(st, "xs", [128, D], BF16)
        ssq = sb(st, "ssq", [128, 4], F32)

        banks = [psb(st, "bank%d" % i, [128, 512]) for i in range(8)]

        def bank_bf(i):
            return banks[i][:].bitcast(BF16)

        def norm_transpose(src_tile, srcname, gcol, dst_ap, dstname, pbank):
            S.op("act", lambda e: e.activation(junk[:], src_tile, AF.Square, accum_out=ssq[:, 0:1]),
                 reads=[srcname], writes=["junk", "ssq"])
            S.op("act", lambda e: e.activation(ssq[:, 1:2], ssq[:, 0:1], AF.Sqrt, bias=RMS_EPS, scale=1.0 / D),
                 reads=["ssq"], writes=["ssq"])
            S.op("dve", lambda e: e.reciprocal(ssq[:, 2:3], ssq[:, 1:2]), reads=["ssq"], writes=["ssq"])
            S.op("dve", lambda e: e.tensor_scalar(xs[:], src_tile, ssq[:, 2:3], 0.0, ALU.mult, ALU.add),
                 reads=[srcname, "ssq"], writes=["xs"])
            pb = bank_bf(pbank)
            for k in range(8):
                S.op("pe", lambda e, k=k: e.transpose(pb[:, k * 128:(k + 1) * 128], xs[:, k * 128:(k + 1) * 128], idb[:]),
                     reads=["xs", "idb"], writes=["bank%d" % pbank])
            S.op("dve", lambda e: e.tensor_tensor(dst_ap, pb[:, 0:1024].rearrange("p (k t) -> p k t", t=128),
                                                 cv[:, gcol:gcol + 8].unsqueeze(2).broadcast_to([128, 8, 128]), ALU.mult),
                 reads=["bank%d" % pbank, "cv"], writes=[dstname])

        with ExitStack() as p1:
          if STAGE >= 1:
            wr = sb(p1, "wr", [128, 8, 1792], BF16)
            S.dma("sp", lambda e: e.dma_start(out=wr[:], in_=w_in_b.rearrange("(k p) n -> p k n", p=128)[:, :, 1536:3328]),
                  reads=["w_in_b"], writes=["wr"])
            w2s = sb(p1, "w2s", [64, 512], F32)
            a2s = sb(p1, "a2s", [128, 512], F32)
            g2s = sb(p1, "g2s", [128, 512], BF16)
            S.dma("sp", lambda e: e.dma_start(out=w2s[:], in_=w2d), writes=["w2s"])
            S.dma("sp", lambda e: e.dma_start(out=a2s[64:128, :], in_=a2d), writes=["a2s"])
            S.dma("pool", lambda e: e.dma_start(out=g2s[:], in_=g2d), writes=["g2s"])
            hTr = [sb(p1, "hTr%d" % i, [128, 8, 128], BF16) for i in range(2)]
            PR = [sb(p1, "PR%d" % i, [128, 14, 129], F32) for i in range(2)]
            S.op("pool", lambda e: e.memset(PR[1][:], 0.0), writes=["PR1"])
            PS = sb(p1, "PS", [128, 14, 128], F32)
            TD = sb(p1, "TD", [128, 14, 128], F32)

            def f32t(name):
                return sb(p1, name, [128, 4, 128], F32)
            tcw = sb(p1, "tcw", [64, 128], F32)
            cmask4 = sb(p1, "cmask4", [128, 512], F32)
            S.op("pool", lambda e: e.memset(cmask4[:], 1.0), writes=["cmask4"])
            S.op("pool", lambda e: e.memset(cmask4[:].rearrange("p (c t) -> p c t", t=64)[:, :, 0:1], 0.0), writes=["cmask4"])
            S.op("pool", lambda e: e.memset(PR[0][:], 0.0), writes=["PR0"])
            S.op("dve", lambda e: e.tensor_scalar(cv2[:, 12:16], cv[:, C_KA:C_KA + 4], -1.0, 1.0, ALU.mult, ALU.add), reads=["cv"], writes=["cv2"])
            sgc = sb(p1, "sgc", [128, 128], BF16)
            lw, ar, kk, k2, rn, kp, bq, cum, e1, e2, e3, t1, gT, rk, yv, dv, d2 = [f32t(n) for n in
                ("lw", "ar", "kk", "k2", "rn", "kp", "bq", "cum", "e1", "e2", "e3", "t1", "gT", "rk", "yv", "dv", "d2")]
            ARt = sb(p1, "ARt", [128, 4, 2, 2, 64], BF16)
            AT = sb(p1, "AT", [128, 4, 128], BF16)
            BT = sb(p1, "BT", [128, 4, 128], BF16)
            KT = sb(p1, "KT", [128, 4, 128], BF16)
            RT = sb(p1, "RT", [128, 4, 128], BF16)
            VT = sb(p1, "VT", [128, 4, 128], BF16)
            gam = sb(p1, "gam", [64, 8, 2], F32)
            TM = sb(p1, "TM", [64, 4, 512], BF16)
            Vpad = sb(p1, "Vpad", [64, 8, 128], BF16)
            Wpad = sb(p1, "Wpad", [64, 8, 128], BF16)
            Hpad = sb(p1, "Hpad", [64, 8, 128], BF16)
            SC1 = sb(p1, "SC1", [64, 8, 128], BF16)
            SC2 = sb(p1, "SC2", [64, 8, 128], BF16)
            SC3 = sb(p1, "SC3", [64, 8, 64], BF16)
            Zb = sb(p1, "Zb", [64, 8, 128], BF16)
            Pq = [sb(p1, "Pq%d" % i, [64, 8, 64], BF16) for i in range(2)]
            PTq = [sb(p1, "PTq%d" % i, [64, 8, 64], BF16) for i in range(2)]
            Ef = sb(p1, "Ef", [64, 8, 64], F32)
            Hf = sb(p1, "Hf", [64, 8, 64], F32)
            QTb = sb(p1, "QTb", [64, 8, 64], BF16)
            for t_, n_ in ((Vpad, "Vpad"), (Wpad, "Wpad"), (Hpad, "Hpad"), (Hf, "Hf")):
                S.op("pool", lambda e, t_=t_: e.memset(t_[:], 0.0), writes=[n_])

            def padview(t):
                return bass.AP(t[:].tensor, t[:].offset, [list(t[:].ap[0]), [256, 4], [192, 2], [1, 64]])

            for i in range(min(NTW, NTL)):
                own = i >= NTW - NTO
                att = i >= NTW - NTA
                io = i - (NTW - NTO)
                ia = i - (NTW - NTA)
                b = i % 2
                ntl = 14 if i >= NTW - NTO - 1 else 9
                colmap = [512 + c * 128 for c in range(4)] + [1024 + c * 128 for c in range(4)] + [1536] + \
                         [c * 128 for c in range(4)] + [1664]
                S.dma("sp", lambda e, b=b, i=i: e.dma_start(out=xt[b][:], in_=xw[i * 128:(i + 1) * 128, :]), writes=["xt%d" % b])
                if att:
                    hsrc = hT_att[:, :, ia * 128:(ia + 1) * 128]
                    hname = "hT_att"
                else:
                    hsrc = hTr[b][:]
                    hname = "hTr%d" % b
                norm_transpose(xt[b][:], "xt%d" % b, C_G1, hsrc, hname, 0)
                if SUB < 2:
                    continue
                for g0 in range(0, ntl, 4):
                    gn = min(4, ntl - g0)
                    pbk = 1 + (g0 // 4) % 2
                    for c in range(gn):
                        co = colmap[g0 + c]
                        for k in range(8):
                            S.op("pe", lambda e, c=c, co=co, k=k, pbk=pbk: e.matmul(
                                banks[pbk][:, c * 128:(c + 1) * 128], wr[:, k, co:co + 128], hsrc[:, k, :],
                                start=(k == 0), stop=(k == 7)), reads=["wr", hname], writes=["bank%d" % pbk])
                    S.op("act", lambda e, g0=g0, gn=gn, pbk=pbk, b=b: e.copy(
                        PR[b][:, g0:g0 + gn, 1:129], banks[pbk][:, 0:gn * 128].rearrange("p (c t) -> p c t", t=128)),
                        reads=["bank%d" % pbk], writes=["PR%d" % b])
                S.op("pool", lambda e, b=b: e.tensor_copy(PR[b][:, :, 0:1], PR[1 - b][:, :, 128:129]),
                     reads=["PR%d" % (1 - b)], writes=["PR%d" % b])
                S.op("pool", lambda e, b=b: e.tensor_tensor(TD[:, 0:ntl, :], PR[b][:, 0:ntl, 0:128], PR[b][:, 0:ntl, 1:129], ALU.subtract),
                     reads=["PR%d" % b], writes=["TD"])
                S.op("pool", lambda e: e.tensor_tensor(TD[:, 0:ntl, :], TD[:, 0:ntl, :],
                                                     cv[:, C_MU:C_MU + ntl].unsqueeze(2).broadcast_to([128, ntl, 128]), ALU.mult),
                     reads=["TD", "cv"], writes=["TD"])
                S.op("pool", lambda e, b=b: e.tensor_tensor(PS[:, 0:ntl, :], TD[:, 0:ntl, :], PR[b][:, 0:ntl, 1:129], ALU.add),
                     reads=["TD", "PR%d" % b], writes=["PS"])
                kS, vS, loS, rS, cgS = PS[:, 0:4, :], PS[:, 4:8, :], PS[:, 8, :], PS[:, 9:13, :], PS[:, 13, :]
                if SUB < 3:
                    continue
                S.op("act", lambda e: e.activation(tcw[:], PS[0:64, 8, :], AF.Tanh), reads=["PS"], writes=["tcw"])
                for p in range(4):
                    S.op("pe", lambda e, p=p: e.matmul(banks[3][:, p * 128:(p + 1) * 128], w2s[:, p * 128:(p + 1) * 128], tcw[:],
                                                      start=True, stop=True), reads=["w2s", "tcw"], writes=["bank3"])
                    S.op("pe", lambda e, p=p: e.matmul(banks[4][:, p * 128:(p + 1) * 128], a2s[64:128, p * 128:(p + 1) * 128], PS[64:128, 8, :],
                                                      start=True, stop=True), reads=["a2s", "PS"], writes=["bank4"])
                for p in range(4):
                    S.op("act", lambda e, p=p: e.activation(lw[:, p, :], banks[3][:, p * 128:(p + 1) * 128], AF.Sigmoid,
                                                           bias=cv[:, C_W0 + p:C_W0 + p + 1]), reads=["bank3", "cv"], writes=["lw"])
                    S.op("act", lambda e, p=p: e.activation(ar[:, p, :], banks[4][:, p * 128:(p + 1) * 128], AF.Sigmoid,
                                                           bias=cv[:, C_A0 + p:C_A0 + p + 1]), reads=["bank4", "cv"], writes=["ar"])
                S.op("dve", lambda e: e.tensor_scalar(lw[:], lw[:], -DEC_C, 0.0, ALU.mult, ALU.add), reads=["lw"], writes=["lw"])
                S.op("dve", lambda e: e.tensor_tensor_scan(cum[:].rearrange("p a t -> p (a t)"), cmask4[:],
                                                         lw[:].rearrange("p a t -> p (a t)"), 0.0, ALU.mult, ALU.add),
                     reads=["lw", "cmask4"], writes=["cum"])
                S.op("act", lambda e: e.activation(e1[:], cum[:], AF.Exp), reads=["cum"], writes=["e1"])
                S.op("act", lambda e: e.activation(e2[:], cum[:], AF.Exp, scale=-1.0), reads=["cum"], writes=["e2"])
                S.op("pool", lambda e: e.tensor_tensor(e3[:], cum[:], lw[:], ALU.subtract), reads=["cum", "lw"], writes=["e3"])
                S.op("act", lambda e: e.activation(e3[:], e3[:], AF.Exp), reads=["e3"], writes=["e3"])
                S.op("dve", lambda e: e.tensor_tensor(kk[:], kS, cv[:, C_KK:C_KK + 4].unsqueeze(2).broadcast_to([128, 4, 128]), ALU.mult),
                     reads=["PS", "cv"], writes=["kk"])
                S.op("pool", lambda e: e.tensor_tensor(k2[:], kk[:], kk[:], ALU.mult), reads=["kk"], writes=["k2"])
                S.op("pe", lambda e: e.matmul(banks[5][:], bones, k2[:].rearrange("p a t -> p (a t)"), start=True, stop=True),
                     reads=["cm", "k2"], writes=["bank5"])
                S.op("act", lambda e: e.activation(rn[:].rearrange("p a t -> p (a t)"), banks[5][:], AF.Sqrt), reads=["bank5"], writes=["rn"])
                S.op("dve", lambda e: e.tensor_scalar(rn[:], rn[:], 1e-12, 0.0, ALU.max, ALU.add), reads=["rn"], writes=["rn"])
                S.op("dve", lambda e: e.reciprocal(rn[:], rn[:]), reads=["rn"], writes=["rn"])
                S.op("dve", lambda e: e.tensor_tensor(kk[:], kk[:], rn[:], ALU.mult), reads=["kk", "rn"], writes=["kk"])
                S.op("pool", lambda e: e.tensor_tensor(t1[:], ar[:], cv[:, C_KA:C_KA + 4].unsqueeze(2).broadcast_to([128, 4, 128]), ALU.mult),
                     reads=["ar", "cv"], writes=["t1"])
                S.op("pool", lambda e: e.tensor_tensor(t1[:], t1[:], cv2[:, C2_OMKA:C2_OMKA + 4].unsqueeze(2).broadcast_to([128, 4, 128]), ALU.add),
                     reads=["t1", "cv2"], writes=["t1"])
                S.op("pool", lambda e: e.tensor_tensor(kp[:], kS, t1[:], ALU.mult), reads=["PS", "t1"], writes=["kp"])
                S.op("dve", lambda e: e.tensor_tensor(bq[:], kk[:], ar[:], ALU.mult), reads=["kk", "ar"], writes=["bq"])
                ARv = ARt[:].rearrange("p a c r t -> p a c (r t)")
                S.op("dve", lambda e: e.scalar_tensor_tensor(ARt[:, :, :, 0, :], kk[:].rearrange("p a (c t) -> p a c t", t=64), -1.0,
                                                            e3[:].rearrange("p a (c t) -> p a c t", t=64), ALU.mult, ALU.mult),
                     reads=["kk", "e3"], writes=["ARt"])
                S.op("dve", lambda e: e.tensor_copy(AT[:].rearrange("p a (c t) -> p a c t", t=64), ARt[:, :, :, 0, :]), reads=["ARt"], writes=["AT"])
                S.op("dve", lambda e: e.tensor_tensor(BT[:], bq[:], e2[:], ALU.mult), reads=["bq", "e2"], writes=["BT"])
                S.op("pool", lambda e: e.tensor_tensor(KT[:], kp[:], e2[:], ALU.mult), reads=["kp", "e2"], writes=["KT"])
                S.op("pool", lambda e: e.tensor_copy(VT[:], vS), reads=["PS"], writes=["VT"])
                if own:
                    S.op("dve", lambda e: e.tensor_tensor(ARt[:, :, :, 1, :], rS.rearrange("p a (c t) -> p a c t", t=64),
                                                         e1[:].rearrange("p a (c t) -> p a c t", t=64), ALU.mult),
                         reads=["PS", "e1"], writes=["ARt"])
                    S.op("dve", lambda e: e.tensor_copy(RT[:].rearrange("p a (c t) -> p a c t", t=64), ARt[:, :, :, 1, :]), reads=["ARt"], writes=["RT"])
                    S.op("act", lambda e: e.activation(sgc[:], cgS, AF.Sigmoid), reads=["PS"], writes=["sgc"])
                    for p in range(4):
                        S.op("pe", lambda e, p=p: e.matmul(banks[6][:, p * 128:(p + 1) * 128], g2s[:, p * 128:(p + 1) * 128], sgc[:],
                                                          start=True, stop=True), reads=["g2s", "sgc"], writes=["bank6"])
                    S.op("act", lambda e: e.copy(gT[:].rearrange("p a t -> p (a t)"), banks[6][:]), reads=["bank6"], writes=["gT"])
                    S.op("pool", lambda e: e.tensor_tensor(rk[:], rS, kp[:], ALU.mult), reads=["PS", "kp"], writes=["rk"])
                    S.op("pool", lambda e: e.tensor_tensor(rk[:], rk[:], cv2[:, C2_RK:C2_RK + 4].unsqueeze(2).broadcast_to([128, 4, 128]), ALU.mult),
                         reads=["rk", "cv2"], writes=["rk"])
                if SUB < 4:
                    continue
                for h in range(8):
                    p, q = h // 2, h % 2
                    S.op("pe", lambda e, h=h, p=p, q=q: e.matmul(
                        banks[7][0:64, h * 2:h * 2 + 2], cm[:, 64 * q:64 * q + 64],
                        e1[:, p, :].rearrange("p (c t) -> p c t", t=64)[:, :, 63], start=True, stop=True),
                        reads=["cm", "e1"], writes=["bank7"])
                S.op("dve", lambda e: e.tensor_copy(gam[:].rearrange("p h c -> p (h c)"), banks[7][0:64, 0:16]), reads=["bank7"], writes=["gam"])

                if SUB < 5:
                    continue
                for c in range(2):
                    cs = slice(c * 64, (c + 1) * 64)
                    for qi_, (src, sn) in enumerate(((AT, "AT"), (BT, "BT"), (KT, "KT"), (VT, "VT"))):
                        for p in range(4):
                            S.op("pe", lambda e, qi_=qi_, src=src, p=p: e.matmul(
                                banks[qi_][0:64, p * 128:(p + 1) * 128], src[:, p, cs], idb[:], start=True, stop=True),
                                reads=[sn, "idb"], writes=["bank%d" % qi_])
                    for qi_ in range(4):
                        eng = "act" if qi_ % 2 == 0 else "dve"
                        if eng == "act":
                            S.op("act", lambda e, qi_=qi_: e.copy(TM[:, qi_, :], banks[qi_][0:64, :]), reads=["bank%d" % qi_], writes=["TM"])
                        else:
                            S.op("dve", lambda e, qi_=qi_: e.tensor_copy(TM[:, qi_, :], banks[qi_][0:64, :]), reads=["bank%d" % qi_], writes=["TM"])
                    S.op("pool", lambda e: e.tensor_copy(padview(Vpad), TM[:, 3, :].rearrange("p (a q d) -> p a q d", q=2, d=64)),
                         reads=["TM"], writes=["Vpad"])
                    if SUB < 6:
                        continue
                    nsc = 128 if own else 64
                    for h in (0, 2, 4, 6, "sep", 1, 3, 5, 7):
                        if h == "sep":
                            S.op("pe", lambda e: e.matmul(banks[1][0:64, 0:64], idb[:, 0:64], idb[:, 0:64], start=True, stop=True),
                                 reads=["idb"], writes=["bank1"])
                            continue
                        p, q = h // 2, h % 2
                        rw = slice(64 * q, 64 * q + 64)
                        S.op("pe", lambda e, h=h, p=p, rw=rw: e.matmul(banks[4 + h // 4][0:64, (h % 4) * 128:(h % 4) * 128 + 64],
                                                                      BT[rw, p, cs], AT[rw, p, cs],
                                                                      start=True, stop=True), reads=["BT", "AT"], writes=["bank%d" % (4 + h // 4)])
                        if own:
                            S.op("pe", lambda e, h=h, p=p, rw=rw: e.matmul(banks[4 + h // 4][0:64, (h % 4) * 128 + 64:(h % 4) * 128 + 128],
                                                                          BT[rw, p, cs], RT[rw, p, cs],
                                                                          start=True, stop=True), reads=["BT", "RT"], writes=["bank%d" % (4 + h // 4)])
                    if SUB2 < 1:
                        continue
                    nr = 2 if own else 1
                    for hg in range(2):
                        S.op("dve", lambda e, hg=hg: e.tensor_tensor(
                            SC1[:, hg * 4:hg * 4 + 4, 0:nsc].rearrange("p h (r t) -> p h r t", t=64),
                            banks[4 + hg][0:64, :].rearrange("p (h r t) -> p h r t", r=2, t=64)[:, :, 0:nr, :],
                            mskb[:, 0:nr, :].unsqueeze(1).broadcast_to([64, 4, nr, 64]), ALU.mult),
                            reads=["bank%d" % (4 + hg), "mskb"], writes=["SC1"])
                    if SUB2 < 2:
                        continue
                    for h in (0, 2, 4, 6, "sep", 1, 3, 5, 7):
                        if h == "sep":
                            S.op("pe", lambda e: e.matmul(banks[1][0:64, 0:64], idb[:, 0:64], idb[:, 0:64], start=True, stop=True),
                                 reads=["idb"], writes=["bank1"])
                            continue
                        p, q = h // 2, h % 2
                        rw = slice(64 * q, 64 * q + 64)
                        S.op("pe", lambda e, h=h, p=p, rw=rw: e.matmul(banks[4 + h // 4][0:64, (h % 4) * 128:(h % 4) * 128 + 64],
                                                                      KT[rw, p, cs], AT[rw, p, cs],
                                                                      start=True, stop=True), reads=["KT", "AT"], writes=["bank%d" % (4 + h // 4)])
                        if own:
                            S.op("pe", lambda e, h=h, p=p, rw=rw: e.matmul(banks[4 + h // 4][0:64, (h % 4) * 128 + 64:(h % 4) * 128 + 128],
                                                                          KT[rw, p, cs], RT[rw, p, cs],
                                                                          start=True, stop=True), reads=["KT", "RT"], writes=["bank%d" % (4 + h // 4)])
                    for hg in range(2):
                        S.op("dve", lambda e, hg=hg: e.tensor_tensor(
                            SC2[:, hg * 4:hg * 4 + 4, 0:nsc].rearrange("p h (r t) -> p h r t", t=64),
                            banks[4 + hg][0:64, :].rearrange("p (h r t) -> p h r t", r=2, t=64)[:, :, 0:nr, :],
                            mskb[:, 0:nr, :].unsqueeze(1).broadcast_to([64, 4, nr, 64]), ALU.mult),
                            reads=["bank%d" % (4 + hg), "mskb"], writes=["SC2"])
                    if SUB2 < 3:
                        continue
                    for h in (0, 2, 4, 6, "sep", 1, 3, 5, 7):
                        if h == "sep":
                            S.op("pe", lambda e: e.matmul(banks[1][0:64, 0:64], idb[:, 0:64], idb[:, 0:64], start=True, stop=True),
                                 reads=["idb"], writes=["bank1"])
                            continue
                        p, q = h // 2, h % 2
                        rw = slice(64 * q, 64 * q + 64)
                        S.op("pe", lambda e, h=h, p=p, rw=rw: e.matmul(banks[0][0:64, h * 64:(h + 1) * 64], AT[rw, p, cs], BT[rw, p, cs],
                                                                      start=True, stop=True), reads=["AT", "BT"], writes=["bank0"])
                    S.op("dve", lambda e: e.tensor_tensor(SC3[:], banks[0][0:64, :].rearrange("p (h t) -> p h t", t=64),
                                                         mskb[:, 2, :].unsqueeze(1).broadcast_to([64, 8, 64]), ALU.mult),
                         reads=["bank0", "mskb"], writes=["SC3"])
                    if SUB2 < 4:
                        continue
                    for h in range(8):
                        S.op("pe", lambda e, h=h: e.matmul(banks[1][0:64, h * 64:(h + 1) * 64], SC2[:, h, 0:64], TM[:, 3, h * 64:(h + 1) * 64],
                                                          start=True, stop=True), reads=["SC2", "TM"], writes=["bank1"])
                    S.op("act", lambda e: e.copy(Zb[:, :, 64:128], banks[1][0:64, :].rearrange("p (h t) -> p h t", t=64)), reads=["bank1"], writes=["Zb"])
                    S.op("pool", lambda e: e.tensor_copy(Zb[:, :, 0:64], TM[:, 0, :].rearrange("p (h t) -> p h t", t=64)), reads=["TM"], writes=["Zb"])
                    if SUB < 7:
                        continue
                    zb = lambda h: banks[2 + h // 4][0:64, (h % 4) * 128:(h % 4) * 128 + 128]
                    for h in range(8):
                        S.op("pe", lambda e, h=h: e.matmul(zb(h), idb[0:64, 0:64], Zb[:, h, :], start=(h % 4 == 0), stop=False, skip_group_check=True),
                             reads=["idb", "Zb"], writes=["bank%d" % (2 + h // 4)])
                    Pc, PTc, pn, ptn = SC3, SC1, "SC3", "SC1"
                    for lv in range(6):
                        for h in range(8):
                            lhs = PTc[:, h, 0:64]
                            S.op("pe", lambda e, h=h, lhs=lhs, lv=lv: e.matmul(zb(h), lhs, Zb[:, h, :], start=False, stop=(lv == 5), skip_group_check=True),
                                 reads=[ptn, "Zb"], writes=["bank%d" % (2 + h // 4)])
                        if lv < 5:
                            for h in range(8):
                                S.op("pe", lambda e, h=h, Pc=Pc, PTc=PTc: e.matmul(banks[0][0:64, h * 64:(h + 1) * 64], PTc[:, h, 0:64], Pc[:, h, 0:64],
                                                                                  start=True, stop=True), reads=[pn, ptn], writes=["bank0"])
                            for h in range(8):
                                S.op("pe", lambda e, h=h, Pc=Pc, PTc=PTc: e.matmul(banks[1][0:64, h * 64:(h + 1) * 64], Pc[:, h, 0:64], PTc[:, h, 0:64],
                                                                                  start=True, stop=True), reads=[pn, ptn], writes=["bank1"])
                        for hg in range(2):
                            if hg == 0:
                                S.op("act", lambda e: e.copy(Zb[:, 0:4, :], banks[2][0:64, :].rearrange("p (h t) -> p h t", t=128)),
                                     reads=["bank2"], writes=["Zb"])
                            else:
                                S.op("dve", lambda e: e.tensor_copy(Zb[:, 4:8, :], banks[3][0:64, :].rearrange("p (h t) -> p h t", t=128)),
                                     reads=["bank3"], writes=["Zb"])
                        if lv < 5:
                            np_, npt = Pq[lv % 2], PTq[lv % 2]
                            S.op("act", lambda e, np_=np_: e.copy(np_[:], banks[0][0:64, :].rearrange("p (h t) -> p h t", t=64)),
                                 reads=["bank0"], writes=["Pq%d" % (lv % 2)])
                            S.op("dve", lambda e, npt=npt: e.tensor_copy(npt[:], banks[1][0:64, :].rearrange("p (h t) -> p h t", t=64)),
                                 reads=["bank1"], writes=["PTq%d" % (lv % 2)])
                            Pc, PTc, pn, ptn = np_, npt, "Pq%d" % (lv % 2), "PTq%d" % (lv % 2)
                    if SUB < 8:
                        continue
                    if own:
                        S.op("pool", lambda e: e.tensor_copy(padview(Wpad), Zb[:, :, 64:128].rearrange("p (a q) d -> p a q d", q=2)),
                             reads=["Zb"], writes=["Wpad"])
                        for h in range(8):
                            p, q = h // 2, h % 2
                            rw = slice(64 * q, 64 * q + 64)
                            S.op("pe", lambda e, h=h: e.matmul(banks[0][0:64, h * 64:(h + 1) * 64], Zb[:, h, 0:64], SC1[:, h, 64:128],
                                                              start=True, stop=False), reads=["Zb", "SC1"], writes=["bank0"])
                            S.op("pe", lambda e, h=h, p=p, rw=rw: e.matmul(banks[0][0:64, h * 64:(h + 1) * 64], idb[:, rw], RT[:, p, cs],
                                                                          start=False, stop=True), reads=["idb", "RT"], writes=["bank0"])
                        S.op("act", lambda e: e.copy(QTb[:], banks[0][0:64, :].rearrange("p (h t) -> p h t", t=64)), reads=["bank0"], writes=["QTb"])
                        for h in range(8):
                            p = h // 2
                            yo = banks[6][:, p * 128 + c * 64:p * 128 + c * 64 + 64]
                            S.op("pe", lambda e, h=h, yo=yo: e.matmul(yo, Wpad[:, h, :], SC1[:, h, 64:128], start=(h % 2 == 0), stop=False),
                                 reads=["Wpad", "SC1"], writes=["bank6"])
                            S.op("pe", lambda e, h=h, yo=yo: e.matmul(yo, Vpad[:, h, :], SC2[:, h, 64:128], start=False, stop=False),
                                 reads=["Vpad", "SC2"], writes=["bank6"])
                            S.op("pe", lambda e, h=h, yo=yo: e.matmul(yo, Hpad[:, h, :], QTb[:, h, :], start=False, stop=(h % 2 == 1)),
                                 reads=["Hpad", "QTb"], writes=["bank6"])
                    for h in range(8):
                        S.op("pe", lambda e, h=h: e.matmul(banks[1][0:64, h * 64:(h + 1) * 64], Zb[:, h, 0:64], TM[:, 1, h * 64:(h + 1) * 64],
                                                          start=True, stop=True), reads=["Zb", "TM"], writes=["bank1"])
                    S.op("act", lambda e: e.copy(Ef[:], banks[1][0:64, :].rearrange("p (h t) -> p h t", t=64)), reads=["bank1"], writes=["Ef"])
                    for h in range(8):
                        ho = banks[7][0:64, h * 64:(h + 1) * 64]
                        S.op("pe", lambda e, h=h, ho=ho: e.matmul(ho, TM[:, 1, h * 64:(h + 1) * 64], Zb[:, h, 64:128], start=True, stop=False),
                             reads=["TM", "Zb"], writes=["bank7"])
                        S.op("pe", lambda e, h=h, ho=ho: e.matmul(ho, TM[:, 2, h * 64:(h + 1) * 64], TM[:, 3, h * 64:(h + 1) * 64], start=False, stop=False),
                             reads=["TM"], writes=["bank7"])
                        S.op("pe", lambda e, h=h, ho=ho: e.matmul(ho, Ef[:, h, :], Hf[:, h, :], start=False, stop=False),
                             reads=["Ef", "Hf"], writes=["bank7"])
                        S.op("pe", lambda e, h=h, ho=ho: e.matmul(ho, cm[0:64, 0:64], Hf[:, h, :], start=False, stop=True),
                             reads=["cm", "Hf"], writes=["bank7"])
                    S.op("dve", lambda e, c=c: e.tensor_tensor(Hf[:], banks[7][0:64, :].rearrange("p (h t) -> p h t", t=64),
                                                              gam[:, :, c:c + 1].broadcast_to([64, 8, 64]), ALU.mult),
                         reads=["bank7", "gam"], writes=["Hf"])
                    if i >= NTW - NTO - 1:
                        S.op("pool", lambda e: e.tensor_copy(padview(Hpad), Hf[:].rearrange("p (a q) d -> p a q d", q=2)),
                             reads=["Hf"], writes=["Hpad"])
                if own and SUB >= 9:
                    S.op("act", lambda e: e.copy(yv[:].rearrange("p a t -> p (a t)"), banks[6][:]), reads=["bank6"], writes=["yv"])
                    S.op("pe", lambda e: e.matmul(banks[0][:], bones, yv[:].rearrange("p a t -> p (a t)"), start=True, stop=True),
                         reads=["cm", "yv"], writes=["bank0"])
                    S.op("dve", lambda e: e.scalar_tensor_tensor(dv[:].rearrange("p a t -> p (a t)"), banks[0][:], -1.0 / 64,
                                                                yv[:].rearrange("p a t -> p (a t)"), ALU.mult, ALU.add),
                         reads=["bank0", "yv"], writes=["dv"])
                    S.op("pool", lambda e: e.tensor_tensor(d2[:], dv[:], dv[:], ALU.mult), reads=["dv"], writes=["d2"])
                    S.op("pe", lambda e: e.matmul(banks[1][:], bones, d2[:].rearrange("p a t -> p (a t)"), start=True, stop=True),
                         reads=["cm", "d2"], writes=["bank1"])
                    S.op("act", lambda e: e.activation(d2[:].rearrange("p a t -> p (a t)"), banks[1][:], AF.Sqrt, bias=GN_EPS, scale=1.0 / 64),
                         reads=["bank1"], writes=["d2"])
                    S.op("dve", lambda e: e.reciprocal(d2[:], d2[:]), reads=["d2"], writes=["d2"])
                    S.op("dve", lambda e: e.tensor_tensor(dv[:], dv[:], d2[:], ALU.mult), reads=["dv", "d2"], writes=["dv"])
                    S.op("dve", lambda e: e.tensor_tensor(dv[:], dv[:], cv2[:, C2_LW:C2_LW + 4].unsqueeze(2).broadcast_to([128, 4, 128]), ALU.mult),
                         reads=["dv", "cv2"], writes=["dv"])
                    S.op("dve", lambda e: e.tensor_tensor(dv[:], dv[:], cv2[:, C2_LB:C2_LB + 4].unsqueeze(2).broadcast_to([128, 4, 128]), ALU.add),
                         reads=["dv", "cv2"], writes=["dv"])
                    S.op("pe", lambda e: e.matmul(banks[0][:], bones, rk[:].rearrange("p a t -> p (a t)"), start=True, stop=True),
                         reads=["cm", "rk"], writes=["bank0"])
                    S.op("dve", lambda e: e.tensor_tensor(d2[:], banks[0][:].rearrange("p (a t) -> p a t", t=128), vS, ALU.mult),
                         reads=["bank0", "PS"], writes=["d2"])
                    S.op("pool", lambda e: e.tensor_tensor(dv[:], dv[:], d2[:], ALU.add), reads=["dv", "d2"], writes=["dv"])
                    S.op("pool", lambda e, io=io: e.tensor_tensor(y_bT[:, :, io * 128:(io + 1) * 128], dv[:], gT[:], ALU.mult),
                         reads=["dv", "gT"], writes=["y_bT"])
        S.barrier()
        y_aT = sb(st, "y_aT", [128, 4, OWN], BF16)
        with ExitStack() as p2:
          if STAGE >= 2:
            wa = sb(p2, "wa", [128, 8, 1536], BF16)
            S.dma("sp", lambda e: e.dma_start(out=wa[:], in_=w_in_b.rearrange("(k p) n -> p k n", p=128)[:, :, 0:1536]),
                  reads=["w_in_b"], writes=["wa"])
            bT = sb(p2, "bT", [128, 5, 8, 128], F32)
            S.dma("sp", lambda e: e.dma_start(out=bT[:].rearrange("p a h t -> p (a h t)"), in_=biasd), writes=["bT"])
            vld = sb(p2, "vld", [128, NTA], F32)
            S.dma("sp", lambda e: e.dma_start(out=vld[:], in_=validd), writes=["vld"])
            qT = sb(p2, "qT", [128, 4, OWN], BF16)
            kTa = sb(p2, "kTa", [128, 4, NTA * 128], BF16)
            Va = sb(p2, "Va", [128, NTA, 8, 65], BF16)
            scf = [sb(p2, "scf%d" % i, [128, 512], F32) for i in range(2)]
            pTb = [sb(p2, "pTb%d" % i, [128, 512], BF16) for i in range(2)]
            rec = sb(p2, "rec", [128, 8], F32)
            ya = sb(p2, "ya", [128, 8, 64], BF16)
            for ia in range(NTA):
                ts_ = slice(ia * 128, (ia + 1) * 128)
                io = ia - 4
                pbk = ia % 2
                for c in range(4):
                    for k in range(8):
                        S.op("pe", lambda e, c=c, k=k, pbk=pbk, ts_=ts_: e.matmul(banks[pbk][:, c * 128:(c + 1) * 128], wa[:, k, 512 + c * 128:512 + (c + 1) * 128],
                                                                                 hT_att[:, k, ts_], start=(k == 0), stop=(k == 7)),
                             reads=["wa", "hT_att"], writes=["bank%d" % pbk])
                S.op("act", lambda e, pbk=pbk, ts_=ts_: e.copy(kTa[:, :, ts_], banks[pbk][:].rearrange("p (c t) -> p c t", t=128)),
                     reads=["bank%d" % pbk], writes=["kTa"])
                pbk2 = 2 + ia % 2
                for k in range(8):
                    S.op("pe", lambda e, k=k, pbk2=pbk2, ts_=ts_: e.matmul(banks[pbk2][:], hT_att[:, k, ts_], wa[:, k, 1024:1536],
                                                                          start=(k == 0), stop=(k == 7)), reads=["wa", "hT_att"], writes=["bank%d" % pbk2])
                S.op("dve", lambda e, pbk2=pbk2, ia=ia: e.tensor_copy(Va[:, ia, :, 0:64], banks[pbk2][:].rearrange("p (h d) -> p h d", d=64)),
                     reads=["bank%d" % pbk2], writes=["Va"])
                S.op("pool", lambda e, ia=ia: e.tensor_copy(Va[:, ia, :, 64:65], vld[:, ia:ia + 1].unsqueeze(1).broadcast_to([128, 8, 1])),
                     reads=["vld"], writes=["Va"])
                if io >= 0:
                    pbk3 = 4 + ia % 2
                    for c in range(4):
                        for k in range(8):
                            S.op("pe", lambda e, c=c, k=k, pbk3=pbk3, ts_=ts_: e.matmul(banks[pbk3][:, c * 128:(c + 1) * 128], wa[:, k, c * 128:(c + 1) * 128],
                                                                                       hT_att[:, k, ts_], start=(k == 0), stop=(k == 7)),
                                 reads=["wa", "hT_att"], writes=["bank%d" % pbk3])
                    S.op("act", lambda e, pbk3=pbk3, io=io: e.copy(qT[:, :, io * 128:(io + 1) * 128], banks[pbk3][:].rearrange("p (c t) -> p c t", t=128)),
                         reads=["bank%d" % pbk3], writes=["qT"])
            for io in range(NTO):
                ia = io + 4
                qs = slice(io * 128, (io + 1) * 128)
                step = 0
                for dl in range(5):
                    kt = ia - dl
                    ks = slice(kt * 128, (kt + 1) * 128)
                    for hg in range(2):
                        pb_ = (step % 2) * 2 + hg
                        sl = step % 2
                        for hh in (0, 2, "sep", 1, 3):
                            if hh == "sep":
                                S.op("pe", lambda e: e.matmul(banks[7][0:64, 0:64], idb[:, 0:64], idb[:, 0:64], start=True, stop=True),
                                     reads=["idb"], writes=["bank7"])
                                continue
                            h = hg * 4 + hh
                            p, q = h // 2, h % 2
                            rw = slice(64 * q, 64 * q + 64)
                            S.op("pe", lambda e, pb_=pb_, hh=hh, rw=rw, p=p, ks=ks: e.matmul(banks[pb_][:, hh * 128:(hh + 1) * 128], kTa[rw, p, ks], qT[rw, p, qs],
                                                                                          start=True, stop=True), reads=["kTa", "qT"], writes=["bank%d" % pb_])
                        S.op("dve", lambda e, pb_=pb_, hg=hg, dl=dl, sl=sl: e.scalar_tensor_tensor(
                            scf[hg][:], banks[pb_][:], 0.125, bT[:, dl, hg * 4:hg * 4 + 4, :].rearrange("p h t -> p (h t)"), ALU.mult, ALU.add),
                            reads=["bank%d" % pb_, "bT"], writes=["scf%d" % hg])
                        S.op("act", lambda e, hg=hg: e.activation(pTb[hg][:], scf[hg][:], AF.Exp), reads=["scf%d" % hg], writes=["pTb%d" % hg])
                        for hh in range(4):
                            h = hg * 4 + hh
                            S.op("pe", lambda e, hg=hg, hh=hh, h=h, kt=kt, dl=dl: e.matmul(banks[4 + hg][:, hh * 65:(hh + 1) * 65], pTb[hg][:, hh * 128:(hh + 1) * 128],
                                                                                       Va[:, kt, h, :], start=(dl == 0 and hh == 0), stop=(dl == 4), skip_group_check=True),
                                 reads=["pTb%d" % hg, "Va"], writes=["bank%d" % (4 + hg)])
                    step += 1
                for hg in range(2):
                    ov = banks[4 + hg][:, 0:260].rearrange("p (h d) -> p h d", d=65)
                    S.op("dve", lambda e, hg=hg, ov=ov: e.reciprocal(rec[:, hg * 4:hg * 4 + 4], ov[:, :, 64]), reads=["bank%d" % (4 + hg)], writes=["rec"])
                    S.op("dve", lambda e, hg=hg, ov=ov: e.tensor_tensor(ya[:, hg * 4:hg * 4 + 4, :], ov[:, :, 0:64],
                                                                       rec[:, hg * 4:hg * 4 + 4].unsqueeze(2).broadcast_to([128, 4, 64]), ALU.mult),
                         reads=["bank%d" % (4 + hg), "rec"], writes=["ya"])
                pb6 = bank_bf(6)
                for p in range(4):
                    S.op("pe", lambda e, p=p: e.transpose(pb6[:, p * 128:(p + 1) * 128], ya[:, 2 * p:2 * p + 2, :].rearrange("p h d -> p (h d)"), idb[:]),
                         reads=["ya", "idb"], writes=["bank6"])
                S.op("act", lambda e, qs=qs: e.copy(y_aT[:, :, qs], pb6[:, 0:512].rearrange("p (c t) -> p c t", t=128)), reads=["bank6"], writes=["y_aT"])
        S.barrier()
        grow = sb(st, "grow", [128, 2, D], F32)
        S.dma("sp", lambda e: e.dma_start(out=grow[:, 0, :], in_=rows[0:1, :].partition_broadcast(128)), writes=["grow"])
        S.dma("sp", lambda e: e.dma_start(out=grow[:, 1, :], in_=rows[1:2, :].partition_broadcast(128)), writes=["grow"])
        mo = sb(st, "mo", [128, D], F32)
        with ExitStack() as p3:
          if STAGE >= 3:
            wg = sb(p3, "wg", [128, 8, 2048], BF16)
            pa = sb(p3, "pa", [128, 4, D], BF16)
            pb = sb(p3, "pb", [128, 4, D], BF16)
            wo = sb(p3, "wo", [128, 8, D], BF16)
            S.dma("sp", lambda e: e.dma_start(out=wg[:], in_=w_in_b.rearrange("(k p) n -> p k n", p=128)[:, :, 3328:5376]), reads=["w_in_b"], writes=["wg"])
            S.dma("sp", lambda e: e.dma_start(out=pa[:], in_=pa_b.rearrange("(k p) n -> p k n", p=128)), reads=["pa_b"], writes=["pa"])
            S.dma("sp", lambda e: e.dma_start(out=pb[:], in_=pb_b.rearrange("(k p) n -> p k n", p=128)), reads=["pb_b"], writes=["pb"])
            S.dma("sp", lambda e: e.dma_start(out=wo[:], in_=wo_b.rearrange("(k p) n -> p k n", p=128)), reads=["wo_b"], writes=["wo"])
            gat = sb(p3, "gat", [128, 16, 512], BF16)
            mT = sb(p3, "mT", [128, 8, 512], BF16)
            ta = sb(p3, "ta", [128, 512], F32)
            tb = sb(p3, "tb", [128, 512], F32)
            for g in range(NG):
                gs = slice(g * 512, (g + 1) * 512)
                hs = slice(512 + g * 512, 512 + (g + 1) * 512)
                for ct in range(16):
                    pbk = ct % 2
                    for k in range(8):
                        S.op("pe", lambda e, ct=ct, k=k, pbk=pbk: e.matmul(banks[pbk][:], wg[:, k, ct * 128:(ct + 1) * 128], hT_att[:, k, hs],
                                                                          start=(k == 0), stop=(k == 7)), reads=["wg", "hT_att"], writes=["bank%d" % pbk])
                    S.op("act", lambda e, ct=ct, pbk=pbk: e.activation(gat[:, ct, :], banks[pbk][:], AF.Sigmoid, bias=cv[:, C_GB + ct:C_GB + ct + 1]),
                         reads=["bank%d" % pbk, "cv"], writes=["gat"])
                for dt_ in range(8):
                    ba, bb = 2 + 2 * (dt_ % 2), 3 + 2 * (dt_ % 2)
                    for k in range(4):
                        S.op("pe", lambda e, dt_=dt_, k=k, ba=ba: e.matmul(banks[ba][:], pa[:, k, dt_ * 128:(dt_ + 1) * 128], y_aT[:, k, gs],
                                                                          start=(k == 0), stop=(k == 3)), reads=["pa", "y_aT"], writes=["bank%d" % ba])
                    for k in range(4):
                        S.op("pe", lambda e, dt_=dt_, k=k, bb=bb: e.matmul(banks[bb][:], pb[:, k, dt_ * 128:(dt_ + 1) * 128], y_bT[:, k, gs],
                                                                          start=(k == 0), stop=(k == 3)), reads=["pb", "y_bT"], writes=["bank%d" % bb])
                    S.op("dve", lambda e, dt_=dt_, ba=ba: e.tensor_tensor(ta[:], banks[ba][:], gat[:, dt_, :], ALU.mult),
                         reads=["bank%d" % ba, "gat"], writes=["ta"])
                    S.op("dve", lambda e, dt_=dt_, bb=bb: e.tensor_tensor(tb[:], banks[bb][:], gat[:, 8 + dt_, :], ALU.mult),
                         reads=["bank%d" % bb, "gat"], writes=["tb"])
                    S.op("pool", lambda e, dt_=dt_: e.tensor_tensor(mT[:, dt_, :], ta[:], tb[:], ALU.add), reads=["ta", "tb"], writes=["mT"])
                for tt in range(4):
                    it = g * 4 + tt
                    b = it % 2
                    S.dma("sp", lambda e, b=b, it=it: e.dma_start(out=xt[b][:], in_=xw[WIN - OWN + it * 128:WIN - OWN + (it + 1) * 128, :]),
                          writes=["xt%d" % b])
                    for half in range(2):
                        pbk = 6 + half
                        for k in range(8):
                            S.op("pe", lambda e, k=k, half=half, pbk=pbk, tt=tt: e.matmul(banks[pbk][:], mT[:, k, tt * 128:(tt + 1) * 128],
                                                                                         wo[:, k, half * 512:(half + 1) * 512], start=(k == 0), stop=(k == 7)),
                                 reads=["mT", "wo"], writes=["bank%d" % pbk])
                        S.op("act", lambda e, half=half, pbk=pbk: e.copy(mo[:, half * 512:(half + 1) * 512], banks[pbk][:]), reads=["bank%d" % pbk], writes=["mo"])
                    post_norm_res(S, nc, mo, "mo", xt[b], "xt%d" % b, grow[:, 0, :], junk, ssq, xt[b], "xt%d" % b)
                    S.dma("sp", lambda e, b=b, it=it: e.dma_start(out=x1s[it * 128:(it + 1) * 128, :], in_=xt[b][:]), reads=["xt%d" % b], writes=["x1s"])
        S.barrier()
        with ExitStack() as p4:
          if STAGE >= 4:
            wus = [sb(p4, "wus%d" % i, [128, 8, 1024], BF16) for i in range(1)]
            wds = [sb(p4, "wds%d" % i, [128, 8, 1024], BF16) for i in range(1)]
            x1g = sb(p4, "x1g", [128, 4, D], F32)
            hfT = sb(p4, "hfT", [128, 8, 512], BF16)
            acc = sb(p4, "acc", [128, 4, D], F32)
            act_ = sb(p4, "act_", [128, 8, 512], BF16)
            rl = [sb(p4, "rl%d" % i, [128, 512], F32) for i in range(2)]
            sl_i = 0
            for g in range(NG):
                for tt in range(4):
                    it = g * 4 + tt
                    S.dma("sp", lambda e, tt=tt, it=it: e.dma_start(out=x1g[:, tt, :], in_=x1s[it * 128:(it + 1) * 128, :]), reads=["x1s"], writes=["x1g"])
                    norm_transpose(x1g[:, tt, :], "x1g", C_G3, hfT[:, :, tt * 128:(tt + 1) * 128], "hfT", 0)
                for s in range(4):
                    slot = 0
                    sl_i += 1
                    S.dma("sp", lambda e, s=s, slot=slot: e.dma_start(out=wus[slot][:], in_=wu_b.rearrange("(k p) n -> p k n", p=128)[:, :, s * 1024:(s + 1) * 1024]),
                          reads=["wu_b"], writes=["wus%d" % slot])
                    S.dma("sp", lambda e, s=s, slot=slot: e.dma_start(out=wds[slot][:], in_=wd_b[s * 1024:(s + 1) * 1024, :].rearrange("(f p) n -> p f n", p=128)),
                          reads=["wd_b"], writes=["wds%d" % slot])
                    for f in range(8):
                        pbk = 1 + f % 2
                        for k in range(8):
                            S.op("pe", lambda e, f=f, k=k, pbk=pbk, slot=slot: e.matmul(banks[pbk][:], wus[slot][:, k, f * 128:(f + 1) * 128], hfT[:, k, :],
                                                                                       start=(k == 0), stop=(k == 7)), reads=["wus%d" % slot, "hfT"], writes=["bank%d" % pbk])
                        S.op("act", lambda e, f=f, pbk=pbk: e.activation(rl[f % 2][:], banks[pbk][:], AF.Relu), reads=["bank%d" % pbk], writes=["rl%d" % (f % 2)])
                        S.op("pool", lambda e, f=f: e.tensor_tensor(act_[:, f, :], rl[f % 2][:], rl[f % 2][:], ALU.mult), reads=["rl%d" % (f % 2)], writes=["act_"])
                    for tt in range(4):
                        for half in range(2):
                            pbk = 3 + (tt % 2) * 2 + half
                            for f in range(8):
                                S.op("pe", lambda e, f=f, tt=tt, half=half, pbk=pbk, slot=slot: e.matmul(
                                    banks[pbk][:], act_[:, f, tt * 128:(tt + 1) * 128], wds[slot][:, f, half * 512:(half + 1) * 512],
                                    start=(f == 0), stop=(f == 7)), reads=["act_", "wds%d" % slot], writes=["bank%d" % pbk])
                            if s == 0:
                                S.op("dve", lambda e, tt=tt, half=half, pbk=pbk: e.tensor_copy(acc[:, tt, half * 512:(half + 1) * 512], banks[pbk][:]),
                                     reads=["bank%d" % pbk], writes=["acc"])
                            else:
                                S.op("dve", lambda e, tt=tt, half=half, pbk=pbk: e.tensor_tensor(acc[:, tt, half * 512:(half + 1) * 512], banks[pbk][:],
                                                                                                acc[:, tt, half * 512:(half + 1) * 512], ALU.add),
                                     reads=["bank%d" % pbk, "acc"], writes=["acc"])
                for tt in range(4):
                    it = g * 4 + tt
                    post_norm_res(S, nc, acc[:, tt, :], "acc", x1g[:, tt, :], "x1g", grow[:, 1, :], junk, ssq, mo, "mo")
                    S.dma("sp", lambda e, it=it: e.dma_start(out=out[it * 128:(it + 1) * 128, :], in_=mo[:]), reads=["mo"], writes=["out"])
        if STAGE < 4:
            S.barrier()
            S.dma("sp", lambda e: e.dma_start(out=out[0:128, :], in_=xw[0:128, :]), reads=["w_in_b", "wd_b", "y_bT", "y_aT", "x1s"], writes=["out"])
        S.wait_all("sp", ["out"])
        S.emit()
    return nc


def post_norm_res(S, nc, u, uname, xres, xname, grow, junk, ssq, dst, dname):
    ua = u if isinstance(u, bass.AP) else u[:]
    xa = xres if isinstance(xres, bass.AP) else xres[:]
    da = dst if isinstance(dst, bass.AP) else dst[:]
    S.op("act", lambda e: e.activation(junk[:], ua, AF.Square, accum_out=ssq[:, 0:1]), reads=[uname], writes=["junk", "ssq"])
    S.op("act", lambda e: e.activation(ssq[:, 1:2], ssq[:, 0:1], AF.Sqrt, bias=RMS_EPS, scale=1.0 / D), reads=["ssq"], writes=["ssq"])
    S.op("dve", lambda e: e.reciprocal(ssq[:, 2:3], ssq[:, 1:2]), reads=["ssq"], writes=["ssq"])
    S.op("dve", lambda e: e.scalar_tensor_tensor(ua, ua, ssq[:, 2:3], grow, ALU.mult, ALU.mult), reads=[uname, "ssq", "grow"], writes=[uname])
    S.op("pool", lambda e: e.tensor_tensor(da, ua, xa, ALU.add), reads=[uname, xname], writes=[dname])


def _host_consts(inp):
    f = np.float32
    g = lambda n: np.asarray(inp[n], dtype=f)[0]
    cvec = np.zeros((128, 64), f)
    cvec[:, 0:8] = g("pre_mix_g").reshape(8, 128).T
    cvec[:, 8:16] = g("pre_ffn_g").reshape(8, 128).T
    cvec[:, 16:32] = g("gate_bias").reshape(16, 128).T
    mu = g("shift_mu")
    colmap = [512 + c * 128 for c in range(4)] + [1024 + c * 128 for c in range(4)] + [1536] + [c * 128 for c in range(4)] + [1664]
    for t, co in enumerate(colmap):
        cvec[:, 32 + t] = mu[co:co + 128]
    cvec[:, 46:50] = g("w0").reshape(4, 128).T
    cvec[:, 50:54] = g("a0").reshape(4, 128).T
    cvec[:, 54:58] = g("k_k").reshape(4, 128).T
    cvec[:, 58:62] = g("k_a").reshape(4, 128).T
    cvec2 = np.zeros((128, 16), f)
    cvec2[:, 0:4] = g("r_k").reshape(512).reshape(4, 128).T
    cvec2[:, 4:8] = g("ln_x_w").reshape(4, 128).T
    cvec2[:, 8:12] = g("ln_x_b").reshape(4, 128).T
    rows = np.stack([g("post_mix_g"), g("post_ffn_g")], 0)
    rb = np.concatenate([g("rel_bias"), np.full((8, 1), -1e30, f)], axis=1)
    kj = np.arange(128)[:, None]
    qi = np.arange(128)[None, :]
    bt = np.zeros((128, 5, 8, 128), f)
    for dl in range(5):
        dist = 128 * dl + qi - kj
        idx = np.clip(dist, -63, 256) + 63
        cd = 2 * dl + qi // 64 - kj // 64
        idx = np.where((cd >= 0) & (cd <= 8), idx, 320)
        bt[:, dl, :, :] = rb[:, idx].transpose(1, 0, 2)
    cmat = np.zeros((128, 768), f)
    cmat[:, 0:128] = np.eye(128, dtype=f)
    cmat[0:64, 128:192] = 1.0
    cmat[64:128, 192:256] = 1.0
    s_ = np.arange(64)[:, None]
    t_ = np.arange(64)[None, :]
    cmat[0:64, 256:320] = (s_ < t_)
    cmat[0:64, 384:448] = (s_ <= t_)
    cmat[0:64, 512:576] = (t_ < s_)
    return dict(cvec=cvec, cvec2=cvec2, rows=rows, biasT=bt.reshape(128, 5 * 8 * 128), cmat=cmat,
                w2=g("w2"), a2=g("a2"), g2=g("g2"), w_in=g("w_in"), proj_a=g("proj_a"), proj_b=g("proj_b"),
                w_out=g("w_out"), w_up=g("w_up"), w_down=g("w_down"))


_NC_CACHE = {}


def run(inputs, trace=False):
    x = np.asarray(inputs["x"], dtype=np.float32)
    B, SEQ, _ = x.shape
    OWN = SEQ // 4
    WIN = SEQ
    NTA = OWN // 128 + 4
    consts = _host_consts(inputs)
    in_maps = []
    for c in range(8):
        b, j = c // 4, c % 4
        end = (j + 1) * OWN
        xw = np.zeros((WIN, D), np.float32)
        xw[WIN - end:, :] = x[b, 0:end, :]
        pos = end - NTA * 128 + np.arange(NTA * 128)
        valid = (pos >= 0).astype(np.float32).reshape(NTA, 128).T.copy()
        m = dict(consts)
        m["xw"] = xw
        m["valid"] = valid
        in_maps.append(m)
    key = (WIN, OWN)
    if key not in _NC_CACHE:
        _NC_CACHE[key] = build_nc(WIN, OWN)
    nc = _NC_CACHE[key]
    res = run_bass_kernel_spmd(nc, in_maps, core_ids=list(range(8)))
    outp = np.zeros((B, SEQ, D), np.float32)
    for c in range(8):
        b, j = c // 4, c % 4
        outp[b, j * OWN:(j + 1) * OWN, :] = res.results[c]["out"]
    return outp


def kernel(**inputs):
    return run(inputs)
```

```python
import numpy as np
import ml_dtypes
import concourse.bass as bass
import concourse.mybir as mybir
from concourse.bass_utils import run_bass_kernel_spmd
from contextlib import ExitStack

F32 = mybir.dt.float32
BF16 = mybir.dt.bfloat16
AF = mybir.ActivationFunctionType
ALU = mybir.AluOpType

D = 1024
N_DMA_SEMS = 24
RMS_EPS = 1e-6
GN_EPS = 64 * 1e-5
DEC_C = float(np.exp(-0.5))


class _Rec:
    def __getattr__(self, name):
        def f(*a, **k):
            self.call = (name, a, k)
        return f


class Sched:
    ENGS = ("pe", "act", "dve", "pool", "sp")

    def __init__(self, nc, stack):
        self.nc = nc
        self.prog = {e: [] for e in self.ENGS}
        self.cnt = {e: 0 for e in self.ENGS}
        self.sems = {}
        for e in self.ENGS:
            self.sems["p_" + e] = stack.enter_context(nc.semaphore("prog_" + e))
        for i in range(N_DMA_SEMS):
            self.sems["d%d" % i] = stack.enter_context(nc.semaphore("dma%d" % i))
        self.dcnt = [0] * N_DMA_SEMS
        self.dnx = {"sp": 0, "pool": 0, "act": 0}
        self.seen = {e: {} for e in self.ENGS}
        self.lastw = {}
        self.reads = {}

    def _deps(self, eng, reads, writes):
        deps = {}
        for b in reads:
            for s, v in self.lastw.get(b, {}).items():
                if deps.get(s, 0) < v:
                    deps[s] = v
        for b in writes:
            for s, v in self.lastw.get(b, {}).items():
                if deps.get(s, 0) < v:
                    deps[s] = v
            for s, v in self.reads.get(b, {}).items():
                if deps.get(s, 0) < v:
                    deps[s] = v
        waits = []
        for s, v in deps.items():
            if s == "p_pe" and eng == "pe":
                continue
            if self.seen[eng].get(s, 0) < v:
                self.seen[eng][s] = v
                waits.append((s, v))
        return waits

    def _commit(self, tok, reads, writes):
        s, v = tok
        for b in writes:
            self.lastw.setdefault(b, {})[s] = v
        for b in reads:
            self.reads.setdefault(b, {})[s] = v

    @staticmethod
    def _rec(fn):
        r = _Rec()
        fn(r)
        return r.call

    def op(self, eng, fn, reads=(), writes=()):
        fn = self._rec(fn)
        waits = self._deps(eng, reads, writes)
        self.cnt[eng] += 1
        tok = ("p_" + eng, self.cnt[eng])
        self._commit(tok, reads, writes)
        self.prog[eng].append((waits, fn, ("p_" + eng, 1)))

    def dma(self, eng, fn, reads=(), writes=()):
        fn = self._rec(fn)
        half = N_DMA_SEMS // 2
        base = 0 if eng == "sp" else half
        i = base + self.dnx[eng]
        self.dnx[eng] = (self.dnx[eng] + 1) % half
        waits = self._deps(eng, reads, writes)
        s = "d%d" % i
        if self.dcnt[i] > 0 and self.seen[eng].get(s, 0) < self.dcnt[i]:
            self.seen[eng][s] = self.dcnt[i]
            waits.append((s, self.dcnt[i]))
        self.dcnt[i] += 16
        self._commit((s, self.dcnt[i]), reads, writes)
        self.prog[eng].append((waits, fn, (s, 16)))

    def barrier(self):
        cur = {"p_" + e: self.cnt[e] for e in self.ENGS}
        for i in range(N_DMA_SEMS):
            cur["d%d" % i] = self.dcnt[i]
        for eng in self.ENGS:
            waits = []
            for s_, v in cur.items():
                if s_ == "p_" + eng and eng == "pe":
                    continue
                if v > 0 and self.seen[eng].get(s_, 0) < v:
                    self.seen[eng][s_] = v
                    waits.append((s_, v))
            self.prog[eng].append((waits, None, None))

    def wait_all(self, eng, bufs):
        waits = self._deps(eng, bufs, ())
        self.prog[eng].append((waits, None, None))

    def emit(self):
        nc = self.nc
        with nc.Block() as block:
            def replay(name, e):
                for waits, fn, inc in self.prog[name]:
                    for s, v in waits:
                        e.wait_ge(self.sems[s], v)
                    if fn is None:
                        continue
                    getattr(e, fn[0])(*fn[1], **fn[2]).then_inc(self.sems[inc[0]], inc[1])

            @block.tensor
            def _(e):
                replay("pe", e)

            @block.scalar
            def _(e):
                replay("act", e)

            @block.vector
            def _(e):
                replay("dve", e)

            @block.gpsimd
            def _(e):
                replay("pool", e)

            @block.sync
            def _(e):
                replay("sp", e)


import os
STAGE = int(os.environ.get("STAGE", "9"))
SUB = int(os.environ.get("SUB", "99"))
NTL = int(os.environ.get("NTL", "9999"))
SUB2 = int(os.environ.get("SUB2", "99"))
LAG = int(os.environ.get("LAG", "9"))


def build_nc(WIN, OWN, dbg=False):
    NTW = WIN // 128
    NTO = OWN // 128
    NTA = NTO + 4
    NG = OWN // 512
    nc = bass.Bass("TRN2", target_bir_lowering=False)

    def din(name, shape, dt=F32):
        return nc.dram_tensor(name, list(shape), dt, kind="ExternalInput").ap()

    xw = din("xw", [WIN, D])
    w_in = din("w_in", [D, 5376])
    proj_a = din("proj_a", [512, D])
    proj_b = din("proj_b", [512, D])
    w_out = din("w_out", [D, D])
    w_up = din("w_up", [D, 4096])
    w_down = din("w_down", [4096, D])
    cvec = din("cvec", [128, 64])
    rows = din("rows", [2, D])
    w2d = din("w2", [64, 512])
    a2d = din("a2", [64, 512])
    g2d = din("g2", [128, 512])
    biasd = din("biasT", [128, 5 * 8 * 128])
    validd = din("valid", [128, NTA])
    cmat = din("cmat", [128, 6 * 128])
    out = nc.dram_tensor("out", [OWN, D], F32, kind="ExternalOutput").ap()
    dbg_o = {}

    w_in_b = nc.dram_tensor("w_in_b", [D, 5376], BF16).ap()
    pa_b = nc.dram_tensor("pa_b", [512, D], BF16).ap()
    pb_b = nc.dram_tensor("pb_b", [512, D], BF16).ap()
    wo_b = nc.dram_tensor("wo_b", [D, D], BF16).ap()
    wu_b = nc.dram_tensor("wu_b", [D, 4096], BF16).ap()
    wd_b = nc.dram_tensor("wd_b", [4096, D], BF16).ap()
    x1s = nc.dram_tensor("x1s", [OWN, D], F32).ap()
    hT_s = nc.dram_tensor("hT_s", [128, 8, NTA * 128], BF16).ap()

    C_G1, C_G3, C_GB, C_MU = 0, 8, 16, 32
    C_W0, C_A0, C_KK, C_KA = 46, 50, 54, 58
    cvec2 = din("cvec2", [128, 16])
    C2_RK, C2_LW, C2_LB, C2_OMKA = 0, 4, 8, 12

    with ExitStack() as st:
        S = Sched(nc, st)

        def sb(stack, name, shape, dt):
            return stack.enter_context(nc.sbuf_tensor(name, list(shape), dt))

        def psb(stack, name, shape, dt=F32):
            return stack.enter_context(nc.psum_tensor(name, list(shape), dt))

        cv = sb(st, "cv", [128, 64], F32)
        cv2 = sb(st, "cv2", [128, 16], F32)
        cm = sb(st, "cm", [128, 768], F32)
        idb = sb(st, "idb", [128, 128], BF16)
        mskb = sb(st, "mskb", [64, 3, 64], BF16)
        S.dma("sp", lambda e: e.dma_start(out=cv[:], in_=cvec), writes=["cv"])
        S.dma("sp", lambda e: e.dma_start(out=cv2[:], in_=cvec2), writes=["cv2"])
        S.dma("sp", lambda e: e.dma_start(out=cm[:], in_=cmat), writes=["cm"])
        ident = cm[:, 0:128]
        bones = cm[:, 128:256]
        scanm = cm[:, 640:768]
        S.op("dve", lambda e: e.tensor_copy(idb[:], ident), reads=["cm"], writes=["idb"])
        S.op("dve", lambda e: e.tensor_copy(mskb[:], cm[0:64, 256:640].rearrange("p (a b) -> p a b", b=128)[:, :, 0:64]),
             reads=["cm"], writes=["mskb"])

        def conv(dst, src, nrows, nm):
            for r in range(0, nrows, 128):
                S.dma("pool", lambda e, r=r: e.dma_start(out=dst[r:r + 128, :], in_=src[r:r + 128, :]), writes=[nm])
        conv(w_in_b, w_in, D, "w_in_b")
        conv(pa_b, proj_a, 512, "pa_b")
        conv(pb_b, proj_b, 512, "pb_b")
        conv(wo_b, w_out, D, "wo_b")
        conv(wu_b, w_up, D, "wu_b")
        conv(wd_b, w_down, 4096, "wd_b")

        y_bT = sb(st, "y_bT", [128, 4, OWN], BF16)
        xt = [sb(st, "xt%d" % i, [128, D], F32) for i in range(2)]
        junk = sb(st, "junk", [128, D], BF16)
        xs = sb(st, "xs", [128, D], BF16)
        ssq = sb(st, "ssq", [128, 4], F32)

        banks = [psb(st, "bank%d" % i, [128, 512]) for i in range(8)]

        def bank_bf(i):
            return banks[i][:].bitcast(BF16)

        def norm_transpose(src_tile, srcname, gcol, dst_ap, dstname, pbank):
            S.op("act", lambda e: e.activation(junk[:], src_tile, AF.Square, accum_out=ssq[:, 0:1]),
                 reads=[srcname], writes=["junk", "ssq"])
            S.op("act", lambda e: e.activation(ssq[:, 1:2], ssq[:, 0:1], AF.Sqrt, bias=RMS_EPS, scale=1.0 / D),
                 reads=["ssq"], writes=["ssq"])
            S.op("dve", lambda e: e.reciprocal(ssq[:, 2:3], ssq[:, 1:2]), reads=["ssq"], writes=["ssq"])
            S.op("dve", lambda e: e.tensor_scalar(xs[:], src_tile, ssq[:, 2:3], 0.0, ALU.mult, ALU.add),
                 reads=[srcname, "ssq"], writes=["xs"])
            pb = bank_bf(pbank)
            for k in range(8):
                S.op("pe", lambda e, k=k: e.transpose(pb[:, k * 128:(k + 1) * 128], xs[:, k * 128:(k + 1) * 128], idb[:]),
                     reads=["xs", "idb"], writes=["bank%d" % pbank])
            S.op("dve", lambda e: e.tensor_tensor(dst_ap, pb[:, 0:1024].rearrange("p (k t) -> p k t", t=128),
                                                 cv[:, gcol:gcol + 8].unsqueeze(2).broadcast_to([128, 8, 128]), ALU.mult),
                 reads=["bank%d" % pbank, "cv"], writes=[dstname])

        with ExitStack() as p1:
          if STAGE >= 1:
            wr = sb(p1, "wr", [128, 8, 1792], BF16)
            S.dma("sp", lambda e: e.dma_start(out=wr[:], in_=w_in_b.rearrange("(k p) n -> p k n", p=128)[:, :, 1536:3328]),
                  reads=["w_in_b"], writes=["wr"])
            w2s = sb(p1, "w2s", [64, 512], F32)
            a2s = sb(p1, "a2s", [128, 512], F32)
            g2s = sb(p1, "g2s", [128, 512], BF16)
            S.dma("sp", lambda e: e.dma_start(out=w2s[:], in_=w2d), writes=["w2s"])
            S.dma("sp", lambda e: e.dma_start(out=a2s[64:128, :], in_=a2d), writes=["a2s"])
            S.dma("pool", lambda e: e.dma_start(out=g2s[:], in_=g2d), writes=["g2s"])
            hTr = [sb(p1, "hTr%d" % i, [128, 8, 128], BF16) for i in range(2)]
            PR = [sb(p1, "PR%d" % i, [128, 14, 129], F32) for i in range(2)]
            S.op("pool", lambda e: e.memset(PR[1][:], 0.0), writes=["PR1"])
            S.op("pool", lambda e: e.memset(PR[0][:], 0.0), writes=["PR0"])
            PS = sb(p1, "PS", [128, 14, 128], F32)

            def f32t(name):
                return sb(p1, name, [128, 4, 128], F32)
            tcw = sb(p1, "tcw", [64, 128], F32)
            cmask4 = sb(p1, "cmask4", [128, 512], F32)
            S.op("pool", lambda e: e.memset(cmask4[:], 1.0), writes=["cmask4"])
            S.op("pool", lambda e: e.memset(cmask4[:].rearrange("p (c t) -> p c t", t=64)[:, :, 0:1], 0.0), writes=["cmask4"])
            S.op("dve", lambda e: e.tensor_scalar(cv2[:, 12:16], cv[:, C_KA:C_KA + 4], -1.0, 1.0, ALU.mult, ALU.add), reads=["cv"], writes=["cv2"])
            sgc = sb(p1, "sgc", [128, 128], BF16)
            lw, ar, kk, k2, rn, kp, bq, cum, e1, e2, e3, rk, dv, d2 = [f32t(n) for n in
                ("lw", "ar", "kk", "k2", "rn", "kp", "bq", "cum", "e1", "e2", "e3", "rk", "dv", "d2")]
            t1 = k2
            AT = [sb(p1, "AT%d" % i, [128, 4, 128], BF16) for i in range(2)]
            BT = [sb(p1, "BT%d" % i, [128, 4, 128], BF16) for i in range(2)]
            KT = [sb(p1, "KT%d" % i, [128, 4, 128], BF16) for i in range(2)]
            RT = [sb(p1, "RT%d" % i, [128, 4, 128], BF16) for i in range(2)]
            VT = [sb(p1, "VT%d" % i, [128, 4, 128], BF16) for i in range(2)]
            ATm = [[sb(p1, "ATm%d_%d" % (i, q), [128, 4, 128], BF16) for q in range(2)] for i in range(2)]
            RTm = [[sb(p1, "RTm%d_%d" % (i, q), [128, 4, 128], BF16) for q in range(2)] for i in range(2)]
            cvm = sb(p1, "cvm", [128, 2], F32)
            S.op("pool", lambda e: e.memset(cvm[:], 0.0), writes=["cvm"])
            S.op("pool", lambda e: e.memset(cvm[0:64, 0:1], 1.0), writes=["cvm"])
            S.op("pool", lambda e: e.memset(cvm[64:128, 1:2], 1.0), writes=["cvm"])
            S.op("pool", lambda e: e.memset(a2s[0:64, :], 0.0), writes=["a2s"])
            gam = [sb(p1, "gam%d" % i, [64, 8, 2], F32) for i in range(2)]
            gT = [f32t("gT%d" % i) for i in range(2)]
            bon = [f32t("bon%d" % i) for i in range(2)]
            yv = [f32t("yv%d" % i) for i in range(2)]
            TM = [sb(p1, "TM%d" % i, [64, 4, 512], BF16) for i in range(2)]
            Vpad = [sb(p1, "Vpad%d" % i, [64, 8, 128], BF16) for i in range(2)]
            Wpad = [sb(p1, "Wpad%d" % i, [64, 8, 128], BF16) for i in range(2)]
            Hpad = [sb(p1, "Hpad%d" % i, [64, 8, 128], BF16) for i in range(2)]
            SC1 = [sb(p1, "SC1_%d" % i, [64, 8, 128], BF16) for i in range(2)]
            SC2 = [sb(p1, "SC2_%d" % i, [64, 8, 128], BF16) for i in range(2)]
            SC3 = [sb(p1, "SC3_%d" % i, [64, 8, 64], BF16) for i in range(2)]
            Zb = [sb(p1, "Zb%d" % i, [64, 8, 128], BF16) for i in range(2)]
            PP = [[sb(p1, "PP%d_%d" % (i, k), [64, 2, 8, 64], BF16) for k in range(2)] for i in range(2)]
            Ef = [sb(p1, "Ef%d" % i, [64, 8, 64], F32) for i in range(2)]
            QTb = [sb(p1, "QTb%d" % i, [64, 8, 64], BF16) for i in range(2)]
            Hf = sb(p1, "Hf", [64, 8, 64], F32)
            for s_ in range(2):
                for t_, n_ in ((Vpad[s_], "Vpad%d" % s_), (Wpad[s_], "Wpad%d" % s_), (Hpad[s_], "Hpad%d" % s_)):
                    S.op("pool", lambda e, t_=t_: e.memset(t_[:], 0.0), writes=[n_])
            S.op("pool", lambda e: e.memset(Hf[:], 0.0), writes=["Hf"])

            def padview(t):
                return bass.AP(t[:].tensor, t[:].offset, [list(t[:].ap[0]), [256, 4], [192, 2], [1, 64]])

            colmap = [512 + c * 128 for c in range(4)] + [1024 + c * 128 for c in range(4)] + [1536] + \
                     [c * 128 for c in range(4)] + [1664]
            XB = (6, 7)
            HORD = (0, 2, 4, 6, "sep", 1, 3, 5, 7)

            def prep_gen(i):
                own = i >= NTW - NTO
                att = i >= NTW - NTA
                ia = i - (NTW - NTA)
                b = i % 2
                ntl = 14 if i >= NTW - NTO - 1 else 9
                S.dma("sp", lambda e: e.dma_start(out=xt[b][:], in_=xw[i * 128:(i + 1) * 128, :]), writes=["xt%d" % b])
                hsrc = hTr[b][:]
                hname = "hTr%d" % b
                norm_transpose(xt[b][:], "xt%d" % b, C_G1, hsrc, hname, XB[0])
                if att:
                    S.dma("sp", lambda e: e.dma_start(out=hT_s[:, :, ia * 128:(ia + 1) * 128], in_=hTr[b][:]), reads=[hname], writes=["hT_s"])
                yield
                for gi, g0 in enumerate(range(0, ntl, 4)):
                    gn = min(4, ntl - g0)
                    pbk = XB[(gi + 1) % 2]
                    for c in range(gn):
                        co = colmap[g0 + c]
                        for k in range(8):
                            S.op("pe", lambda e, c=c, co=co, k=k: e.matmul(
                                banks[pbk][:, c * 128:(c + 1) * 128], wr[:, k, co:co + 128], hsrc[:, k, :],
                                start=(k == 0), stop=(k == 7)), reads=["wr", hname], writes=["bank%d" % pbk])
                    S.op("act", lambda e: e.copy(
                        PR[b][:, g0:g0 + gn, 1:129], banks[pbk][:, 0:gn * 128].rearrange("p (c t) -> p c t", t=128)),
                        reads=["bank%d" % pbk], writes=["PR%d" % b])
                    yield
                S.op("pool", lambda e: e.tensor_copy(PR[b][:, :, 0:1], PR[1 - b][:, :, 128:129]),
                     reads=["PR%d" % (1 - b)], writes=["PR%d" % b])
                S.op("pool", lambda e: e.tensor_tensor(PS[:, 0:ntl, :], PR[b][:, 0:ntl, 0:128], PR[b][:, 0:ntl, 1:129], ALU.subtract),
                     reads=["PR%d" % b], writes=["PS"])
                S.op("pool", lambda e: e.tensor_tensor(PS[:, 0:ntl, :], PS[:, 0:ntl, :],
                                                     cv[:, C_MU:C_MU + ntl].unsqueeze(2).broadcast_to([128, ntl, 128]), ALU.mult),
                     reads=["PS", "cv"], writes=["PS"])
                S.op("pool", lambda e: e.tensor_tensor(PS[:, 0:ntl, :], PS[:, 0:ntl, :], PR[b][:, 0:ntl, 1:129], ALU.add),
                     reads=["PS", "PR%d" % b], writes=["PS"])
                kS, vS, rS, cgS = PS[:, 0:4, :], PS[:, 4:8, :], PS[:, 9:13, :], PS[:, 13, :]
                yield
                S.op("act", lambda e: e.activation(tcw[:], PS[0:64, 8, :], AF.Tanh), reads=["PS"], writes=["tcw"])
                for p in range(4):
                    S.op("pe", lambda e, p=p: e.matmul(banks[XB[0]][:, p * 128:(p + 1) * 128], w2s[:, p * 128:(p + 1) * 128], tcw[:],
                                                      start=True, stop=True), reads=["w2s", "tcw"], writes=["bank%d" % XB[0]])
                    S.op("pe", lambda e, p=p: e.matmul(banks[XB[1]][:, p * 128:(p + 1) * 128], a2s[:, p * 128:(p + 1) * 128], PS[:, 8, :],
                                                      start=True, stop=True), reads=["a2s", "PS"], writes=["bank%d" % XB[1]])
                for p in range(4):
                    S.op("act", lambda e, p=p: e.activation(lw[:, p, :], banks[XB[0]][:, p * 128:(p + 1) * 128], AF.Sigmoid,
                                                           bias=cv[:, C_W0 + p:C_W0 + p + 1]), reads=["bank%d" % XB[0], "cv"], writes=["lw"])
                    S.op("act", lambda e, p=p: e.activation(ar[:, p, :], banks[XB[1]][:, p * 128:(p + 1) * 128], AF.Sigmoid,
                                                           bias=cv[:, C_A0 + p:C_A0 + p + 1]), reads=["bank%d" % XB[1], "cv"], writes=["ar"])
                if own:
                    S.op("act", lambda e: e.activation(sgc[:], cgS, AF.Sigmoid), reads=["PS"], writes=["sgc"])
                yield
                S.op("dve", lambda e: e.tensor_scalar(lw[:], lw[:], -DEC_C, 0.0, ALU.mult, ALU.add), reads=["lw"], writes=["lw"])
                S.op("dve", lambda e: e.tensor_tensor_scan(cum[:].rearrange("p a t -> p (a t)"), cmask4[:],
                                                         lw[:].rearrange("p a t -> p (a t)"), 0.0, ALU.mult, ALU.add),
                     reads=["lw", "cmask4"], writes=["cum"])
                S.op("act", lambda e: e.activation(e1[:], cum[:], AF.Exp), reads=["cum"], writes=["e1"])
                S.op("act", lambda e: e.activation(e2[:], cum[:], AF.Exp, scale=-1.0), reads=["cum"], writes=["e2"])
                S.op("pool", lambda e: e.tensor_tensor(e3[:], cum[:], lw[:], ALU.subtract), reads=["cum", "lw"], writes=["e3"])
                S.op("act", lambda e: e.activation(e3[:], e3[:], AF.Exp), reads=["e3"], writes=["e3"])
                S.op("dve", lambda e: e.tensor_tensor(kk[:], kS, cv[:, C_KK:C_KK + 4].unsqueeze(2).broadcast_to([128, 4, 128]), ALU.mult),
                     reads=["PS", "cv"], writes=["kk"])
                S.op("pool", lambda e: e.tensor_tensor(k2[:], kk[:], kk[:], ALU.mult), reads=["kk"], writes=["k2"])
                S.op("pe", lambda e: e.matmul(banks[XB[0]][:], bones, k2[:].rearrange("p a t -> p (a t)"), start=True, stop=True),
                     reads=["cm", "k2"], writes=["bank%d" % XB[0]])
                S.op("act", lambda e: e.activation(rn[:].rearrange("p a t -> p (a t)"), banks[XB[0]][:], AF.Sqrt), reads=["bank%d" % XB[0]], writes=["rn"])
                yield
                S.op("dve", lambda e: e.tensor_scalar(rn[:], rn[:], 1e-12, 0.0, ALU.max, ALU.add), reads=["rn"], writes=["rn"])
                S.op("dve", lambda e: e.reciprocal(rn[:], rn[:]), reads=["rn"], writes=["rn"])
                S.op("dve", lambda e: e.tensor_tensor(kk[:], kk[:], rn[:], ALU.mult), reads=["kk", "rn"], writes=["kk"])
                S.op("pool", lambda e: e.tensor_tensor(t1[:], ar[:], cv[:, C_KA:C_KA + 4].unsqueeze(2).broadcast_to([128, 4, 128]), ALU.mult),
                     reads=["ar", "cv"], writes=["k2"])
                S.op("pool", lambda e: e.tensor_tensor(t1[:], t1[:], cv2[:, C2_OMKA:C2_OMKA + 4].unsqueeze(2).broadcast_to([128, 4, 128]), ALU.add),
                     reads=["k2", "cv2"], writes=["k2"])
                S.op("pool", lambda e: e.tensor_tensor(kp[:], kS, t1[:], ALU.mult), reads=["PS", "k2"], writes=["kp"])
                S.op("dve", lambda e: e.tensor_tensor(bq[:], kk[:], ar[:], ALU.mult), reads=["kk", "ar"], writes=["bq"])
                S.op("dve", lambda e: e.scalar_tensor_tensor(AT[b][:], kk[:], -1.0, e3[:], ALU.mult, ALU.mult),
                     reads=["kk", "e3"], writes=["AT%d" % b])
                for q in range(2):
                    S.op("pool", lambda e, q=q: e.tensor_scalar(ATm[b][q][:], AT[b][:], cvm[:, q:q + 1], 0.0, ALU.mult, ALU.add),
                         reads=["AT%d" % b, "cvm"], writes=["ATm%d_%d" % (b, q)])
                S.op("dve", lambda e: e.tensor_tensor(BT[b][:], bq[:], e2[:], ALU.mult), reads=["bq", "e2"], writes=["BT%d" % b])
                S.op("pool", lambda e: e.tensor_tensor(KT[b][:], kp[:], e2[:], ALU.mult), reads=["kp", "e2"], writes=["KT%d" % b])
                S.op("pool", lambda e: e.tensor_copy(VT[b][:], vS), reads=["PS"], writes=["VT%d" % b])
                if own:
                    S.op("dve", lambda e: e.tensor_tensor(RT[b][:], rS, e1[:], ALU.mult), reads=["PS", "e1"], writes=["RT%d" % b])
                    for q in range(2):
                        S.op("pool", lambda e, q=q: e.tensor_scalar(RTm[b][q][:], RT[b][:], cvm[:, q:q + 1], 0.0, ALU.mult, ALU.add),
                             reads=["RT%d" % b, "cvm"], writes=["RTm%d_%d" % (b, q)])
                for h in range(8):
                    p, q = h // 2, h % 2
                    S.op("pe", lambda e, h=h, p=p, q=q: e.matmul(
                        banks[XB[1]][0:64, h * 2:h * 2 + 2], cm[:, 64 * q:64 * q + 64],
                        e1[:, p, :].rearrange("p (c t) -> p c t", t=64)[:, :, 63], start=True, stop=True),
                        reads=["cm", "e1"], writes=["bank%d" % XB[1]])
                S.op("dve", lambda e: e.tensor_copy(gam[b][:].rearrange("p h c -> p (h c)"), banks[XB[1]][0:64, 0:16]),
                     reads=["bank%d" % XB[1]], writes=["gam%d" % b])
                yield
                if own:
                    for p in range(4):
                        S.op("pe", lambda e, p=p: e.matmul(banks[XB[0]][:, p * 128:(p + 1) * 128], g2s[:, p * 128:(p + 1) * 128], sgc[:],
                                                          start=True, stop=True), reads=["g2s", "sgc"], writes=["bank%d" % XB[0]])
                    S.op("act", lambda e: e.copy(gT[b][:].rearrange("p a t -> p (a t)"), banks[XB[0]][:]), reads=["bank%d" % XB[0]], writes=["gT%d" % b])
                    S.op("pool", lambda e: e.tensor_tensor(rk[:], rS, kp[:], ALU.mult), reads=["PS", "kp"], writes=["rk"])
                    S.op("pool", lambda e: e.tensor_tensor(rk[:], rk[:], cv2[:, C2_RK:C2_RK + 4].unsqueeze(2).broadcast_to([128, 4, 128]), ALU.mult),
                         reads=["rk", "cv2"], writes=["rk"])
                    S.op("pe", lambda e: e.matmul(banks[XB[1]][:], bones, rk[:].rearrange("p a t -> p (a t)"), start=True, stop=True),
                         reads=["cm", "rk"], writes=["bank%d" % XB[1]])
                    S.op("dve", lambda e: e.tensor_tensor(bon[b][:], banks[XB[1]][:].rearrange("p (a t) -> p a t", t=128), vS, ALU.mult),
                         reads=["bank%d" % XB[1], "PS"], writes=["bon%d" % b])
                    yield

            def chunk_gen(n):
                i, c = n // 2, n % 2
                s = n % 2
                b = i % 2
                own = i >= NTW - NTO
                io = i - (NTW - NTO)
                cs = slice(c * 64, (c + 1) * 64)
                Z0, Z1, WB = 3 * s, 3 * s + 1, 3 * s + 2
                zb_of = lambda h: (Z0 if h < 4 else Z1)
                bn = lambda k: "bank%d" % k
                ATn, BTn, KTn, RTn, VTn = "AT%d" % b, "BT%d" % b, "KT%d" % b, "RT%d" % b, "VT%d" % b
                TMn, SC1n, SC2n, SC3n, Zbn = "TM%d" % s, "SC1_%d" % s, "SC2_%d" % s, "SC3_%d" % s, "Zb%d" % s
                tm, sc1, sc2, sc3, zbs = TM[s], SC1[s], SC2[s], SC3[s], Zb[s]

                def sep(bank, col):
                    S.op("pe", lambda e: e.matmul(banks[bank][0:64, col:col + 64], idb[:, 0:64], idb[:, 0:64], start=True, stop=True),
                         reads=["idb"], writes=[bn(bank)])
                for qi_, (src, sn, bk) in enumerate(((AT[b], ATn, Z0), (BT[b], BTn, Z1))):
                    for p in range(4):
                        S.op("pe", lambda e, src=src, p=p, bk=bk: e.matmul(banks[bk][0:64, p * 128:(p + 1) * 128], src[:, p, cs], idb[:], start=True, stop=True),
                             reads=[sn, "idb"], writes=[bn(bk)])
                S.op("act", lambda e: e.copy(tm[:, 0, :], banks[Z0][0:64, :]), reads=[bn(Z0)], writes=[TMn])
                S.op("dve", lambda e: e.tensor_copy(tm[:, 1, :], banks[Z1][0:64, :]), reads=[bn(Z1)], writes=[TMn])
                for h in range(8):
                    p, q = h // 2, h % 2
                    S.op("pe", lambda e, h=h, p=p, q=q: e.matmul(banks[WB][0:64, h * 64:(h + 1) * 64], ATm[b][q][:, p, cs], BT[b][:, p, cs],
                                                                start=True, stop=True), reads=["ATm%d_%d" % (b, q), BTn], writes=[bn(WB)])
                S.op("dve", lambda e: e.tensor_tensor(sc3[:], banks[WB][0:64, :].rearrange("p (h t) -> p h t", t=64),
                                                     mskb[:, 2, :].unsqueeze(1).broadcast_to([64, 8, 64]), ALU.mult),
                     reads=[bn(WB), "mskb"], writes=[SC3n])
                yield
                for qi_, (src, sn, bk) in enumerate(((KT[b], KTn, Z0), (VT[b], VTn, Z1))):
                    for p in range(4):
                        S.op("pe", lambda e, src=src, p=p, bk=bk: e.matmul(banks[bk][0:64, p * 128:(p + 1) * 128], src[:, p, cs], idb[:], start=True, stop=True),
                             reads=[sn, "idb"], writes=[bn(bk)])
                S.op("act", lambda e: e.copy(tm[:, 2, :], banks[Z0][0:64, :]), reads=[bn(Z0)], writes=[TMn])
                S.op("dve", lambda e: e.tensor_copy(tm[:, 3, :], banks[Z1][0:64, :]), reads=[bn(Z1)], writes=[TMn])
                S.op("pool", lambda e: e.tensor_copy(padview(Vpad[s]), tm[:, 3, :].rearrange("p (a q d) -> p a q d", q=2, d=64)),
                     reads=[TMn], writes=["Vpad%d" % s])
                yield
                nsc = 128 if own else 64
                nr = 2 if own else 1
                for (lt, ltn, dst, dstn, viaact) in ((BT[b], BTn, sc1, SC1n, False), (KT[b], KTn, sc2, SC2n, True)):
                    for h in range(8):
                        p, q = h // 2, h % 2
                        bk = zb_of(h)
                        S.op("pe", lambda e, h=h, p=p, q=q, bk=bk, lt=lt: e.matmul(banks[bk][0:64, (h % 4) * 128:(h % 4) * 128 + 64],
                                                                                 lt[:, p, cs], ATm[b][q][:, p, cs],
                                                                                 start=True, stop=True), reads=[ltn, "ATm%d_%d" % (b, q)], writes=[bn(bk)])
                        if own:
                            S.op("pe", lambda e, h=h, p=p, q=q, bk=bk, lt=lt: e.matmul(banks[bk][0:64, (h % 4) * 128 + 64:(h % 4) * 128 + 128],
                                                                                     lt[:, p, cs], RTm[b][q][:, p, cs],
                                                                                     start=True, stop=True), reads=[ltn, "RTm%d_%d" % (b, q)], writes=[bn(bk)])
                    for hg in range(2):
                        bk = Z0 if hg == 0 else Z1
                        S.op("dve", lambda e, hg=hg, bk=bk, dst=dst: e.tensor_tensor(
                            dst[:, hg * 4:hg * 4 + 4, 0:nsc].rearrange("p h (r t) -> p h r t", t=64),
                            banks[bk][0:64, :].rearrange("p (h r t) -> p h r t", r=2, t=64)[:, :, 0:nr, :],
                            mskb[:, 0:nr, :].unsqueeze(1).broadcast_to([64, 4, nr, 64]), ALU.mult),
                            reads=[bn(bk), "mskb"], writes=[dstn])
                    yield
                for h in range(8):
                    S.op("pe", lambda e, h=h: e.matmul(banks[WB][0:64, h * 64:(h + 1) * 64], sc2[:, h, 0:64], tm[:, 3, h * 64:(h + 1) * 64],
                                                      start=True, stop=True), reads=[SC2n, TMn], writes=[bn(WB)])
                S.op("act", lambda e: e.copy(zbs[:, :, 64:128], banks[WB][0:64, :].rearrange("p (h t) -> p h t", t=64)), reads=[bn(WB)], writes=[Zbn])
                S.op("pool", lambda e: e.tensor_copy(zbs[:, :, 0:64], tm[:, 0, :].rearrange("p (h t) -> p h t", t=64)), reads=[TMn], writes=[Zbn])
                yield
                zb = lambda h: banks[zb_of(h)][0:64, (h % 4) * 128:(h % 4) * 128 + 128]
                for h in range(8):
                    S.op("pe", lambda e, h=h: e.matmul(zb(h), idb[0:64, 0:64], zbs[:, h, :], start=(h % 4 == 0), stop=False, skip_group_check=True),
                         reads=["idb", Zbn], writes=[bn(zb_of(h))])
                Pc, PTc, pn, ptn = sc3, sc1, SC3n, SC1n
                for lv in range(6):
                    for h in range(8):
                        lhs = PTc[:, h, 0:64]
                        S.op("pe", lambda e, h=h, lhs=lhs: e.matmul(zb(h), lhs, zbs[:, h, :], start=False, stop=(lv == 5), skip_group_check=True),
                             reads=[ptn, Zbn], writes=[bn(zb_of(h))])
                    S.op("act", lambda e: e.copy(zbs[:, 0:4, :], banks[Z0][0:64, :].rearrange("p (h t) -> p h t", t=128)),
                         reads=[bn(Z0)], writes=[Zbn])
                    S.op("dve", lambda e: e.tensor_copy(zbs[:, 4:8, :], banks[Z1][0:64, :].rearrange("p (h t) -> p h t", t=128)),
                         reads=[bn(Z1)], writes=[Zbn])
                    if lv < 5:
                        ppb = PP[s][lv % 2]
                        np_, npt = ppb[:, 0], ppb[:, 1]
                        npn = nptn = "PP%d_%d" % (s, lv % 2)
                        for hg in range(2):
                            for hh in range(4):
                                h = hg * 4 + hh
                                S.op("pe", lambda e, h=h, hh=hh: e.matmul(banks[WB][0:64, hh * 64:(hh + 1) * 64], PTc[:, h, 0:64], Pc[:, h, 0:64],
                                                                          start=True, stop=True), reads=[pn, ptn], writes=[bn(WB)])
                            for hh in range(4):
                                h = hg * 4 + hh
                                S.op("pe", lambda e, h=h, hh=hh: e.matmul(banks[WB][0:64, 256 + hh * 64:256 + (hh + 1) * 64], Pc[:, h, 0:64], PTc[:, h, 0:64],
                                                                          start=True, stop=True), reads=[pn, ptn], writes=[bn(WB)])
                            if hg == 0:
                                S.op("act", lambda e, hg=hg: e.copy(ppb[:, :, hg * 4:hg * 4 + 4, :], banks[WB][0:64, :].rearrange("p (a h t) -> p a h t", a=2, t=64)),
                                     reads=[bn(WB)], writes=[npn])
                            else:
                                S.op("dve", lambda e, hg=hg: e.tensor_copy(ppb[:, :, hg * 4:hg * 4 + 4, :], banks[WB][0:64, :].rearrange("p (a h t) -> p a h t", a=2, t=64)),
                                     reads=[bn(WB)], writes=[npn])
                            yield
                        Pc, PTc, pn, ptn = np_, npt, npn, nptn
                    else:
                        yield
                if own:
                    S.op("pool", lambda e: e.tensor_copy(padview(Wpad[s]), zbs[:, :, 64:128].rearrange("p (a q) d -> p a q d", q=2)),
                         reads=[Zbn], writes=["Wpad%d" % s])
                    for h in range(8):
                        p, q = h // 2, h % 2
                        rw = slice(64 * q, 64 * q + 64)
                        S.op("pe", lambda e, h=h: e.matmul(banks[WB][0:64, h * 64:(h + 1) * 64], zbs[:, h, 0:64], sc1[:, h, 64:128],
                                                          start=True, stop=False), reads=[Zbn, SC1n], writes=[bn(WB)])
                        S.op("pe", lambda e, h=h, p=p, rw=rw: e.matmul(banks[WB][0:64, h * 64:(h + 1) * 64], idb[:, rw], RT[b][:, p, cs],
                                                                      start=False, stop=True), reads=["idb", RTn], writes=[bn(WB)])
                    S.op("act", lambda e: e.copy(QTb[s][:], banks[WB][0:64, :].rearrange("p (h t) -> p h t", t=64)), reads=[bn(WB)], writes=["QTb%d" % s])
                for h in range(8):
                    S.op("pe", lambda e, h=h: e.matmul(banks[Z0][0:64, h * 64:(h + 1) * 64], zbs[:, h, 0:64], tm[:, 1, h * 64:(h + 1) * 64],
                                                      start=True, stop=True), reads=[Zbn, TMn], writes=[bn(Z0)])
                S.op("act", lambda e: e.copy(Ef[s][:], banks[Z0][0:64, :].rearrange("p (h t) -> p h t", t=64)), reads=[bn(Z0)], writes=["Ef%d" % s])
                yield
                if own:
                    for h in range(8):
                        p = h // 2
                        yo = banks[Z1][:, p * 64:(p + 1) * 64]
                        S.op("pe", lambda e, h=h, yo=yo: e.matmul(yo, Wpad[s][:, h, :], sc1[:, h, 64:128], start=(h % 2 == 0), stop=False),
                             reads=["Wpad%d" % s, SC1n], writes=[bn(Z1)])
                        S.op("pe", lambda e, h=h, yo=yo: e.matmul(yo, Vpad[s][:, h, :], sc2[:, h, 64:128], start=False, stop=False),
                             reads=["Vpad%d" % s, SC2n], writes=[bn(Z1)])
                        S.op("pe", lambda e, h=h, yo=yo: e.matmul(yo, Hpad[s][:, h, :], QTb[s][:, h, :], start=False, stop=(h % 2 == 1)),
                             reads=["Hpad%d" % s, "QTb%d" % s], writes=[bn(Z1)])
                    S.op("act", lambda e: e.copy(yv[b][:, :, cs], banks[Z1][:, 0:256].rearrange("p (a t) -> p a t", t=64)), reads=[bn(Z1)], writes=["yv%d" % b])
                    yield
                for h in range(8):
                    ho = banks[WB][0:64, h * 64:(h + 1) * 64]
                    S.op("pe", lambda e, h=h, ho=ho: e.matmul(ho, tm[:, 1, h * 64:(h + 1) * 64], zbs[:, h, 64:128], start=True, stop=False),
                         reads=[TMn, Zbn], writes=[bn(WB)])
                    S.op("pe", lambda e, h=h, ho=ho: e.matmul(ho, tm[:, 2, h * 64:(h + 1) * 64], tm[:, 3, h * 64:(h + 1) * 64], start=False, stop=False),
                         reads=[TMn], writes=[bn(WB)])
                    S.op("pe", lambda e, h=h, ho=ho: e.matmul(ho, Ef[s][:, h, :], Hf[:, h, :], start=False, stop=False),
                         reads=["Ef%d" % s, "Hf"], writes=[bn(WB)])
                    S.op("pe", lambda e, h=h, ho=ho: e.matmul(ho, cm[0:64, 0:64], Hf[:, h, :], start=False, stop=True),
                         reads=["cm", "Hf"], writes=[bn(WB)])
                S.op("dve", lambda e: e.tensor_tensor(Hf[:], banks[WB][0:64, :].rearrange("p (h t) -> p h t", t=64),
                                                     gam[b][:, :, c:c + 1].broadcast_to([64, 8, 64]), ALU.mult),
                     reads=[bn(WB), "gam%d" % b], writes=["Hf"])
                if n + 1 >= 2 * (NTW - NTO):
                    S.op("pool", lambda e: e.tensor_copy(padview(Hpad[1 - s]), Hf[:].rearrange("p (a q) d -> p a q d", q=2)),
                         reads=["Hf"], writes=["Hpad%d" % (1 - s)])
                yield
                if own and c == 1:
                    yvb = yv[b]
                    S.op("pe", lambda e: e.matmul(banks[Z0][:], bones, yvb[:].rearrange("p a t -> p (a t)"), start=True, stop=True),
                         reads=["cm", "yv%d" % b], writes=[bn(Z0)])
                    S.op("dve", lambda e: e.scalar_tensor_tensor(dv[:].rearrange("p a t -> p (a t)"), banks[Z0][:], -1.0 / 64,
                                                                yvb[:].rearrange("p a t -> p (a t)"), ALU.mult, ALU.add),
                         reads=[bn(Z0), "yv%d" % b], writes=["dv"])
                    S.op("pool", lambda e: e.tensor_tensor(d2[:], dv[:], dv[:], ALU.mult), reads=["dv"], writes=["d2"])
                    S.op("pe", lambda e: e.matmul(banks[Z1][:], bones, d2[:].rearrange("p a t -> p (a t)"), start=True, stop=True),
                         reads=["cm", "d2"], writes=[bn(Z1)])
                    S.op("act", lambda e: e.activation(d2[:].rearrange("p a t -> p (a t)"), banks[Z1][:], AF.Sqrt, bias=GN_EPS, scale=1.0 / 64),
                         reads=[bn(Z1)], writes=["d2"])
                    yield
                    S.op("dve", lambda e: e.reciprocal(d2[:], d2[:]), reads=["d2"], writes=["d2"])
                    S.op("dve", lambda e: e.tensor_tensor(dv[:], dv[:], d2[:], ALU.mult), reads=["dv", "d2"], writes=["dv"])
                    S.op("pool", lambda e: e.tensor_tensor(dv[:], dv[:], cv2[:, C2_LW:C2_LW + 4].unsqueeze(2).broadcast_to([128, 4, 128]), ALU.mult),
                         reads=["dv", "cv2"], writes=["dv"])
                    S.op("pool", lambda e: e.tensor_tensor(dv[:], dv[:], cv2[:, C2_LB:C2_LB + 4].unsqueeze(2).broadcast_to([128, 4, 128]), ALU.add),
                         reads=["dv", "cv2"], writes=["dv"])
                    S.op("pool", lambda e: e.tensor_tensor(dv[:], dv[:], bon[b][:], ALU.add), reads=["dv", "bon%d" % b], writes=["dv"])
                    S.op("pool", lambda e: e.tensor_tensor(y_bT[:, :, io * 128:(io + 1) * 128], dv[:], gT[b][:], ALU.mult),
                         reads=["dv", "gT%d" % b], writes=["y_bT"])
                    yield

            NTL_ = min(NTW, NTL)
            prep_done = -1
            next_prep = 0
            next_chunk = 0
            prep_g = None
            active = []
            steps = {}
            n_fin = 0
            while True:
                if prep_g is None and next_prep < NTL_ and n_fin >= 2 * (next_prep - 1):
                    prep_g = prep_gen(next_prep)
                if prep_g is not None:
                    try:
                        next(prep_g)
                    except StopIteration:
                        prep_done = next_prep
                        next_prep += 1
                        prep_g = None
                for ent in list(active):
                    try:
                        next(ent[1])
                        steps[ent[1]] += 1
                    except StopIteration:
                        active.remove(ent)
                        n_fin += 1
                if (len(active) < 2 and next_chunk < 2 * NTL_ and next_chunk // 2 <= prep_done
                        and all(sl != next_chunk % 2 for sl, _ in active)
                        and (not active or steps[active[-1][1]] >= LAG)):
                    g = chunk_gen(next_chunk)
                    steps[g] = 0
                    active.append((next_chunk % 2, g))
                    next_chunk += 1
                if prep_g is None and not active and next_prep >= NTL_ and next_chunk >= 2 * NTL_:
                    break
        S.barrier()
        y_aT = sb(st, "y_aT", [128, 4, OWN], BF16)
        grow = sb(st, "grow", [128, 2, D], F32)
        S.dma("sp", lambda e: e.dma_start(out=grow[:, 0, :], in_=rows[0:1, :].partition_broadcast(128)), writes=["grow"])
        S.dma("sp", lambda e: e.dma_start(out=grow[:, 1, :], in_=rows[1:2, :].partition_broadcast(128)), writes=["grow"])
        mo = sb(st, "mo", [128, D], F32)
        pm = ExitStack()
        hT_att = sb(pm, "hT_att", [128, 8, NTA * 128], BF16)
        S.dma("sp", lambda e: e.dma_start(out=hT_att[:], in_=hT_s), reads=["hT_s"], writes=["hT_att"])
        with ExitStack() as p2:
          if STAGE >= 2:
            wa = sb(p2, "wa", [128, 8, 1536], BF16)
            S.dma("sp", lambda e: e.dma_start(out=wa[:], in_=w_in_b.rearrange("(k p) n -> p k n", p=128)[:, :, 0:1536]),
                  reads=["w_in_b"], writes=["wa"])
            bT = sb(p2, "bT", [128, 5, 8, 128], F32)
            S.dma("sp", lambda e: e.dma_start(out=bT[:].rearrange("p a h t -> p (a h t)"), in_=biasd), writes=["bT"])
            vld = sb(p2, "vld", [128, NTA], F32)
            S.dma("sp", lambda e: e.dma_start(out=vld[:], in_=validd), writes=["vld"])
            qT = sb(p2, "qT", [128, 4, OWN], BF16)
            kTa = sb(p2, "kTa", [128, 4, NTA * 128], BF16)
            Va = sb(p2, "Va", [128, NTA, 8, 65], BF16)
            scf = [sb(p2, "scf%d" % i, [128, 512], F32) for i in range(2)]
            pTb = [sb(p2, "pTb%d" % i, [128, 512], BF16) for i in range(2)]
            rec = sb(p2, "rec", [128, 8], F32)
            ya = sb(p2, "ya", [128, 8, 64], BF16)
            for ia in range(NTA):
                ts_ = slice(ia * 128, (ia + 1) * 128)
                io = ia - 4
                pbk = ia % 2
                for c in range(4):
                    for k in range(8):
                        S.op("pe", lambda e, c=c, k=k, pbk=pbk, ts_=ts_: e.matmul(banks[pbk][:, c * 128:(c + 1) * 128], wa[:, k, 512 + c * 128:512 + (c + 1) * 128],
                                                                                 hT_att[:, k, ts_], start=(k == 0), stop=(k == 7)),
                             reads=["wa", "hT_att"], writes=["bank%d" % pbk])
                S.op("act", lambda e, pbk=pbk, ts_=ts_: e.copy(kTa[:, :, ts_], banks[pbk][:].rearrange("p (c t) -> p c t", t=128)),
                     reads=["bank%d" % pbk], writes=["kTa"])
                pbk2 = 2 + ia % 2
                for k in range(8):
                    S.op("pe", lambda e, k=k, pbk2=pbk2, ts_=ts_: e.matmul(banks[pbk2][:], hT_att[:, k, ts_], wa[:, k, 1024:1536],
                                                                          start=(k == 0), stop=(k == 7)), reads=["wa", "hT_att"], writes=["bank%d" % pbk2])
                S.op("dve", lambda e, pbk2=pbk2, ia=ia: e.tensor_copy(Va[:, ia, :, 0:64], banks[pbk2][:].rearrange("p (h d) -> p h d", d=64)),
                     reads=["bank%d" % pbk2], writes=["Va"])
                S.op("pool", lambda e, ia=ia: e.tensor_copy(Va[:, ia, :, 64:65], vld[:, ia:ia + 1].unsqueeze(1).broadcast_to([128, 8, 1])),
                     reads=["vld"], writes=["Va"])
                if io >= 0:
                    pbk3 = 4 + ia % 2
                    for c in range(4):
                        for k in range(8):
                            S.op("pe", lambda e, c=c, k=k, pbk3=pbk3, ts_=ts_: e.matmul(banks[pbk3][:, c * 128:(c + 1) * 128], wa[:, k, c * 128:(c + 1) * 128],
                                                                                       hT_att[:, k, ts_], start=(k == 0), stop=(k == 7)),
                                 reads=["wa", "hT_att"], writes=["bank%d" % pbk3])
                    S.op("act", lambda e, pbk3=pbk3, io=io: e.copy(qT[:, :, io * 128:(io + 1) * 128], banks[pbk3][:].rearrange("p (c t) -> p c t", t=128)),
                         reads=["bank%d" % pbk3], writes=["qT"])
            for io in range(NTO):
                ia = io + 4
                qs = slice(io * 128, (io + 1) * 128)
                step = 0
                for dl in range(5):
                    kt = ia - dl
                    ks = slice(kt * 128, (kt + 1) * 128)
                    for hg in range(2):
                        pb_ = (step % 2) * 2 + hg
                        sl = step % 2
                        for hh in (0, 2, "sep", 1, 3):
                            if hh == "sep":
                                S.op("pe", lambda e: e.matmul(banks[7][0:64, 0:64], idb[:, 0:64], idb[:, 0:64], start=True, stop=True),
                                     reads=["idb"], writes=["bank7"])
                                continue
                            h = hg * 4 + hh
                            p, q = h // 2, h % 2
                            rw = slice(64 * q, 64 * q + 64)
                            S.op("pe", lambda e, pb_=pb_, hh=hh, rw=rw, p=p, ks=ks: e.matmul(banks[pb_][:, hh * 128:(hh + 1) * 128], kTa[rw, p, ks], qT[rw, p, qs],
                                                                                          start=True, stop=True), reads=["kTa", "qT"], writes=["bank%d" % pb_])
                        S.op("dve", lambda e, pb_=pb_, hg=hg, dl=dl, sl=sl: e.scalar_tensor_tensor(
                            scf[hg][:], banks[pb_][:], 0.125, bT[:, dl, hg * 4:hg * 4 + 4, :].rearrange("p h t -> p (h t)"), ALU.mult, ALU.add),
                            reads=["bank%d" % pb_, "bT"], writes=["scf%d" % hg])
                        S.op("act", lambda e, hg=hg: e.activation(pTb[hg][:], scf[hg][:], AF.Exp), reads=["scf%d" % hg], writes=["pTb%d" % hg])
                        for hh in range(4):
                            h = hg * 4 + hh
                            S.op("pe", lambda e, hg=hg, hh=hh, h=h, kt=kt, dl=dl: e.matmul(banks[4 + hg][:, hh * 65:(hh + 1) * 65], pTb[hg][:, hh * 128:(hh + 1) * 128],
                                                                                       Va[:, kt, h, :], start=(dl == 0 and hh == 0), stop=(dl == 4), skip_group_check=True),
                                 reads=["pTb%d" % hg, "Va"], writes=["bank%d" % (4 + hg)])
                    step += 1
                for hg in range(2):
                    ov = banks[4 + hg][:, 0:260].rearrange("p (h d) -> p h d", d=65)
                    S.op("dve", lambda e, hg=hg, ov=ov: e.reciprocal(rec[:, hg * 4:hg * 4 + 4], ov[:, :, 64]), reads=["bank%d" % (4 + hg)], writes=["rec"])
                    S.op("dve", lambda e, hg=hg, ov=ov: e.tensor_tensor(ya[:, hg * 4:hg * 4 + 4, :], ov[:, :, 0:64],
                                                                       rec[:, hg * 4:hg * 4 + 4].unsqueeze(2).broadcast_to([128, 4, 64]), ALU.mult),
                         reads=["bank%d" % (4 + hg), "rec"], writes=["ya"])
                pb6 = bank_bf(6)
                for p in range(4):
                    S.op("pe", lambda e, p=p: e.transpose(pb6[:, p * 128:(p + 1) * 128], ya[:, 2 * p:2 * p + 2, :].rearrange("p h d -> p (h d)"), idb[:]),
                         reads=["ya", "idb"], writes=["bank6"])
                S.op("act", lambda e, qs=qs: e.copy(y_aT[:, :, qs], pb6[:, 0:512].rearrange("p (c t) -> p c t", t=128)), reads=["bank6"], writes=["y_aT"])
        S.barrier()
        with ExitStack() as p3:
          if STAGE >= 3:
            wg = sb(p3, "wg", [128, 8, 2048], BF16)
            pa = sb(p3, "pa", [128, 4, D], BF16)
            pb = sb(p3, "pb", [128, 4, D], BF16)
            wo = sb(p3, "wo", [128, 8, D], BF16)
            S.dma("sp", lambda e: e.dma_start(out=wg[:], in_=w_in_b.rearrange("(k p) n -> p k n", p=128)[:, :, 3328:5376]), reads=["w_in_b"], writes=["wg"])
            S.dma("sp", lambda e: e.dma_start(out=pa[:], in_=pa_b.rearrange("(k p) n -> p k n", p=128)), reads=["pa_b"], writes=["pa"])
            S.dma("sp", lambda e: e.dma_start(out=pb[:], in_=pb_b.rearrange("(k p) n -> p k n", p=128)), reads=["pb_b"], writes=["pb"])
            S.dma("sp", lambda e: e.dma_start(out=wo[:], in_=wo_b.rearrange("(k p) n -> p k n", p=128)), reads=["wo_b"], writes=["wo"])
            gat = sb(p3, "gat", [128, 16, 512], BF16)
            mT = sb(p3, "mT", [128, 8, 512], BF16)
            ta = sb(p3, "ta", [128, 512], F32)
            tb = sb(p3, "tb", [128, 512], F32)
            for g in range(NG):
                gs = slice(g * 512, (g + 1) * 512)
                hs = slice(512 + g * 512, 512 + (g + 1) * 512)
                for ct in range(16):
                    pbk = ct % 2
                    for k in range(8):
                        S.op("pe", lambda e, ct=ct, k=k, pbk=pbk: e.matmul(banks[pbk][:], wg[:, k, ct * 128:(ct + 1) * 128], hT_att[:, k, hs],
                                                                          start=(k == 0), stop=(k == 7)), reads=["wg", "hT_att"], writes=["bank%d" % pbk])
                    S.op("act", lambda e, ct=ct, pbk=pbk: e.activation(gat[:, ct, :], banks[pbk][:], AF.Sigmoid, bias=cv[:, C_GB + ct:C_GB + ct + 1]),
                         reads=["bank%d" % pbk, "cv"], writes=["gat"])
                for dt_ in range(8):
                    ba, bb = 2 + 2 * (dt_ % 2), 3 + 2 * (dt_ % 2)
                    for k in range(4):
                        S.op("pe", lambda e, dt_=dt_, k=k, ba=ba: e.matmul(banks[ba][:], pa[:, k, dt_ * 128:(dt_ + 1) * 128], y_aT[:, k, gs],
                                                                          start=(k == 0), stop=(k == 3)), reads=["pa", "y_aT"], writes=["bank%d" % ba])
                    for k in range(4):
                        S.op("pe", lambda e, dt_=dt_, k=k, bb=bb: e.matmul(banks[bb][:], pb[:, k, dt_ * 128:(dt_ + 1) * 128], y_bT[:, k, gs],
                                                                          start=(k == 0), stop=(k == 3)), reads=["pb", "y_bT"], writes=["bank%d" % bb])
                    S.op("dve", lambda e, dt_=dt_, ba=ba: e.tensor_tensor(ta[:], banks[ba][:], gat[:, dt_, :], ALU.mult),
                         reads=["bank%d" % ba, "gat"], writes=["ta"])
                    S.op("dve", lambda e, dt_=dt_, bb=bb: e.tensor_tensor(tb[:], banks[bb][:], gat[:, 8 + dt_, :], ALU.mult),
                         reads=["bank%d" % bb, "gat"], writes=["tb"])
                    S.op("pool", lambda e, dt_=dt_: e.tensor_tensor(mT[:, dt_, :], ta[:], tb[:], ALU.add), reads=["ta", "tb"], writes=["mT"])
                for tt in range(4):
                    it = g * 4 + tt
                    b = it % 2
                    S.dma("sp", lambda e, b=b, it=it: e.dma_start(out=xt[b][:], in_=xw[WIN - OWN + it * 128:WIN - OWN + (it + 1) * 128, :]),
                          writes=["xt%d" % b])
                    for half in range(2):
                        pbk = 6 + half
                        for k in range(8):
                            S.op("pe", lambda e, k=k, half=half, pbk=pbk, tt=tt: e.matmul(banks[pbk][:], mT[:, k, tt * 128:(tt + 1) * 128],
                                                                                         wo[:, k, half * 512:(half + 1) * 512], start=(k == 0), stop=(k == 7)),
                                 reads=["mT", "wo"], writes=["bank%d" % pbk])
                        S.op("act", lambda e, half=half, pbk=pbk: e.copy(mo[:, half * 512:(half + 1) * 512], banks[pbk][:]), reads=["bank%d" % pbk], writes=["mo"])
                    post_norm_res(S, nc, mo, "mo", xt[b], "xt%d" % b, grow[:, 0, :], junk, ssq, xt[b], "xt%d" % b)
                    S.dma("sp", lambda e, b=b, it=it: e.dma_start(out=x1s[it * 128:(it + 1) * 128, :], in_=xt[b][:]), reads=["xt%d" % b], writes=["x1s"])
        pm.close()
        S.barrier()
        with ExitStack() as p4:
          if STAGE >= 4:
            wus = [sb(p4, "wus%d" % i, [128, 8, 1024], BF16) for i in range(2)]
            wds = [sb(p4, "wds%d" % i, [128, 8, 1024], BF16) for i in range(2)]
            x1g = sb(p4, "x1g", [128, 4, D], F32)
            hfT = sb(p4, "hfT", [128, 8, 512], BF16)
            acc = sb(p4, "acc", [128, 4, D], F32)
            act_ = sb(p4, "act_", [128, 8, 512], BF16)
            rl = [sb(p4, "rl%d" % i, [128, 512], F32) for i in range(2)]
            sl_i = 0
            for g in range(NG):
                for tt in range(4):
                    it = g * 4 + tt
                    S.dma("sp", lambda e, tt=tt, it=it: e.dma_start(out=x1g[:, tt, :], in_=x1s[it * 128:(it + 1) * 128, :]), reads=["x1s"], writes=["x1g"])
                    norm_transpose(x1g[:, tt, :], "x1g", C_G3, hfT[:, :, tt * 128:(tt + 1) * 128], "hfT", 0)
                for s in range(4):
                    slot = sl_i % 2
                    sl_i += 1
                    S.dma("sp", lambda e, s=s, slot=slot: e.dma_start(out=wus[slot][:], in_=wu_b.rearrange("(k p) n -> p k n", p=128)[:, :, s * 1024:(s + 1) * 1024]),
                          reads=["wu_b"], writes=["wus%d" % slot])
                    S.dma("sp", lambda e, s=s, slot=slot: e.dma_start(out=wds[slot][:], in_=wd_b[s * 1024:(s + 1) * 1024, :].rearrange("(f p) n -> p f n", p=128)),
                          reads=["wd_b"], writes=["wds%d" % slot])
                    for f in range(8):
                        pbk = 1 + f % 2
                        for k in range(8):
                            S.op("pe", lambda e, f=f, k=k, pbk=pbk, slot=slot: e.matmul(banks[pbk][:], wus[slot][:, k, f * 128:(f + 1) * 128], hfT[:, k, :],
                                                                                       start=(k == 0), stop=(k == 7)), reads=["wus%d" % slot, "hfT"], writes=["bank%d" % pbk])
                        S.op("act", lambda e, f=f, pbk=pbk: e.activation(rl[f % 2][:], banks[pbk][:], AF.Relu), reads=["bank%d" % pbk], writes=["rl%d" % (f % 2)])
                        S.op("pool", lambda e, f=f: e.tensor_tensor(act_[:, f, :], rl[f % 2][:], rl[f % 2][:], ALU.mult), reads=["rl%d" % (f % 2)], writes=["act_"])
                    for tt in range(4):
                        for half in range(2):
                            pbk = 3 + (tt % 2) * 2 + half
                            for f in range(8):
                                S.op("pe", lambda e, f=f, tt=tt, half=half, pbk=pbk, slot=slot: e.matmul(
                                    banks[pbk][:], act_[:, f, tt * 128:(tt + 1) * 128], wds[slot][:, f, half * 512:(half + 1) * 512],
                                    start=(f == 0), stop=(f == 7)), reads=["act_", "wds%d" % slot], writes=["bank%d" % pbk])
                            if s == 0:
                                S.op("dve", lambda e, tt=tt, half=half, pbk=pbk: e.tensor_copy(acc[:, tt, half * 512:(half + 1) * 512], banks[pbk][:]),
                                     reads=["bank%d" % pbk], writes=["acc"])
                            else:
                                S.op("dve", lambda e, tt=tt, half=half, pbk=pbk: e.tensor_tensor(acc[:, tt, half * 512:(half + 1) * 512], banks[pbk][:],
                                                                                                acc[:, tt, half * 512:(half + 1) * 512], ALU.add),
                                     reads=["bank%d" % pbk, "acc"], writes=["acc"])
                for tt in range(4):
                    it = g * 4 + tt
                    post_norm_res(S, nc, acc[:, tt, :], "acc", x1g[:, tt, :], "x1g", grow[:, 1, :], junk, ssq, mo, "mo")
                    S.dma("sp", lambda e, it=it: e.dma_start(out=out[it * 128:(it + 1) * 128, :], in_=mo[:]), reads=["mo"], writes=["out"])
        if STAGE < 4:
            S.barrier()
            S.dma("sp", lambda e: e.dma_start(out=out[0:128, :], in_=xw[0:128, :]), reads=["w_in_b", "wd_b", "y_bT", "y_aT", "x1s"], writes=["out"])
        S.wait_all("sp", ["out"])
        S.emit()
    return nc


def post_norm_res(S, nc, u, uname, xres, xname, grow, junk, ssq, dst, dname):
    ua = u if isinstance(u, bass.AP) else u[:]
    xa = xres if isinstance(xres, bass.AP) else xres[:]
    da = dst if isinstance(dst, bass.AP) else dst[:]
    S.op("act", lambda e: e.activation(junk[:], ua, AF.Square, accum_out=ssq[:, 0:1]), reads=[uname], writes=["junk", "ssq"])
    S.op("act", lambda e: e.activation(ssq[:, 1:2], ssq[:, 0:1], AF.Sqrt, bias=RMS_EPS, scale=1.0 / D), reads=["ssq"], writes=["ssq"])
    S.op("dve", lambda e: e.reciprocal(ssq[:, 2:3], ssq[:, 1:2]), reads=["ssq"], writes=["ssq"])
    S.op("dve", lambda e: e.scalar_tensor_tensor(ua, ua, ssq[:, 2:3], grow, ALU.mult, ALU.mult), reads=[uname, "ssq", "grow"], writes=[uname])
    S.op("pool", lambda e: e.tensor_tensor(da, ua, xa, ALU.add), reads=[uname, xname], writes=[dname])


def _host_consts(inp):
    f = np.float32
    g = lambda n: np.asarray(inp[n], dtype=f)[0]
    cvec = np.zeros((128, 64), f)
    cvec[:, 0:8] = g("pre_mix_g").reshape(8, 128).T
    cvec[:, 8:16] = g("pre_ffn_g").reshape(8, 128).T
    cvec[:, 16:32] = g("gate_bias").reshape(16, 128).T
    mu = g("shift_mu")
    colmap = [512 + c * 128 for c in range(4)] + [1024 + c * 128 for c in range(4)] + [1536] + [c * 128 for c in range(4)] + [1664]
    for t, co in enumerate(colmap):
        cvec[:, 32 + t] = mu[co:co + 128]
    cvec[:, 46:50] = g("w0").reshape(4, 128).T
    cvec[:, 50:54] = g("a0").reshape(4, 128).T
    cvec[:, 54:58] = g("k_k").reshape(4, 128).T
    cvec[:, 58:62] = g("k_a").reshape(4, 128).T
    cvec2 = np.zeros((128, 16), f)
    cvec2[:, 0:4] = g("r_k").reshape(512).reshape(4, 128).T
    cvec2[:, 4:8] = g("ln_x_w").reshape(4, 128).T
    cvec2[:, 8:12] = g("ln_x_b").reshape(4, 128).T
    rows = np.stack([g("post_mix_g"), g("post_ffn_g")], 0)
    rb = np.concatenate([g("rel_bias"), np.full((8, 1), -1e30, f)], axis=1)
    kj = np.arange(128)[:, None]
    qi = np.arange(128)[None, :]
    bt = np.zeros((128, 5, 8, 128), f)
    for dl in range(5):
        dist = 128 * dl + qi - kj
        idx = np.clip(dist, -63, 256) + 63
        cd = 2 * dl + qi // 64 - kj // 64
        idx = np.where((cd >= 0) & (cd <= 8), idx, 320)
        bt[:, dl, :, :] = rb[:, idx].transpose(1, 0, 2)
    cmat = np.zeros((128, 768), f)
    cmat[:, 0:128] = np.eye(128, dtype=f)
    cmat[0:64, 128:192] = 1.0
    cmat[64:128, 192:256] = 1.0
    s_ = np.arange(64)[:, None]
    t_ = np.arange(64)[None, :]
    cmat[0:64, 256:320] = (s_ < t_)
    cmat[0:64, 384:448] = (s_ <= t_)
    cmat[0:64, 512:576] = (t_ < s_)
    return dict(cvec=cvec, cvec2=cvec2, rows=rows, biasT=bt.reshape(128, 5 * 8 * 128), cmat=cmat,
                w2=g("w2"), a2=g("a2"), g2=g("g2"), w_in=g("w_in"), proj_a=g("proj_a"), proj_b=g("proj_b"),
                w_out=g("w_out"), w_up=g("w_up"), w_down=g("w_down"))


_NC_CACHE = {}


def run(inputs, trace=False):
    x = np.asarray(inputs["x"], dtype=np.float32)
    B, SEQ, _ = x.shape
    OWN = SEQ // 4
    WIN = SEQ
    NTA = OWN // 128 + 4
    consts = _host_consts(inputs)
    in_maps = []
    for c in range(8):
        b, j = c // 4, c % 4
        end = (j + 1) * OWN
        xw = np.zeros((WIN, D), np.float32)
        xw[WIN - end:, :] = x[b, 0:end, :]
        pos = end - NTA * 128 + np.arange(NTA * 128)
        valid = (pos >= 0).astype(np.float32).reshape(NTA, 128).T.copy()
        m = dict(consts)
        m["xw"] = xw
        m["valid"] = valid
        in_maps.append(m)
    key = (WIN, OWN)
    if key not in _NC_CACHE:
        _NC_CACHE[key] = build_nc(WIN, OWN)
    nc = _NC_CACHE[key]
    res = run_bass_kernel_spmd(nc, in_maps, core_ids=list(range(8)))
    outp = np.zeros((B, SEQ, D), np.float32)
    for c in range(8):
        b, j = c // 4, c % 4
        outp[b, j * OWN:(j + 1) * OWN, :] = res.results[c]["out"]
    return outp


def kernel(**inputs):
    return run(inputs)
```

```python
import numpy as np
import ml_dtypes
import concourse.bass as bass
import concourse.mybir as mybir
from concourse.bass_utils import run_bass_kernel_spmd
from contextlib import ExitStack

F32 = mybir.dt.float32
BF16 = mybir.dt.bfloat16
AF = mybir.ActivationFunctionType
ALU = mybir.AluOpType

D = 1024
N_DMA_SEMS = 24
RMS_EPS = 1e-6
GN_EPS = 64 * 1e-5
DEC_C = float(np.exp(-0.5))


class _Rec:
    def __getattr__(self, name):
        def f(*a, **k):
            self.call = (name, a, k)
        return f


class Sched:
    ENGS = ("pe", "act", "dve", "pool", "sp")

    def __init__(self, nc, stack):
        self.nc = nc
        self.prog = {e: [] for e in self.ENGS}
        self.cnt = {e: 0 for e in self.ENGS}
        self.sems = {}
        for e in self.ENGS:
            self.sems["p_" + e] = stack.enter_context(nc.semaphore("prog_" + e))
        for i in range(N_DMA_SEMS):
            self.sems["d%d" % i] = stack.enter_context(nc.semaphore("dma%d" % i))
        self.dcnt = [0] * N_DMA_SEMS
        self.dnx = {"sp": 0, "pool": 0, "act": 0}
        self.seen = {e: {} for e in self.ENGS}
        self.lastw = {}
        self.reads = {}

    def _deps(self, eng, reads, writes):
        deps = {}
        for b in reads:
            for s, v in self.lastw.get(b, {}).items():
                if deps.get(s, 0) < v:
                    deps[s] = v
        for b in writes:
            for s, v in self.lastw.get(b, {}).items():
                if deps.get(s, 0) < v:
                    deps[s] = v
            for s, v in self.reads.get(b, {}).items():
                if deps.get(s, 0) < v:
                    deps[s] = v
        waits = []
        for s, v in deps.items():
            if s == "p_pe" and eng == "pe":
                continue
            if self.seen[eng].get(s, 0) < v:
                self.seen[eng][s] = v
                waits.append((s, v))
        return waits

    def _commit(self, tok, reads, writes):
        s, v = tok
        for b in writes:
            self.lastw.setdefault(b, {})[s] = v
        for b in reads:
            self.reads.setdefault(b, {})[s] = v

    @staticmethod
    def _rec(fn):
        r = _Rec()
        fn(r)
        return r.call

    def op(self, eng, fn, reads=(), writes=()):
        fn = self._rec(fn)
        waits = self._deps(eng, reads, writes)
        self.cnt[eng] += 1
        tok = ("p_" + eng, self.cnt[eng])
        self._commit(tok, reads, writes)
        self.prog[eng].append((waits, fn, ("p_" + eng, 1)))

    def dma(self, eng, fn, reads=(), writes=()):
        fn = self._rec(fn)
        half = N_DMA_SEMS // 2
        base = 0 if eng == "sp" else half
        i = base + self.dnx[eng]
        self.dnx[eng] = (self.dnx[eng] + 1) % half
        waits = self._deps(eng, reads, writes)
        s = "d%d" % i
        if self.dcnt[i] > 0 and self.seen[eng].get(s, 0) < self.dcnt[i]:
            self.seen[eng][s] = self.dcnt[i]
            waits.append((s, self.dcnt[i]))
        self.dcnt[i] += 16
        self._commit((s, self.dcnt[i]), reads, writes)
        self.prog[eng].append((waits, fn, (s, 16)))

    def barrier(self):
        cur = {"p_" + e: self.cnt[e] for e in self.ENGS}
        for i in range(N_DMA_SEMS):
            cur["d%d" % i] = self.dcnt[i]
        for eng in self.ENGS:
            waits = []
            for s_, v in cur.items():
                if s_ == "p_" + eng and eng == "pe":
                    continue
                if v > 0 and self.seen[eng].get(s_, 0) < v:
                    self.seen[eng][s_] = v
                    waits.append((s_, v))
            self.prog[eng].append((waits, None, None))

    def wait_all(self, eng, bufs):
        waits = self._deps(eng, bufs, ())
        self.prog[eng].append((waits, None, None))

    def emit(self):
        nc = self.nc
        with nc.Block() as block:
            def replay(name, e):
                for waits, fn, inc in self.prog[name]:
                    for s, v in waits:
                        e.wait_ge(self.sems[s], v)
                    if fn is None:
                        continue
                    getattr(e, fn[0])(*fn[1], **fn[2]).then_inc(self.sems[inc[0]], inc[1])

            @block.tensor
            def _(e):
                replay("pe", e)

            @block.scalar
            def _(e):
                replay("act", e)

            @block.vector
            def _(e):
                replay("dve", e)

            @block.gpsimd
            def _(e):
                replay("pool", e)

            @block.sync
            def _(e):
                replay("sp", e)


import os
STAGE = int(os.environ.get("STAGE", "9"))
SUB = int(os.environ.get("SUB", "99"))
NTL = int(os.environ.get("NTL", "9999"))
SUB2 = int(os.environ.get("SUB2", "99"))
LAG = int(os.environ.get("LAG", "9"))


def build_nc(WIN, OWN, dbg=False):
    NTW = WIN // 128
    NTO = OWN // 128
    NTA = NTO + 4
    NG = OWN // 512
    nc = bass.Bass("TRN2", target_bir_lowering=False)

    def din(name, shape, dt=F32):
        return nc.dram_tensor(name, list(shape), dt, kind="ExternalInput").ap()

    xw = din("xw", [WIN, D])
    w_in = din("w_in", [D, 5376])
    proj_a = din("proj_a", [512, D])
    proj_b = din("proj_b", [512, D])
    w_out = din("w_out", [D, D])
    w_up = din("w_up", [D, 4096])
    w_down = din("w_down", [4096, D])
    cvec = din("cvec", [128, 64])
    rows = din("rows", [2, D])
    w2d = din("w2", [64, 512])
    a2d = din("a2", [64, 512])
    g2d = din("g2", [128, 512])
    biasd = din("biasT", [128, 5 * 8 * 128])
    validd = din("valid", [128, NTA])
    cmat = din("cmat", [128, 6 * 128])
    out = nc.dram_tensor("out", [OWN, D], F32, kind="ExternalOutput").ap()
    dbg_o = {}

    w_in_b = nc.dram_tensor("w_in_b", [D, 5376], BF16).ap()
    pa_b = nc.dram_tensor("pa_b", [512, D], BF16).ap()
    pb_b = nc.dram_tensor("pb_b", [512, D], BF16).ap()
    wo_b = nc.dram_tensor("wo_b", [D, D], BF16).ap()
    wu_b = nc.dram_tensor("wu_b", [D, 4096], BF16).ap()
    wd_b = nc.dram_tensor("wd_b", [4096, D], BF16).ap()
    x1s = nc.dram_tensor("x1s", [OWN, D], F32).ap()
    hT_s = nc.dram_tensor("hT_s", [128, 8, NTA * 128], BF16).ap()

    C_G1, C_G3, C_GB, C_MU = 0, 8, 16, 32
    C_W0, C_A0, C_KK, C_KA = 46, 50, 54, 58
    cvec2 = din("cvec2", [128, 16])
    C2_RK, C2_LW, C2_LB, C2_OMKA = 0, 4, 8, 12

    with ExitStack() as st:
        S = Sched(nc, st)

        def sb(stack, name, shape, dt):
            return stack.enter_context(nc.sbuf_tensor(name, list(shape), dt))

        def psb(stack, name, shape, dt=F32):
            return stack.enter_context(nc.psum_tensor(name, list(shape), dt))

        cv = sb(st, "cv", [128, 64], F32)
        cv2 = sb(st, "cv2", [128, 16], F32)
        cm = sb(st, "cm", [128, 768], F32)
        idb = sb(st, "idb", [128, 128], BF16)
        mskb = sb(st, "mskb", [64, 3, 64], BF16)
        S.dma("sp", lambda e: e.dma_start(out=cv[:], in_=cvec), writes=["cv"])
        S.dma("sp", lambda e: e.dma_start(out=cv2[:], in_=cvec2), writes=["cv2"])
        S.dma("sp", lambda e: e.dma_start(out=cm[:], in_=cmat), writes=["cm"])
        ident = cm[:, 0:128]
        bones = cm[:, 128:256]
        scanm = cm[:, 640:768]
        S.op("dve", lambda e: e.tensor_copy(idb[:], ident), reads=["cm"], writes=["idb"])
        S.op("dve", lambda e: e.tensor_copy(mskb[:], cm[0:64, 256:640].rearrange("p (a b) -> p a b", b=128)[:, :, 0:64]),
             reads=["cm"], writes=["mskb"])

        def conv(dst, src, nrows, nm):
            for r in range(0, nrows, 128):
                S.dma("pool", lambda e, r=r: e.dma_start(out=dst[r:r + 128, :], in_=src[r:r + 128, :]), writes=[nm])
        conv(w_in_b, w_in, D, "w_in_b")
        conv_tasks = []
        for dst_, src_, nrows_, nm_ in ((pa_b, proj_a, 512, "pa_b"), (pb_b, proj_b, 512, "pb_b"), (wo_b, w_out, D, "wo_b"),
                                       (wu_b, w_up, D, "wu_b"), (wd_b, w_down, 4096, "wd_b")):
            for r_ in range(0, nrows_, 128):
                conv_tasks.append((dst_, src_, r_, nm_))

        def emit_conv(k):
            for _ in range(k):
                if conv_tasks:
                    dst_, src_, r_, nm_ = conv_tasks.pop(0)
                    S.dma("pool", lambda e: e.dma_start(out=dst_[r_:r_ + 128, :], in_=src_[r_:r_ + 128, :]), writes=[nm_])

        y_bT = sb(st, "y_bT", [128, 4, OWN], BF16)
        xt = [sb(st, "xt%d" % i, [128, D], F32) for i in range(2)]
        junk = sb(st, "junk", [128, D], BF16)
        xs = sb(st, "xs", [128, D], BF16)
        ssq = sb(st, "ssq", [128, 4], F32)

        banks = [psb(st, "bank%d" % i, [128, 512]) for i in range(8)]

        def bank_bf(i):
            return banks[i][:].bitcast(BF16)

        def norm_transpose(src_tile, srcname, gcol, dst_ap, dstname, pbank):
            S.op("act", lambda e: e.activation(junk[:], src_tile, AF.Square, accum_out=ssq[:, 0:1]),
                 reads=[srcname], writes=["junk", "ssq"])
            S.op("act", lambda e: e.activation(ssq[:, 1:2], ssq[:, 0:1], AF.Sqrt, bias=RMS_EPS, scale=1.0 / D),
                 reads=["ssq"], writes=["ssq"])
            S.op("dve", lambda e: e.reciprocal(ssq[:, 2:3], ssq[:, 1:2]), reads=["ssq"], writes=["ssq"])
            S.op("dve", lambda e: e.tensor_scalar(xs[:], src_tile, ssq[:, 2:3], 0.0, ALU.mult, ALU.add),
                 reads=[srcname, "ssq"], writes=["xs"])
            pb = bank_bf(pbank)
            for k in range(8):
                S.op("pe", lambda e, k=k: e.transpose(pb[:, k * 128:(k + 1) * 128], xs[:, k * 128:(k + 1) * 128], idb[:]),
                     reads=["xs", "idb"], writes=["bank%d" % pbank])
            S.op("dve", lambda e: e.tensor_tensor(dst_ap, pb[:, 0:1024].rearrange("p (k t) -> p k t", t=128),
                                                 cv[:, gcol:gcol + 8].unsqueeze(2).broadcast_to([128, 8, 128]), ALU.mult),
                 reads=["bank%d" % pbank, "cv"], writes=[dstname])

        with ExitStack() as p1:
          if STAGE >= 1:
            wr = sb(p1, "wr", [128, 8, 1792], BF16)
            S.dma("sp", lambda e: e.dma_start(out=wr[:], in_=w_in_b.rearrange("(k p) n -> p k n", p=128)[:, :, 1536:3328]),
                  reads=["w_in_b"], writes=["wr"])
            w2s = sb(p1, "w2s", [64, 512], F32)
            a2s = sb(p1, "a2s", [128, 512], F32)
            g2s = sb(p1, "g2s", [128, 512], BF16)
            S.dma("sp", lambda e: e.dma_start(out=w2s[:], in_=w2d), writes=["w2s"])
            S.dma("sp", lambda e: e.dma_start(out=a2s[64:128, :], in_=a2d), writes=["a2s"])
            S.dma("pool", lambda e: e.dma_start(out=g2s[:], in_=g2d), writes=["g2s"])
            hTr = [sb(p1, "hTr%d" % i, [128, 8, 128], BF16) for i in range(2)]
            PR = [sb(p1, "PR%d" % i, [128, 14, 129], F32) for i in range(2)]
            S.op("pool", lambda e: e.memset(PR[1][:], 0.0), writes=["PR1"])
            S.op("pool", lambda e: e.memset(PR[0][:], 0.0), writes=["PR0"])
            PS = sb(p1, "PS", [128, 14, 128], F32)

            def f32t(name):
                return sb(p1, name, [128, 4, 128], F32)
            tcw = sb(p1, "tcw", [64, 128], F32)
            cmask4 = sb(p1, "cmask4", [128, 512], F32)
            S.op("pool", lambda e: e.memset(cmask4[:], 1.0), writes=["cmask4"])
            S.op("pool", lambda e: e.memset(cmask4[:].rearrange("p (c t) -> p c t", t=64)[:, :, 0:1], 0.0), writes=["cmask4"])
            S.op("dve", lambda e: e.tensor_scalar(cv2[:, 12:16], cv[:, C_KA:C_KA + 4], -1.0, 1.0, ALU.mult, ALU.add), reads=["cv"], writes=["cv2"])
            sgc = sb(p1, "sgc", [128, 128], BF16)
            lw, ar, kk, k2, rn, kp, bq, cum, e1, e2, e3, rk, dv, d2 = [f32t(n) for n in
                ("lw", "ar", "kk", "k2", "rn", "kp", "bq", "cum", "e1", "e2", "e3", "rk", "dv", "d2")]
            t1 = k2
            AT = [sb(p1, "AT%d" % i, [128, 4, 128], BF16) for i in range(2)]
            BT = [sb(p1, "BT%d" % i, [128, 4, 128], BF16) for i in range(2)]
            KT = [sb(p1, "KT%d" % i, [128, 4, 128], BF16) for i in range(2)]
            RT = [sb(p1, "RT%d" % i, [128, 4, 128], BF16) for i in range(2)]
            VT = [sb(p1, "VT%d" % i, [128, 4, 128], BF16) for i in range(2)]
            ATm = [[sb(p1, "ATm%d_%d" % (i, q), [128, 4, 128], BF16) for q in range(2)] for i in range(2)]
            RTm = [[sb(p1, "RTm%d_%d" % (i, q), [128, 4, 128], BF16) for q in range(2)] for i in range(2)]
            cvm = sb(p1, "cvm", [128, 2], F32)
            S.op("pool", lambda e: e.memset(cvm[:], 0.0), writes=["cvm"])
            S.op("pool", lambda e: e.memset(cvm[0:64, 0:1], 1.0), writes=["cvm"])
            S.op("pool", lambda e: e.memset(cvm[64:128, 1:2], 1.0), writes=["cvm"])
            S.op("pool", lambda e: e.memset(a2s[0:64, :], 0.0), writes=["a2s"])
            gam = [sb(p1, "gam%d" % i, [64, 8, 2], F32) for i in range(2)]
            gT = [f32t("gT%d" % i) for i in range(2)]
            bon = [f32t("bon%d" % i) for i in range(2)]
            yv = [f32t("yv%d" % i) for i in range(2)]
            TM = [sb(p1, "TM%d" % i, [64, 4, 512], BF16) for i in range(2)]
            Vpad = [sb(p1, "Vpad%d" % i, [64, 8, 128], BF16) for i in range(2)]
            Wpad = [sb(p1, "Wpad%d" % i, [64, 8, 128], BF16) for i in range(2)]
            Hpad = [sb(p1, "Hpad%d" % i, [64, 8, 128], BF16) for i in range(2)]
            SC1 = [sb(p1, "SC1_%d" % i, [64, 8, 128], BF16) for i in range(2)]
            SC2 = [sb(p1, "SC2_%d" % i, [64, 8, 128], BF16) for i in range(2)]
            SC3 = [sb(p1, "SC3_%d" % i, [64, 8, 64], BF16) for i in range(2)]
            Zb = [sb(p1, "Zb%d" % i, [64, 8, 128], BF16) for i in range(2)]
            PP = [[sb(p1, "PP%d_%d" % (i, k), [64, 2, 8, 64], BF16) for k in range(2)] for i in range(2)]
            Ef = [sb(p1, "Ef%d" % i, [64, 8, 64], BF16) for i in range(2)]
            Hb = sb(p1, "Hb", [64, 8, 64], BF16)
            S.op("pool", lambda e: e.memset(Hb[:], 0.0), writes=["Hb"])
            QTb = [sb(p1, "QTb%d" % i, [64, 8, 64], BF16) for i in range(2)]
            Hf = sb(p1, "Hf", [64, 8, 64], F32)
            for s_ in range(2):
                for t_, n_ in ((Vpad[s_], "Vpad%d" % s_), (Wpad[s_], "Wpad%d" % s_), (Hpad[s_], "Hpad%d" % s_)):
                    S.op("pool", lambda e, t_=t_: e.memset(t_[:], 0.0), writes=[n_])
            S.op("pool", lambda e: e.memset(Hf[:], 0.0), writes=["Hf"])

            def padview(t):
                return bass.AP(t[:].tensor, t[:].offset, [list(t[:].ap[0]), [256, 4], [192, 2], [1, 64]])

            colmap = [512 + c * 128 for c in range(4)] + [1024 + c * 128 for c in range(4)] + [1536] + \
                     [c * 128 for c in range(4)] + [1664]
            XB = (6, 7)
            HORD = (0, 2, 4, 6, "sep", 1, 3, 5, 7)

            def prep_gen(i):
                own = i >= NTW - NTO
                att = i >= NTW - NTA
                ia = i - (NTW - NTA)
                b = i % 2
                ntl = 14 if i >= NTW - NTO - 1 else 9
                S.dma("sp", lambda e: e.dma_start(out=xt[b][:], in_=xw[i * 128:(i + 1) * 128, :]), writes=["xt%d" % b])
                hsrc = hTr[b][:]
                hname = "hTr%d" % b
                norm_transpose(xt[b][:], "xt%d" % b, C_G1, hsrc, hname, XB[0])
                emit_conv(1)
                if att:
                    S.dma("sp", lambda e: e.dma_start(out=hT_s[:, :, ia * 128:(ia + 1) * 128], in_=hTr[b][:]), reads=[hname], writes=["hT_s"])
                yield
                for gi, g0 in enumerate(range(0, ntl, 4)):
                    gn = min(4, ntl - g0)
                    pbk = XB[(gi + 1) % 2]
                    for c in range(gn):
                        co = colmap[g0 + c]
                        for k in range(8):
                            S.op("pe", lambda e, c=c, co=co, k=k: e.matmul(
                                banks[pbk][:, c * 128:(c + 1) * 128], wr[:, k, co:co + 128], hsrc[:, k, :],
                                start=(k == 0), stop=(k == 7)), reads=["wr", hname], writes=["bank%d" % pbk])
                    S.op("act", lambda e: e.copy(
                        PR[b][:, g0:g0 + gn, 1:129], banks[pbk][:, 0:gn * 128].rearrange("p (c t) -> p c t", t=128)),
                        reads=["bank%d" % pbk], writes=["PR%d" % b])
                    yield
                S.op("pool", lambda e: e.tensor_copy(PR[b][:, :, 0:1], PR[1 - b][:, :, 128:129]),
                     reads=["PR%d" % (1 - b)], writes=["PR%d" % b])
                S.op("pool", lambda e: e.tensor_tensor(PS[:, 0:ntl, :], PR[b][:, 0:ntl, 0:128], PR[b][:, 0:ntl, 1:129], ALU.subtract),
                     reads=["PR%d" % b], writes=["PS"])
                S.op("pool", lambda e: e.tensor_tensor(PS[:, 0:ntl, :], PS[:, 0:ntl, :],
                                                     cv[:, C_MU:C_MU + ntl].unsqueeze(2).broadcast_to([128, ntl, 128]), ALU.mult),
                     reads=["PS", "cv"], writes=["PS"])
                S.op("pool", lambda e: e.tensor_tensor(PS[:, 0:ntl, :], PS[:, 0:ntl, :], PR[b][:, 0:ntl, 1:129], ALU.add),
                     reads=["PS", "PR%d" % b], writes=["PS"])
                kS, vS, rS, cgS = PS[:, 0:4, :], PS[:, 4:8, :], PS[:, 9:13, :], PS[:, 13, :]
                yield
                S.op("act", lambda e: e.activation(tcw[:], PS[0:64, 8, :], AF.Tanh), reads=["PS"], writes=["tcw"])
                for p in range(4):
                    S.op("pe", lambda e, p=p: e.matmul(banks[XB[0]][:, p * 128:(p + 1) * 128], w2s[:, p * 128:(p + 1) * 128], tcw[:],
                                                      start=True, stop=True), reads=["w2s", "tcw"], writes=["bank%d" % XB[0]])
                    S.op("pe", lambda e, p=p: e.matmul(banks[XB[1]][:, p * 128:(p + 1) * 128], a2s[:, p * 128:(p + 1) * 128], PS[:, 8, :],
                                                      start=True, stop=True), reads=["a2s", "PS"], writes=["bank%d" % XB[1]])
                for p in range(4):
                    S.op("act", lambda e, p=p: e.activation(lw[:, p, :], banks[XB[0]][:, p * 128:(p + 1) * 128], AF.Sigmoid,
                                                           bias=cv[:, C_W0 + p:C_W0 + p + 1]), reads=["bank%d" % XB[0], "cv"], writes=["lw"])
                    S.op("act", lambda e, p=p: e.activation(ar[:, p, :], banks[XB[1]][:, p * 128:(p + 1) * 128], AF.Sigmoid,
                                                           bias=cv[:, C_A0 + p:C_A0 + p + 1]), reads=["bank%d" % XB[1], "cv"], writes=["ar"])
                if own:
                    S.op("act", lambda e: e.activation(sgc[:], cgS, AF.Sigmoid), reads=["PS"], writes=["sgc"])
                yield
                S.op("dve", lambda e: e.tensor_scalar(lw[:], lw[:], -DEC_C, 0.0, ALU.mult, ALU.add), reads=["lw"], writes=["lw"])
                S.op("dve", lambda e: e.tensor_tensor_scan(cum[:].rearrange("p a t -> p (a t)"), cmask4[:],
                                                         lw[:].rearrange("p a t -> p (a t)"), 0.0, ALU.mult, ALU.add),
                     reads=["lw", "cmask4"], writes=["cum"])
                S.op("act", lambda e: e.activation(e1[:], cum[:], AF.Exp), reads=["cum"], writes=["e1"])
                S.op("act", lambda e: e.activation(e2[:], cum[:], AF.Exp, scale=-1.0), reads=["cum"], writes=["e2"])
                S.op("pool", lambda e: e.tensor_tensor(e3[:], cum[:], lw[:], ALU.subtract), reads=["cum", "lw"], writes=["e3"])
                S.op("act", lambda e: e.activation(e3[:], e3[:], AF.Exp), reads=["e3"], writes=["e3"])
                S.op("dve", lambda e: e.tensor_tensor(kk[:], kS, cv[:, C_KK:C_KK + 4].unsqueeze(2).broadcast_to([128, 4, 128]), ALU.mult),
                     reads=["PS", "cv"], writes=["kk"])
                S.op("pool", lambda e: e.tensor_tensor(k2[:], kk[:], kk[:], ALU.mult), reads=["kk"], writes=["k2"])
                S.op("pe", lambda e: e.matmul(banks[XB[0]][:], bones, k2[:].rearrange("p a t -> p (a t)"), start=True, stop=True),
                     reads=["cm", "k2"], writes=["bank%d" % XB[0]])
                S.op("act", lambda e: e.activation(rn[:].rearrange("p a t -> p (a t)"), banks[XB[0]][:], AF.Sqrt), reads=["bank%d" % XB[0]], writes=["rn"])
                yield
                S.op("dve", lambda e: e.tensor_scalar(rn[:], rn[:], 1e-12, 0.0, ALU.max, ALU.add), reads=["rn"], writes=["rn"])
                S.op("dve", lambda e: e.reciprocal(rn[:], rn[:]), reads=["rn"], writes=["rn"])
                S.op("dve", lambda e: e.tensor_tensor(kk[:], kk[:], rn[:], ALU.mult), reads=["kk", "rn"], writes=["kk"])
                S.op("pool", lambda e: e.tensor_tensor(t1[:], ar[:], cv[:, C_KA:C_KA + 4].unsqueeze(2).broadcast_to([128, 4, 128]), ALU.mult),
                     reads=["ar", "cv"], writes=["k2"])
                S.op("pool", lambda e: e.tensor_tensor(t1[:], t1[:], cv2[:, C2_OMKA:C2_OMKA + 4].unsqueeze(2).broadcast_to([128, 4, 128]), ALU.add),
                     reads=["k2", "cv2"], writes=["k2"])
                S.op("pool", lambda e: e.tensor_tensor(kp[:], kS, t1[:], ALU.mult), reads=["PS", "k2"], writes=["kp"])
                S.op("dve", lambda e: e.tensor_tensor(bq[:], kk[:], ar[:], ALU.mult), reads=["kk", "ar"], writes=["bq"])
                S.op("dve", lambda e: e.scalar_tensor_tensor(AT[b][:], kk[:], -1.0, e3[:], ALU.mult, ALU.mult),
                     reads=["kk", "e3"], writes=["AT%d" % b])
                for q in range(2):
                    S.op("pool", lambda e, q=q: e.tensor_scalar(ATm[b][q][:], AT[b][:], cvm[:, q:q + 1], 0.0, ALU.mult, ALU.add),
                         reads=["AT%d" % b, "cvm"], writes=["ATm%d_%d" % (b, q)])
                S.op("dve", lambda e: e.tensor_tensor(BT[b][:], bq[:], e2[:], ALU.mult), reads=["bq", "e2"], writes=["BT%d" % b])
                S.op("pool", lambda e: e.tensor_tensor(KT[b][:], kp[:], e2[:], ALU.mult), reads=["kp", "e2"], writes=["KT%d" % b])
                S.op("pool", lambda e: e.tensor_copy(VT[b][:], vS), reads=["PS"], writes=["VT%d" % b])
                if own:
                    S.op("dve", lambda e: e.tensor_tensor(RT[b][:], rS, e1[:], ALU.mult), reads=["PS", "e1"], writes=["RT%d" % b])
                    for q in range(2):
                        S.op("pool", lambda e, q=q: e.tensor_scalar(RTm[b][q][:], RT[b][:], cvm[:, q:q + 1], 0.0, ALU.mult, ALU.add),
                             reads=["RT%d" % b, "cvm"], writes=["RTm%d_%d" % (b, q)])
                for h in range(8):
                    p, q = h // 2, h % 2
                    S.op("pe", lambda e, h=h, p=p, q=q: e.matmul(
                        banks[XB[1]][0:64, h * 2:h * 2 + 2], cm[:, 64 * q:64 * q + 64],
                        e1[:, p, :].rearrange("p (c t) -> p c t", t=64)[:, :, 63], start=True, stop=True),
                        reads=["cm", "e1"], writes=["bank%d" % XB[1]])
                S.op("dve", lambda e: e.tensor_copy(gam[b][:].rearrange("p h c -> p (h c)"), banks[XB[1]][0:64, 0:16]),
                     reads=["bank%d" % XB[1]], writes=["gam%d" % b])
                yield
                if own:
                    for p in range(4):
                        S.op("pe", lambda e, p=p: e.matmul(banks[XB[0]][:, p * 128:(p + 1) * 128], g2s[:, p * 128:(p + 1) * 128], sgc[:],
                                                          start=True, stop=True), reads=["g2s", "sgc"], writes=["bank%d" % XB[0]])
                    S.op("act", lambda e: e.copy(gT[b][:].rearrange("p a t -> p (a t)"), banks[XB[0]][:]), reads=["bank%d" % XB[0]], writes=["gT%d" % b])
                    S.op("pool", lambda e: e.tensor_tensor(rk[:], rS, kp[:], ALU.mult), reads=["PS", "kp"], writes=["rk"])
                    S.op("pool", lambda e: e.tensor_tensor(rk[:], rk[:], cv2[:, C2_RK:C2_RK + 4].unsqueeze(2).broadcast_to([128, 4, 128]), ALU.mult),
                         reads=["rk", "cv2"], writes=["rk"])
                    S.op("pe", lambda e: e.matmul(banks[XB[1]][:], bones, rk[:].rearrange("p a t -> p (a t)"), start=True, stop=True),
                         reads=["cm", "rk"], writes=["bank%d" % XB[1]])
                    S.op("dve", lambda e: e.tensor_tensor(bon[b][:], banks[XB[1]][:].rearrange("p (a t) -> p a t", t=128), vS, ALU.mult),
                         reads=["bank%d" % XB[1], "PS"], writes=["bon%d" % b])
                    yield

            def chunk_gen(n):
                i, c = n // 2, n % 2
                s = n % 2
                b = i % 2
                own = i >= NTW - NTO
                io = i - (NTW - NTO)
                cs = slice(c * 64, (c + 1) * 64)
                Z0, Z1, WB = 3 * s, 3 * s + 1, 3 * s + 2
                zb_of = lambda h: (Z0 if h < 4 else Z1)
                bn = lambda k: "bank%d" % k
                ATn, BTn, KTn, RTn, VTn = "AT%d" % b, "BT%d" % b, "KT%d" % b, "RT%d" % b, "VT%d" % b
                TMn, SC1n, SC2n, SC3n, Zbn = "TM%d" % s, "SC1_%d" % s, "SC2_%d" % s, "SC3_%d" % s, "Zb%d" % s
                tm, sc1, sc2, sc3, zbs = TM[s], SC1[s], SC2[s], SC3[s], Zb[s]

                def sep(bank, col):
                    S.op("pe", lambda e: e.matmul(banks[bank][0:64, col:col + 64], idb[:, 0:64], idb[:, 0:64], start=True, stop=True),
                         reads=["idb"], writes=[bn(bank)])
                for qi_, (src, sn, bk) in enumerate(((AT[b], ATn, Z0), (BT[b], BTn, Z1))):
                    for p in range(4):
                        S.op("pe", lambda e, src=src, p=p, bk=bk: e.matmul(banks[bk][0:64, p * 128:(p + 1) * 128], src[:, p, cs], idb[:], start=True, stop=True),
                             reads=[sn, "idb"], writes=[bn(bk)])
                S.op("act", lambda e: e.copy(tm[:, 0, :], banks[Z0][0:64, :]), reads=[bn(Z0)], writes=[TMn])
                S.op("dve", lambda e: e.tensor_copy(tm[:, 1, :], banks[Z1][0:64, :]), reads=[bn(Z1)], writes=[TMn])
                for h in range(8):
                    p, q = h // 2, h % 2
                    S.op("pe", lambda e, h=h, p=p, q=q: e.matmul(banks[WB][0:64, h * 64:(h + 1) * 64], ATm[b][q][:, p, cs], BT[b][:, p, cs],
                                                                start=True, stop=True), reads=["ATm%d_%d" % (b, q), BTn], writes=[bn(WB)])
                S.op("dve", lambda e: e.tensor_tensor(sc3[:], banks[WB][0:64, :].rearrange("p (h t) -> p h t", t=64),
                                                     mskb[:, 2, :].unsqueeze(1).broadcast_to([64, 8, 64]), ALU.mult),
                     reads=[bn(WB), "mskb"], writes=[SC3n])
                yield
                for qi_, (src, sn, bk) in enumerate(((KT[b], KTn, Z0), (VT[b], VTn, Z1))):
                    for p in range(4):
                        S.op("pe", lambda e, src=src, p=p, bk=bk: e.matmul(banks[bk][0:64, p * 128:(p + 1) * 128], src[:, p, cs], idb[:], start=True, stop=True),
                             reads=[sn, "idb"], writes=[bn(bk)])
                S.op("act", lambda e: e.copy(tm[:, 2, :], banks[Z0][0:64, :]), reads=[bn(Z0)], writes=[TMn])
                S.op("dve", lambda e: e.tensor_copy(tm[:, 3, :], banks[Z1][0:64, :]), reads=[bn(Z1)], writes=[TMn])
                S.op("pool", lambda e: e.tensor_copy(padview(Vpad[s]), tm[:, 3, :].rearrange("p (a q d) -> p a q d", q=2, d=64)),
                     reads=[TMn], writes=["Vpad%d" % s])
                yield
                nsc = 128 if own else 64
                nr = 2 if own else 1
                for (lt, ltn, dst, dstn, viaact) in ((BT[b], BTn, sc1, SC1n, False), (KT[b], KTn, sc2, SC2n, True)):
                    for h in range(8):
                        p, q = h // 2, h % 2
                        bk = zb_of(h)
                        S.op("pe", lambda e, h=h, p=p, q=q, bk=bk, lt=lt: e.matmul(banks[bk][0:64, (h % 4) * 128:(h % 4) * 128 + 64],
                                                                                 lt[:, p, cs], ATm[b][q][:, p, cs],
                                                                                 start=True, stop=True), reads=[ltn, "ATm%d_%d" % (b, q)], writes=[bn(bk)])
                        if own:
                            S.op("pe", lambda e, h=h, p=p, q=q, bk=bk, lt=lt: e.matmul(banks[bk][0:64, (h % 4) * 128 + 64:(h % 4) * 128 + 128],
                                                                                     lt[:, p, cs], RTm[b][q][:, p, cs],
                                                                                     start=True, stop=True), reads=[ltn, "RTm%d_%d" % (b, q)], writes=[bn(bk)])
                    for hg in range(2):
                        bk = Z0 if hg == 0 else Z1
                        S.op("dve", lambda e, hg=hg, bk=bk, dst=dst: e.tensor_tensor(
                            dst[:, hg * 4:hg * 4 + 4, 0:nsc].rearrange("p h (r t) -> p h r t", t=64),
                            banks[bk][0:64, :].rearrange("p (h r t) -> p h r t", r=2, t=64)[:, :, 0:nr, :],
                            mskb[:, 0:nr, :].unsqueeze(1).broadcast_to([64, 4, nr, 64]), ALU.mult),
                            reads=[bn(bk), "mskb"], writes=[dstn])
                    yield
                for h in range(8):
                    S.op("pe", lambda e, h=h: e.matmul(banks[WB][0:64, h * 64:(h + 1) * 64], sc2[:, h, 0:64], tm[:, 3, h * 64:(h + 1) * 64],
                                                      start=True, stop=True), reads=[SC2n, TMn], writes=[bn(WB)])
                S.op("act", lambda e: e.copy(zbs[:, :, 64:128], banks[WB][0:64, :].rearrange("p (h t) -> p h t", t=64)), reads=[bn(WB)], writes=[Zbn])
                S.op("pool", lambda e: e.tensor_copy(zbs[:, :, 0:64], tm[:, 0, :].rearrange("p (h t) -> p h t", t=64)), reads=[TMn], writes=[Zbn])
                yield
                zb = lambda h: banks[zb_of(h)][0:64, (h % 4) * 128:(h % 4) * 128 + 128]
                for h in range(8):
                    S.op("pe", lambda e, h=h: e.matmul(zb(h), idb[0:64, 0:64], zbs[:, h, :], start=(h % 4 == 0), stop=False, skip_group_check=True),
                         reads=["idb", Zbn], writes=[bn(zb_of(h))])
                Pc, PTc, pn, ptn = sc3, sc1, SC3n, SC1n
                for lv in range(6):
                    for h in range(8):
                        lhs = PTc[:, h, 0:64]
                        S.op("pe", lambda e, h=h, lhs=lhs: e.matmul(zb(h), lhs, zbs[:, h, :], start=False, stop=(lv == 5), skip_group_check=True),
                             reads=[ptn, Zbn], writes=[bn(zb_of(h))])
                    S.op("act", lambda e: e.copy(zbs[:, 0:4, :], banks[Z0][0:64, :].rearrange("p (h t) -> p h t", t=128)),
                         reads=[bn(Z0)], writes=[Zbn])
                    S.op("dve", lambda e: e.tensor_copy(zbs[:, 4:8, :], banks[Z1][0:64, :].rearrange("p (h t) -> p h t", t=128)),
                         reads=[bn(Z1)], writes=[Zbn])
                    if lv < 5:
                        ppb = PP[s][lv % 2]
                        np_, npt = ppb[:, 0], ppb[:, 1]
                        npn = nptn = "PP%d_%d" % (s, lv % 2)
                        for hg in range(2):
                            for hh in range(4):
                                h = hg * 4 + hh
                                S.op("pe", lambda e, h=h, hh=hh: e.matmul(banks[WB][0:64, hh * 64:(hh + 1) * 64], PTc[:, h, 0:64], Pc[:, h, 0:64],
                                                                          start=True, stop=True), reads=[pn, ptn], writes=[bn(WB)])
                            for hh in range(4):
                                h = hg * 4 + hh
                                S.op("pe", lambda e, h=h, hh=hh: e.matmul(banks[WB][0:64, 256 + hh * 64:256 + (hh + 1) * 64], Pc[:, h, 0:64], PTc[:, h, 0:64],
                                                                          start=True, stop=True), reads=[pn, ptn], writes=[bn(WB)])
                            if hg == 0:
                                S.op("act", lambda e, hg=hg: e.copy(ppb[:, :, hg * 4:hg * 4 + 4, :], banks[WB][0:64, :].rearrange("p (a h t) -> p a h t", a=2, t=64)),
                                     reads=[bn(WB)], writes=[npn])
                            else:
                                S.op("dve", lambda e, hg=hg: e.tensor_copy(ppb[:, :, hg * 4:hg * 4 + 4, :], banks[WB][0:64, :].rearrange("p (a h t) -> p a h t", a=2, t=64)),
                                     reads=[bn(WB)], writes=[npn])
                            yield
                        Pc, PTc, pn, ptn = np_, npt, npn, nptn
                    else:
                        yield
                if own:
                    S.op("pool", lambda e: e.tensor_copy(padview(Wpad[s]), zbs[:, :, 64:128].rearrange("p (a q) d -> p a q d", q=2)),
                         reads=[Zbn], writes=["Wpad%d" % s])
                    for h in range(8):
                        p, q = h // 2, h % 2
                        rw = slice(64 * q, 64 * q + 64)
                        S.op("pe", lambda e, h=h: e.matmul(banks[WB][0:64, h * 64:(h + 1) * 64], zbs[:, h, 0:64], sc1[:, h, 64:128],
                                                          start=True, stop=False), reads=[Zbn, SC1n], writes=[bn(WB)])
                        S.op("pe", lambda e, h=h, p=p, rw=rw: e.matmul(banks[WB][0:64, h * 64:(h + 1) * 64], idb[:, rw], RT[b][:, p, cs],
                                                                      start=False, stop=True), reads=["idb", RTn], writes=[bn(WB)])
                    S.op("act", lambda e: e.copy(QTb[s][:], banks[WB][0:64, :].rearrange("p (h t) -> p h t", t=64)), reads=[bn(WB)], writes=["QTb%d" % s])
                for h in range(8):
                    S.op("pe", lambda e, h=h: e.matmul(banks[Z0][0:64, h * 64:(h + 1) * 64], zbs[:, h, 0:64], tm[:, 1, h * 64:(h + 1) * 64],
                                                      start=True, stop=True), reads=[Zbn, TMn], writes=[bn(Z0)])
                S.op("act", lambda e: e.copy(Ef[s][:], banks[Z0][0:64, :].rearrange("p (h t) -> p h t", t=64)), reads=[bn(Z0)], writes=["Ef%d" % s])
                yield
                if own:
                    for h in range(8):
                        p = h // 2
                        yo = banks[Z1][:, p * 64:(p + 1) * 64]
                        S.op("pe", lambda e, h=h, yo=yo: e.matmul(yo, Wpad[s][:, h, :], sc1[:, h, 64:128], start=(h % 2 == 0), stop=False),
                             reads=["Wpad%d" % s, SC1n], writes=[bn(Z1)])
                        S.op("pe", lambda e, h=h, yo=yo: e.matmul(yo, Vpad[s][:, h, :], sc2[:, h, 64:128], start=False, stop=False),
                             reads=["Vpad%d" % s, SC2n], writes=[bn(Z1)])
                        S.op("pe", lambda e, h=h, yo=yo: e.matmul(yo, Hpad[s][:, h, :], QTb[s][:, h, :], start=False, stop=(h % 2 == 1)),
                             reads=["Hpad%d" % s, "QTb%d" % s], writes=[bn(Z1)])
                    S.op("act", lambda e: e.copy(yv[b][:, :, cs], banks[Z1][:, 0:256].rearrange("p (a t) -> p a t", t=64)), reads=[bn(Z1)], writes=["yv%d" % b])
                    yield
                for h in range(8):
                    ho = banks[WB][0:64, h * 64:(h + 1) * 64]
                    S.op("pe", lambda e, h=h, ho=ho: e.matmul(ho, tm[:, 1, h * 64:(h + 1) * 64], zbs[:, h, 64:128], start=True, stop=False),
                         reads=[TMn, Zbn], writes=[bn(WB)])
                    S.op("pe", lambda e, h=h, ho=ho: e.matmul(ho, tm[:, 2, h * 64:(h + 1) * 64], tm[:, 3, h * 64:(h + 1) * 64], start=False, stop=False),
                         reads=[TMn], writes=[bn(WB)])
                    S.op("pe", lambda e, h=h, ho=ho: e.matmul(ho, Ef[s][:, h, :], Hb[:, h, :], start=False, stop=True),
                         reads=["Ef%d" % s, "Hb"], writes=[bn(WB)])
                S.op("dve", lambda e: e.tensor_tensor(Hf[:], banks[WB][0:64, :].rearrange("p (h t) -> p h t", t=64), Hf[:], ALU.add),
                     reads=[bn(WB), "Hf"], writes=["Hf"])
                S.op("dve", lambda e: e.tensor_tensor(Hf[:], Hf[:], gam[b][:, :, c:c + 1].broadcast_to([64, 8, 64]), ALU.mult),
                     reads=["Hf", "gam%d" % b], writes=["Hf"])
                S.op("pool", lambda e: e.tensor_copy(Hb[:], Hf[:]), reads=["Hf"], writes=["Hb"])
                if n + 1 >= 2 * (NTW - NTO):
                    S.op("pool", lambda e: e.tensor_copy(padview(Hpad[1 - s]), Hf[:].rearrange("p (a q) d -> p a q d", q=2)),
                         reads=["Hf"], writes=["Hpad%d" % (1 - s)])
                yield
                if own and c == 1:
                    yvb = yv[b]
                    S.op("pe", lambda e: e.matmul(banks[Z0][:], bones, yvb[:].rearrange("p a t -> p (a t)"), start=True, stop=True),
                         reads=["cm", "yv%d" % b], writes=[bn(Z0)])
                    S.op("dve", lambda e: e.scalar_tensor_tensor(dv[:].rearrange("p a t -> p (a t)"), banks[Z0][:], -1.0 / 64,
                                                                yvb[:].rearrange("p a t -> p (a t)"), ALU.mult, ALU.add),
                         reads=[bn(Z0), "yv%d" % b], writes=["dv"])
                    S.op("pool", lambda e: e.tensor_tensor(d2[:], dv[:], dv[:], ALU.mult), reads=["dv"], writes=["d2"])
                    S.op("pe", lambda e: e.matmul(banks[Z1][:], bones, d2[:].rearrange("p a t -> p (a t)"), start=True, stop=True),
                         reads=["cm", "d2"], writes=[bn(Z1)])
                    S.op("act", lambda e: e.activation(d2[:].rearrange("p a t -> p (a t)"), banks[Z1][:], AF.Sqrt, bias=GN_EPS, scale=1.0 / 64),
                         reads=[bn(Z1)], writes=["d2"])
                    yield
                    S.op("dve", lambda e: e.reciprocal(d2[:], d2[:]), reads=["d2"], writes=["d2"])
                    S.op("dve", lambda e: e.tensor_tensor(dv[:], dv[:], d2[:], ALU.mult), reads=["dv", "d2"], writes=["dv"])
                    S.op("pool", lambda e: e.tensor_tensor(dv[:], dv[:], cv2[:, C2_LW:C2_LW + 4].unsqueeze(2).broadcast_to([128, 4, 128]), ALU.mult),
                         reads=["dv", "cv2"], writes=["dv"])
                    S.op("pool", lambda e: e.tensor_tensor(dv[:], dv[:], cv2[:, C2_LB:C2_LB + 4].unsqueeze(2).broadcast_to([128, 4, 128]), ALU.add),
                         reads=["dv", "cv2"], writes=["dv"])
                    S.op("pool", lambda e: e.tensor_tensor(dv[:], dv[:], bon[b][:], ALU.add), reads=["dv", "bon%d" % b], writes=["dv"])
                    S.op("pool", lambda e: e.tensor_tensor(y_bT[:, :, io * 128:(io + 1) * 128], dv[:], gT[b][:], ALU.mult),
                         reads=["dv", "gT%d" % b], writes=["y_bT"])
                    yield

            NTL_ = min(NTW, NTL)
            prep_done = -1
            next_prep = 0
            next_chunk = 0
            prep_g = None
            active = []
            steps = {}
            n_fin = 0
            while True:
                if prep_g is None and next_prep < NTL_ and n_fin >= 2 * (next_prep - 1):
                    prep_g = prep_gen(next_prep)
                if prep_g is not None:
                    try:
                        next(prep_g)
                    except StopIteration:
                        prep_done = next_prep
                        next_prep += 1
                        prep_g = None
                for ent in list(active):
                    try:
                        next(ent[1])
                        steps[ent[1]] += 1
                    except StopIteration:
                        active.remove(ent)
                        n_fin += 1
                if (len(active) < 2 and next_chunk < 2 * NTL_ and next_chunk // 2 <= prep_done
                        and all(sl != next_chunk % 2 for sl, _ in active)
                        and (not active or steps[active[-1][1]] >= LAG)):
                    g = chunk_gen(next_chunk)
                    steps[g] = 0
                    active.append((next_chunk % 2, g))
                    next_chunk += 1
                if prep_g is None and not active and next_prep >= NTL_ and next_chunk >= 2 * NTL_:
                    break
        emit_conv(len(conv_tasks))
        S.barrier()
        y_aT = sb(st, "y_aT", [128, 4, OWN], BF16)
        grow = sb(st, "grow", [128, 2, D], F32)
        S.dma("sp", lambda e: e.dma_start(out=grow[:, 0, :], in_=rows[0:1, :].partition_broadcast(128)), writes=["grow"])
        S.dma("sp", lambda e: e.dma_start(out=grow[:, 1, :], in_=rows[1:2, :].partition_broadcast(128)), writes=["grow"])
        mo = sb(st, "mo", [128, D], F32)
        pm = ExitStack()
        hT_att = sb(pm, "hT_att", [128, 8, NTA * 128], BF16)
        S.dma("sp", lambda e: e.dma_start(out=hT_att[:], in_=hT_s), reads=["hT_s"], writes=["hT_att"])
        with ExitStack() as p2:
          if STAGE >= 2:
            wa = sb(p2, "wa", [128, 8, 1536], BF16)
            S.dma("sp", lambda e: e.dma_start(out=wa[:], in_=w_in_b.rearrange("(k p) n -> p k n", p=128)[:, :, 0:1536]),
                  reads=["w_in_b"], writes=["wa"])
            bT = sb(p2, "bT", [128, 5, 8, 128], F32)
            S.dma("sp", lambda e: e.dma_start(out=bT[:].rearrange("p a h t -> p (a h t)"), in_=biasd), writes=["bT"])
            vld = sb(p2, "vld", [128, NTA], F32)
            S.dma("sp", lambda e: e.dma_start(out=vld[:], in_=validd), writes=["vld"])
            qT = sb(p2, "qT", [128, 4, OWN], BF16)
            kTa = sb(p2, "kTa", [128, 4, NTA * 128], BF16)
            Va = sb(p2, "Va", [128, NTA, 8, 65], BF16)
            scf = [sb(p2, "scf%d" % i, [128, 512], F32) for i in range(2)]
            pTb = [sb(p2, "pTb%d" % i, [128, 512], BF16) for i in range(2)]
            rec = sb(p2, "rec", [128, 8], F32)
            ya = sb(p2, "ya", [128, 8, 64], BF16)
            for ia in range(NTA):
                ts_ = slice(ia * 128, (ia + 1) * 128)
                io = ia - 4
                pbk = ia % 2
                for c in range(4):
                    for k in range(8):
                        S.op("pe", lambda e, c=c, k=k, pbk=pbk, ts_=ts_: e.matmul(banks[pbk][:, c * 128:(c + 1) * 128], wa[:, k, 512 + c * 128:512 + (c + 1) * 128],
                                                                                 hT_att[:, k, ts_], start=(k == 0), stop=(k == 7)),
                             reads=["wa", "hT_att"], writes=["bank%d" % pbk])
                S.op("act", lambda e, pbk=pbk, ts_=ts_: e.copy(kTa[:, :, ts_], banks[pbk][:].rearrange("p (c t) -> p c t", t=128)),
                     reads=["bank%d" % pbk], writes=["kTa"])
                pbk2 = 2 + ia % 2
                for k in range(8):
                    S.op("pe", lambda e, k=k, pbk2=pbk2, ts_=ts_: e.matmul(banks[pbk2][:], hT_att[:, k, ts_], wa[:, k, 1024:1536],
                                                                          start=(k == 0), stop=(k == 7)), reads=["wa", "hT_att"], writes=["bank%d" % pbk2])
                S.op("dve", lambda e, pbk2=pbk2, ia=ia: e.tensor_copy(Va[:, ia, :, 0:64], banks[pbk2][:].rearrange("p (h d) -> p h d", d=64)),
                     reads=["bank%d" % pbk2], writes=["Va"])
                S.op("pool", lambda e, ia=ia: e.tensor_copy(Va[:, ia, :, 64:65], vld[:, ia:ia + 1].unsqueeze(1).broadcast_to([128, 8, 1])),
                     reads=["vld"], writes=["Va"])
                if io >= 0:
                    pbk3 = 4 + ia % 2
                    for c in range(4):
                        for k in range(8):
                            S.op("pe", lambda e, c=c, k=k, pbk3=pbk3, ts_=ts_: e.matmul(banks[pbk3][:, c * 128:(c + 1) * 128], wa[:, k, c * 128:(c + 1) * 128],
                                                                                       hT_att[:, k, ts_], start=(k == 0), stop=(k == 7)),
                                 reads=["wa", "hT_att"], writes=["bank%d" % pbk3])
                    S.op("act", lambda e, pbk3=pbk3, io=io: e.copy(qT[:, :, io * 128:(io + 1) * 128], banks[pbk3][:].rearrange("p (c t) -> p c t", t=128)),
                         reads=["bank%d" % pbk3], writes=["qT"])
            for io in range(NTO):
                ia = io + 4
                qs = slice(io * 128, (io + 1) * 128)
                step = 0
                for dl in range(5):
                    kt = ia - dl
                    ks = slice(kt * 128, (kt + 1) * 128)
                    for hg in range(2):
                        pb_ = (step % 2) * 2 + hg
                        sl = step % 2
                        for hh in (0, 2, "sep", 1, 3):
                            if hh == "sep":
                                S.op("pe", lambda e: e.matmul(banks[7][0:64, 0:64], idb[:, 0:64], idb[:, 0:64], start=True, stop=True),
                                     reads=["idb"], writes=["bank7"])
                                continue
                            h = hg * 4 + hh
                            p, q = h // 2, h % 2
                            rw = slice(64 * q, 64 * q + 64)
                            S.op("pe", lambda e, pb_=pb_, hh=hh, rw=rw, p=p, ks=ks: e.matmul(banks[pb_][:, hh * 128:(hh + 1) * 128], kTa[rw, p, ks], qT[rw, p, qs],
                                                                                          start=True, stop=True), reads=["kTa", "qT"], writes=["bank%d" % pb_])
                        S.op("dve", lambda e, pb_=pb_, hg=hg, dl=dl, sl=sl: e.scalar_tensor_tensor(
                            scf[hg][:], banks[pb_][:], 0.125, bT[:, dl, hg * 4:hg * 4 + 4, :].rearrange("p h t -> p (h t)"), ALU.mult, ALU.add),
                            reads=["bank%d" % pb_, "bT"], writes=["scf%d" % hg])
                        S.op("act", lambda e, hg=hg: e.activation(pTb[hg][:], scf[hg][:], AF.Exp), reads=["scf%d" % hg], writes=["pTb%d" % hg])
                        for hh in range(4):
                            h = hg * 4 + hh
                            S.op("pe", lambda e, hg=hg, hh=hh, h=h, kt=kt, dl=dl: e.matmul(banks[4 + hg][:, hh * 65:(hh + 1) * 65], pTb[hg][:, hh * 128:(hh + 1) * 128],
                                                                                       Va[:, kt, h, :], start=(dl == 0 and hh == 0), stop=(dl == 4), skip_group_check=True),
                                 reads=["pTb%d" % hg, "Va"], writes=["bank%d" % (4 + hg)])
                    step += 1
                for hg in range(2):
                    ov = banks[4 + hg][:, 0:260].rearrange("p (h d) -> p h d", d=65)
                    S.op("dve", lambda e, hg=hg, ov=ov: e.reciprocal(rec[:, hg * 4:hg * 4 + 4], ov[:, :, 64]), reads=["bank%d" % (4 + hg)], writes=["rec"])
                    S.op("dve", lambda e, hg=hg, ov=ov: e.tensor_tensor(ya[:, hg * 4:hg * 4 + 4, :], ov[:, :, 0:64],
                                                                       rec[:, hg * 4:hg * 4 + 4].unsqueeze(2).broadcast_to([128, 4, 64]), ALU.mult),
                         reads=["bank%d" % (4 + hg), "rec"], writes=["ya"])
                pb6 = bank_bf(6)
                for p in range(4):
                    S.op("pe", lambda e, p=p: e.transpose(pb6[:, p * 128:(p + 1) * 128], ya[:, 2 * p:2 * p + 2, :].rearrange("p h d -> p (h d)"), idb[:]),
                         reads=["ya", "idb"], writes=["bank6"])
                S.op("act", lambda e, qs=qs: e.copy(y_aT[:, :, qs], pb6[:, 0:512].rearrange("p (c t) -> p c t", t=128)), reads=["bank6"], writes=["y_aT"])
        S.barrier()
        with ExitStack() as p3:
          if STAGE >= 3:
            wg = sb(p3, "wg", [128, 8, 2048], BF16)
            pa = sb(p3, "pa", [128, 4, D], BF16)
            pb = sb(p3, "pb", [128, 4, D], BF16)
            wo = sb(p3, "wo", [128, 8, D], BF16)
            S.dma("sp", lambda e: e.dma_start(out=wg[:], in_=w_in_b.rearrange("(k p) n -> p k n", p=128)[:, :, 3328:5376]), reads=["w_in_b"], writes=["wg"])
            S.dma("sp", lambda e: e.dma_start(out=pa[:], in_=pa_b.rearrange("(k p) n -> p k n", p=128)), reads=["pa_b"], writes=["pa"])
            S.dma("sp", lambda e: e.dma_start(out=pb[:], in_=pb_b.rearrange("(k p) n -> p k n", p=128)), reads=["pb_b"], writes=["pb"])
            S.dma("sp", lambda e: e.dma_start(out=wo[:], in_=wo_b.rearrange("(k p) n -> p k n", p=128)), reads=["wo_b"], writes=["wo"])
            gat = sb(p3, "gat", [128, 16, 512], BF16)
            mT = sb(p3, "mT", [128, 8, 512], BF16)
            ta = sb(p3, "ta", [128, 512], F32)
            tb = sb(p3, "tb", [128, 512], F32)
            for g in range(NG):
                gs = slice(g * 512, (g + 1) * 512)
                hs = slice(512 + g * 512, 512 + (g + 1) * 512)
                for ct in range(16):
                    pbk = ct % 2
                    for k in range(8):
                        S.op("pe", lambda e, ct=ct, k=k, pbk=pbk: e.matmul(banks[pbk][:], wg[:, k, ct * 128:(ct + 1) * 128], hT_att[:, k, hs],
                                                                          start=(k == 0), stop=(k == 7)), reads=["wg", "hT_att"], writes=["bank%d" % pbk])
                    S.op("act", lambda e, ct=ct, pbk=pbk: e.activation(gat[:, ct, :], banks[pbk][:], AF.Sigmoid, bias=cv[:, C_GB + ct:C_GB + ct + 1]),
                         reads=["bank%d" % pbk, "cv"], writes=["gat"])
                for dt_ in range(8):
                    ba, bb = 2 + 2 * (dt_ % 2), 3 + 2 * (dt_ % 2)
                    for k in range(4):
                        S.op("pe", lambda e, dt_=dt_, k=k, ba=ba: e.matmul(banks[ba][:], pa[:, k, dt_ * 128:(dt_ + 1) * 128], y_aT[:, k, gs],
                                                                          start=(k == 0), stop=(k == 3)), reads=["pa", "y_aT"], writes=["bank%d" % ba])
                    for k in range(4):
                        S.op("pe", lambda e, dt_=dt_, k=k, bb=bb: e.matmul(banks[bb][:], pb[:, k, dt_ * 128:(dt_ + 1) * 128], y_bT[:, k, gs],
                                                                          start=(k == 0), stop=(k == 3)), reads=["pb", "y_bT"], writes=["bank%d" % bb])
                    S.op("dve", lambda e, dt_=dt_, ba=ba: e.tensor_tensor(ta[:], banks[ba][:], gat[:, dt_, :], ALU.mult),
                         reads=["bank%d" % ba, "gat"], writes=["ta"])
                    S.op("dve", lambda e, dt_=dt_, bb=bb: e.tensor_tensor(tb[:], banks[bb][:], gat[:, 8 + dt_, :], ALU.mult),
                         reads=["bank%d" % bb, "gat"], writes=["tb"])
                    S.op("pool", lambda e, dt_=dt_: e.tensor_tensor(mT[:, dt_, :], ta[:], tb[:], ALU.add), reads=["ta", "tb"], writes=["mT"])
                for tt in range(4):
                    it = g * 4 + tt
                    b = it % 2
                    S.dma("sp", lambda e, b=b, it=it: e.dma_start(out=xt[b][:], in_=xw[WIN - OWN + it * 128:WIN - OWN + (it + 1) * 128, :]),
                          writes=["xt%d" % b])
                    for half in range(2):
                        pbk = 6 + half
                        for k in range(8):
                            S.op("pe", lambda e, k=k, half=half, pbk=pbk, tt=tt: e.matmul(banks[pbk][:], mT[:, k, tt * 128:(tt + 1) * 128],
                                                                                         wo[:, k, half * 512:(half + 1) * 512], start=(k == 0), stop=(k == 7)),
                                 reads=["mT", "wo"], writes=["bank%d" % pbk])
                        S.op("act", lambda e, half=half, pbk=pbk: e.copy(mo[:, half * 512:(half + 1) * 512], banks[pbk][:]), reads=["bank%d" % pbk], writes=["mo"])
                    post_norm_res(S, nc, mo, "mo", xt[b], "xt%d" % b, grow[:, 0, :], junk, ssq, xt[b], "xt%d" % b)
                    S.dma("sp", lambda e, b=b, it=it: e.dma_start(out=x1s[it * 128:(it + 1) * 128, :], in_=xt[b][:]), reads=["xt%d" % b], writes=["x1s"])
        pm.close()
        S.barrier()
        with ExitStack() as p4:
          if STAGE >= 4:
            wus = [sb(p4, "wus%d" % i, [128, 8, 1024], BF16) for i in range(2)]
            wds = [sb(p4, "wds%d" % i, [128, 8, 1024], BF16) for i in range(2)]
            x1g = sb(p4, "x1g", [128, 4, D], F32)
            hfT = sb(p4, "hfT", [128, 8, 512], BF16)
            acc = sb(p4, "acc", [128, 4, D], F32)
            act_ = sb(p4, "act_", [128, 8, 512], BF16)
            rl = [sb(p4, "rl%d" % i, [128, 512], F32) for i in range(2)]
            sl_i = 0
            for g in range(NG):
                for tt in range(4):
                    it = g * 4 + tt
                    S.dma("sp", lambda e, tt=tt, it=it: e.dma_start(out=x1g[:, tt, :], in_=x1s[it * 128:(it + 1) * 128, :]), reads=["x1s"], writes=["x1g"])
                    norm_transpose(x1g[:, tt, :], "x1g", C_G3, hfT[:, :, tt * 128:(tt + 1) * 128], "hfT", 0)
                for s in range(4):
                    slot = sl_i % 2
                    sl_i += 1
                    S.dma("sp", lambda e, s=s, slot=slot: e.dma_start(out=wus[slot][:], in_=wu_b.rearrange("(k p) n -> p k n", p=128)[:, :, s * 1024:(s + 1) * 1024]),
                          reads=["wu_b"], writes=["wus%d" % slot])
                    S.dma("sp", lambda e, s=s, slot=slot: e.dma_start(out=wds[slot][:], in_=wd_b[s * 1024:(s + 1) * 1024, :].rearrange("(f p) n -> p f n", p=128)),
                          reads=["wd_b"], writes=["wds%d" % slot])
                    for f in range(8):
                        pbk = 1 + f % 2
                        for k in range(8):
                            S.op("pe", lambda e, f=f, k=k, pbk=pbk, slot=slot: e.matmul(banks[pbk][:], wus[slot][:, k, f * 128:(f + 1) * 128], hfT[:, k, :],
                                                                                       start=(k == 0), stop=(k == 7)), reads=["wus%d" % slot, "hfT"], writes=["bank%d" % pbk])
                        S.op("act", lambda e, f=f, pbk=pbk: e.activation(rl[f % 2][:], banks[pbk][:], AF.Relu), reads=["bank%d" % pbk], writes=["rl%d" % (f % 2)])
                        S.op("pool", lambda e, f=f: e.tensor_tensor(act_[:, f, :], rl[f % 2][:], rl[f % 2][:], ALU.mult), reads=["rl%d" % (f % 2)], writes=["act_"])
                    for tt in range(4):
                        for half in range(2):
                            pbk = 3 + (tt % 2) * 2 + half
                            for f in range(8):
                                S.op("pe", lambda e, f=f, tt=tt, half=half, pbk=pbk, slot=slot: e.matmul(
                                    banks[pbk][:], act_[:, f, tt * 128:(tt + 1) * 128], wds[slot][:, f, half * 512:(half + 1) * 512],
                                    start=(f == 0), stop=(f == 7)), reads=["act_", "wds%d" % slot], writes=["bank%d" % pbk])
                            if s == 0:
                                S.op("dve", lambda e, tt=tt, half=half, pbk=pbk: e.tensor_copy(acc[:, tt, half * 512:(half + 1) * 512], banks[pbk][:]),
                                     reads=["bank%d" % pbk], writes=["acc"])
                            else:
                                S.op("dve", lambda e, tt=tt, half=half, pbk=pbk: e.tensor_tensor(acc[:, tt, half * 512:(half + 1) * 512], banks[pbk][:],
                                                                                                acc[:, tt, half * 512:(half + 1) * 512], ALU.add),
                                     reads=["bank%d" % pbk, "acc"], writes=["acc"])
                for tt in range(4):
                    it = g * 4 + tt
                    post_norm_res(S, nc, acc[:, tt, :], "acc", x1g[:, tt, :], "x1g", grow[:, 1, :], junk, ssq, mo, "mo")
                    S.dma("sp", lambda e, it=it: e.dma_start(out=out[it * 128:(it + 1) * 128, :], in_=mo[:]), reads=["mo"], writes=["out"])
        if STAGE < 4:
            S.barrier()
            S.dma("sp", lambda e: e.dma_start(out=out[0:128, :], in_=xw[0:128, :]), reads=["w_in_b", "wd_b", "y_bT", "y_aT", "x1s"], writes=["out"])
        S.wait_all("sp", ["out"])
        S.emit()
    return nc


def post_norm_res(S, nc, u, uname, xres, xname, grow, junk, ssq, dst, dname):
    ua = u if isinstance(u, bass.AP) else u[:]
    xa = xres if isinstance(xres, bass.AP) else xres[:]
    da = dst if isinstance(dst, bass.AP) else dst[:]
    S.op("act", lambda e: e.activation(junk[:], ua, AF.Square, accum_out=ssq[:, 0:1]), reads=[uname], writes=["junk", "ssq"])
    S.op("act", lambda e: e.activation(ssq[:, 1:2], ssq[:, 0:1], AF.Sqrt, bias=RMS_EPS, scale=1.0 / D), reads=["ssq"], writes=["ssq"])
    S.op("dve", lambda e: e.reciprocal(ssq[:, 2:3], ssq[:, 1:2]), reads=["ssq"], writes=["ssq"])
    S.op("dve", lambda e: e.scalar_tensor_tensor(ua, ua, ssq[:, 2:3], grow, ALU.mult, ALU.mult), reads=[uname, "ssq", "grow"], writes=[uname])
    S.op("pool", lambda e: e.tensor_tensor(da, ua, xa, ALU.add), reads=[uname, xname], writes=[dname])


def _host_consts(inp):
    f = np.float32
    g = lambda n: np.asarray(inp[n], dtype=f)[0]
    cvec = np.zeros((128, 64), f)
    cvec[:, 0:8] = g("pre_mix_g").reshape(8, 128).T
    cvec[:, 8:16] = g("pre_ffn_g").reshape(8, 128).T
    cvec[:, 16:32] = g("gate_bias").reshape(16, 128).T
    mu = g("shift_mu")
    colmap = [512 + c * 128 for c in range(4)] + [1024 + c * 128 for c in range(4)] + [1536] + [c * 128 for c in range(4)] + [1664]
    for t, co in enumerate(colmap):
        cvec[:, 32 + t] = mu[co:co + 128]
    cvec[:, 46:50] = g("w0").reshape(4, 128).T
    cvec[:, 50:54] = g("a0").reshape(4, 128).T
    cvec[:, 54:58] = g("k_k").reshape(4, 128).T
    cvec[:, 58:62] = g("k_a").reshape(4, 128).T
    cvec2 = np.zeros((128, 16), f)
    cvec2[:, 0:4] = g("r_k").reshape(512).reshape(4, 128).T
    cvec2[:, 4:8] = g("ln_x_w").reshape(4, 128).T
    cvec2[:, 8:12] = g("ln_x_b").reshape(4, 128).T
    rows = np.stack([g("post_mix_g"), g("post_ffn_g")], 0)
    rb = np.concatenate([g("rel_bias"), np.full((8, 1), -1e30, f)], axis=1)
    kj = np.arange(128)[:, None]
    qi = np.arange(128)[None, :]
    bt = np.zeros((128, 5, 8, 128), f)
    for dl in range(5):
        dist = 128 * dl + qi - kj
        idx = np.clip(dist, -63, 256) + 63
        cd = 2 * dl + qi // 64 - kj // 64
        idx = np.where((cd >= 0) & (cd <= 8), idx, 320)
        bt[:, dl, :, :] = rb[:, idx].transpose(1, 0, 2)
    cmat = np.zeros((128, 768), f)
    cmat[:, 0:128] = np.eye(128, dtype=f)
    cmat[0:64, 128:192] = 1.0
    cmat[64:128, 192:256] = 1.0
    s_ = np.arange(64)[:, None]
    t_ = np.arange(64)[None, :]
    cmat[0:64, 256:320] = (s_ < t_)
    cmat[0:64, 384:448] = (s_ <= t_)
    cmat[0:64, 512:576] = (t_ < s_)
    return dict(cvec=cvec, cvec2=cvec2, rows=rows, biasT=bt.reshape(128, 5 * 8 * 128), cmat=cmat,
                w2=g("w2"), a2=g("a2"), g2=g("g2"), w_in=g("w_in"), proj_a=g("proj_a"), proj_b=g("proj_b"),
                w_out=g("w_out"), w_up=g("w_up"), w_down=g("w_down"))


_NC_CACHE = {}


def run(inputs, trace=False):
    x = np.asarray(inputs["x"], dtype=np.float32)
    B, SEQ, _ = x.shape
    OWN = SEQ // 4
    WIN = SEQ
    NTA = OWN // 128 + 4
    consts = _host_consts(inputs)
    in_maps = []
    for c in range(8):
        b, j = c // 4, c % 4
        end = (j + 1) * OWN
        xw = np.zeros((WIN, D), np.float32)
        xw[WIN - end:, :] = x[b, 0:end, :]
        pos = end - NTA * 128 + np.arange(NTA * 128)
        valid = (pos >= 0).astype(np.float32).reshape(NTA, 128).T.copy()
        m = dict(consts)
        m["xw"] = xw
        m["valid"] = valid
        in_maps.append(m)
    key = (WIN, OWN)
    if key not in _NC_CACHE:
        _NC_CACHE[key] = build_nc(WIN, OWN)
    nc = _NC_CACHE[key]
    res = run_bass_kernel_spmd(nc, in_maps, core_ids=list(range(8)))
    outp = np.zeros((B, SEQ, D), np.float32)
    for c in range(8):
        b, j = c // 4, c % 4
        outp[b, j * OWN:(j + 1) * OWN, :] = res.results[c]["out"]
    return outp


def kernel(**inputs):
    return run(inputs)
```

```python
import numpy as np
import ml_dtypes
import concourse.bass as bass
import concourse.mybir as mybir
from concourse.bass_utils import run_bass_kernel_spmd
from contextlib import ExitStack

F32 = mybir.dt.float32
BF16 = mybir.dt.bfloat16
AF = mybir.ActivationFunctionType
ALU = mybir.AluOpType

D = 1024
N_DMA_SEMS = 24
RMS_EPS = 1e-6
GN_EPS = 64 * 1e-5
DEC_C = float(np.exp(-0.5))


class _Rec:
    def __getattr__(self, name):
        def f(*a, **k):
            self.call = (name, a, k)
        return f


class Sched:
    ENGS = ("pe", "act", "dve", "pool", "sp")

    def __init__(self, nc, stack):
        self.nc = nc
        self.prog = {e: [] for e in self.ENGS}
        self.cnt = {e: 0 for e in self.ENGS}
        self.sems = {}
        for e in self.ENGS:
            self.sems["p_" + e] = stack.enter_context(nc.semaphore("prog_" + e))
        for i in range(N_DMA_SEMS):
            self.sems["d%d" % i] = stack.enter_context(nc.semaphore("dma%d" % i))
        self.dcnt = [0] * N_DMA_SEMS
        self.dnx = {"sp": 0, "pool": 0, "act": 0}
        self.seen = {e: {} for e in self.ENGS}
        self.lastw = {}
        self.reads = {}

    def _deps(self, eng, reads, writes):
        deps = {}
        for b in reads:
            for s, v in self.lastw.get(b, {}).items():
                if deps.get(s, 0) < v:
                    deps[s] = v
        for b in writes:
            for s, v in self.lastw.get(b, {}).items():
                if deps.get(s, 0) < v:
                    deps[s] = v
            for s, v in self.reads.get(b, {}).items():
                if deps.get(s, 0) < v:
                    deps[s] = v
        waits = []
        for s, v in deps.items():
            if s == "p_pe" and eng == "pe":
                continue
            if self.seen[eng].get(s, 0) < v:
                self.seen[eng][s] = v
                waits.append((s, v))
        return waits

    def _commit(self, tok, reads, writes):
        s, v = tok
        for b in writes:
            self.lastw.setdefault(b, {})[s] = v
        for b in reads:
            self.reads.setdefault(b, {})[s] = v

    @staticmethod
    def _rec(fn):
        r = _Rec()
        fn(r)
        return r.call

    def op(self, eng, fn, reads=(), writes=()):
        fn = self._rec(fn)
        waits = self._deps(eng, reads, writes)
        self.cnt[eng] += 1
        tok = ("p_" + eng, self.cnt[eng])
        self._commit(tok, reads, writes)
        self.prog[eng].append((waits, fn, ("p_" + eng, 1)))

    def dma(self, eng, fn, reads=(), writes=()):
        fn = self._rec(fn)
        half = N_DMA_SEMS // 2
        base = 0 if eng == "sp" else half
        i = base + self.dnx[eng]
        self.dnx[eng] = (self.dnx[eng] + 1) % half
        waits = self._deps(eng, reads, writes)
        s = "d%d" % i
        if self.dcnt[i] > 0 and self.seen[eng].get(s, 0) < self.dcnt[i]:
            self.seen[eng][s] = self.dcnt[i]
            waits.append((s, self.dcnt[i]))
        self.dcnt[i] += 16
        self._commit((s, self.dcnt[i]), reads, writes)
        self.prog[eng].append((waits, fn, (s, 16)))

    def barrier(self):
        cur = {"p_" + e: self.cnt[e] for e in self.ENGS}
        for i in range(N_DMA_SEMS):
            cur["d%d" % i] = self.dcnt[i]
        for eng in self.ENGS:
            waits = []
            for s_, v in cur.items():
                if s_ == "p_" + eng and eng == "pe":
                    continue
                if v > 0 and self.seen[eng].get(s_, 0) < v:
                    self.seen[eng][s_] = v
                    waits.append((s_, v))
            self.prog[eng].append((waits, None, None))

    def wait_all(self, eng, bufs):
        waits = self._deps(eng, bufs, ())
        self.prog[eng].append((waits, None, None))

    def emit(self):
        nc = self.nc
        with nc.Block() as block:
            def replay(name, e):
                for waits, fn, inc in self.prog[name]:
                    for s, v in waits:
                        e.wait_ge(self.sems[s], v)
                    if fn is None:
                        continue
                    getattr(e, fn[0])(*fn[1], **fn[2]).then_inc(self.sems[inc[0]], inc[1])

            @block.tensor
            def _(e):
                replay("pe", e)

            @block.scalar
            def _(e):
                replay("act", e)

            @block.vector
            def _(e):
                replay("dve", e)

            @block.gpsimd
            def _(e):
                replay("pool", e)

            @block.sync
            def _(e):
                replay("sp", e)


import os
STAGE = int(os.environ.get("STAGE", "9"))
SUB = int(os.environ.get("SUB", "99"))
NTL = int(os.environ.get("NTL", "9999"))
SUB2 = int(os.environ.get("SUB2", "99"))
LAG = int(os.environ.get("LAG", "12"))


def build_nc(WIN, OWN, dbg=False):
    NTW = WIN // 128
    NTO = OWN // 128
    NTA = NTO + 4
    NG = OWN // 512
    nc = bass.Bass("TRN2", target_bir_lowering=False)

    def din(name, shape, dt=F32):
        return nc.dram_tensor(name, list(shape), dt, kind="ExternalInput").ap()

    xw = din("xw", [WIN, D])
    w_in = din("w_in", [D, 5376])
    proj_a = din("proj_a", [512, D])
    proj_b = din("proj_b", [512, D])
    w_out = din("w_out", [D, D])
    w_up = din("w_up", [D, 4096])
    w_down = din("w_down", [4096, D])
    cvec = din("cvec", [128, 64])
    rows = din("rows", [2, D])
    w2d = din("w2", [64, 512])
    a2d = din("a2", [64, 512])
    g2d = din("g2", [128, 512])
    biasd = din("biasT", [128, 5 * 8 * 128])
    validd = din("valid", [128, NTA])
    cmat = din("cmat", [128, 6 * 128])
    out = nc.dram_tensor("out", [OWN, D], F32, kind="ExternalOutput").ap()
    dbg_o = {}

    w_in_b = nc.dram_tensor("w_in_b", [D, 5376], BF16).ap()
    pa_b = nc.dram_tensor("pa_b", [512, D], BF16).ap()
    pb_b = nc.dram_tensor("pb_b", [512, D], BF16).ap()
    wo_b = nc.dram_tensor("wo_b", [D, D], BF16).ap()
    wu_b = nc.dram_tensor("wu_b", [D, 4096], BF16).ap()
    wd_b = nc.dram_tensor("wd_b", [4096, D], BF16).ap()
    x1s = nc.dram_tensor("x1s", [OWN, D], F32).ap()
    hT_s = nc.dram_tensor("hT_s", [128, 8, NTA * 128], BF16).ap()

    C_G1, C_G3, C_GB, C_MU = 0, 8, 16, 32
    C_W0, C_A0, C_KK, C_KA = 46, 50, 54, 58
    cvec2 = din("cvec2", [128, 16])
    C2_RK, C2_LW, C2_LB, C2_OMKA = 0, 4, 8, 12

    with ExitStack() as st:
        S = Sched(nc, st)

        def sb(stack, name, shape, dt):
            return stack.enter_context(nc.sbuf_tensor(name, list(shape), dt))

        def psb(stack, name, shape, dt=F32):
            return stack.enter_context(nc.psum_tensor(name, list(shape), dt))

        cv = sb(st, "cv", [128, 64], F32)
        cv2 = sb(st, "cv2", [128, 16], F32)
        cm = sb(st, "cm", [128, 768], F32)
        idb = sb(st, "idb", [128, 128], BF16)
        mskb = sb(st, "mskb", [128, 3, 128], BF16)
        S.dma("sp", lambda e: e.dma_start(out=cv[:], in_=cvec), writes=["cv"])
        S.dma("sp", lambda e: e.dma_start(out=cv2[:], in_=cvec2), writes=["cv2"])
        S.dma("sp", lambda e: e.dma_start(out=cm[:], in_=cmat), writes=["cm"])
        ident = cm[:, 0:128]
        bones = cm[:, 128:256]
        scanm = cm[:, 640:768]
        S.op("dve", lambda e: e.tensor_copy(idb[:], ident), reads=["cm"], writes=["idb"])
        S.op("dve", lambda e: e.tensor_copy(mskb[:], cm[:, 256:640].rearrange("p (a b) -> p a b", b=128)),
             reads=["cm"], writes=["mskb"])

        def conv(dst, src, nrows, nm):
            for r in range(0, nrows, 128):
                S.dma("pool", lambda e, r=r: e.dma_start(out=dst[r:r + 128, :], in_=src[r:r + 128, :]), writes=[nm])
        conv(w_in_b, w_in, D, "w_in_b")
        conv_tasks = []
        for dst_, src_, nrows_, nm_ in ((pa_b, proj_a, 512, "pa_b"), (pb_b, proj_b, 512, "pb_b"), (wo_b, w_out, D, "wo_b"),
                                       (wu_b, w_up, D, "wu_b"), (wd_b, w_down, 4096, "wd_b")):
            for r_ in range(0, nrows_, 128):
                conv_tasks.append((dst_, src_, r_, nm_))

        def emit_conv(k):
            for _ in range(k):
                if conv_tasks:
                    dst_, src_, r_, nm_ = conv_tasks.pop(0)
                    S.dma("pool", lambda e: e.dma_start(out=dst_[r_:r_ + 128, :], in_=src_[r_:r_ + 128, :]), writes=[nm_])

        y_bT = sb(st, "y_bT", [128, 4, OWN], BF16)
        xt = [sb(st, "xt%d" % i, [128, D], F32) for i in range(2)]
        xs = sb(st, "xs", [128, D], BF16)
        junk = xs
        ssq = sb(st, "ssq", [128, 4], F32)

        banks = [psb(st, "bank%d" % i, [128, 512]) for i in range(8)]

        def bank_bf(i):
            return banks[i][:].bitcast(BF16)

        def norm_transpose(src_tile, srcname, gcol, dst_ap, dstname, pbank):
            S.op("act", lambda e: e.activation(junk[:], src_tile, AF.Square, accum_out=ssq[:, 0:1]),
                 reads=[srcname], writes=["xs", "ssq"])
            S.op("act", lambda e: e.activation(ssq[:, 1:2], ssq[:, 0:1], AF.Sqrt, bias=RMS_EPS, scale=1.0 / D),
                 reads=["ssq"], writes=["ssq"])
            S.op("dve", lambda e: e.reciprocal(ssq[:, 2:3], ssq[:, 1:2]), reads=["ssq"], writes=["ssq"])
            S.op("dve", lambda e: e.tensor_scalar(xs[:], src_tile, ssq[:, 2:3], 0.0, ALU.mult, ALU.add),
                 reads=[srcname, "ssq"], writes=["xs"])
            pb = bank_bf(pbank)
            for k in range(8):
                S.op("pe", lambda e, k=k: e.transpose(pb[:, k * 128:(k + 1) * 128], xs[:, k * 128:(k + 1) * 128], idb[:]),
                     reads=["xs", "idb"], writes=["bank%d" % pbank])
            S.op("dve", lambda e: e.tensor_tensor(dst_ap, pb[:, 0:1024].rearrange("p (k t) -> p k t", t=128),
                                                 cv[:, gcol:gcol + 8].unsqueeze(2).broadcast_to([128, 8, 128]), ALU.mult),
                 reads=["bank%d" % pbank, "cv"], writes=[dstname])

        with ExitStack() as p1:
          if STAGE >= 1:
            wr = sb(p1, "wr", [128, 8, 1792], BF16)
            S.dma("sp", lambda e: e.dma_start(out=wr[:], in_=w_in_b.rearrange("(k p) n -> p k n", p=128)[:, :, 1536:3328]),
                  reads=["w_in_b"], writes=["wr"])
            w2s = sb(p1, "w2s", [64, 512], F32)
            a2s = sb(p1, "a2s", [128, 512], F32)
            g2s = sb(p1, "g2s", [128, 512], BF16)
            S.dma("sp", lambda e: e.dma_start(out=w2s[:], in_=w2d), writes=["w2s"])
            S.dma("sp", lambda e: e.dma_start(out=a2s[64:128, :], in_=a2d), writes=["a2s"])
            S.dma("pool", lambda e: e.dma_start(out=g2s[:], in_=g2d), writes=["g2s"])
            hTr1 = sb(p1, "hTr0", [128, 8, 128], BF16)
            hTr = [hTr1, hTr1]
            PR = sb(p1, "PR", [128, 14, 129], F32)
            prevcol = sb(p1, "prevcol", [128, 14, 1], F32)
            S.op("pool", lambda e: e.memset(PR[:], 0.0), writes=["PR"])
            S.op("pool", lambda e: e.memset(prevcol[:], 0.0), writes=["prevcol"])
            PS = sb(p1, "PS", [128, 14, 128], F32)

            def f32t(name):
                return sb(p1, name, [128, 4, 128], F32)
            tcw = sb(p1, "tcw", [64, 128], F32)
            cmask4 = sb(p1, "cmask4", [128, 512], F32)
            S.op("pool", lambda e: e.memset(cmask4[:], 1.0), writes=["cmask4"])
            S.op("pool", lambda e: e.memset(cmask4[:].rearrange("p (c t) -> p c t", t=64)[:, :, 0:1], 0.0), writes=["cmask4"])
            S.op("dve", lambda e: e.tensor_scalar(cv2[:, 12:16], cv[:, C_KA:C_KA + 4], -1.0, 1.0, ALU.mult, ALU.add), reads=["cv"], writes=["cv2"])
            sgc = sb(p1, "sgc", [128, 128], BF16)
            lw, ar, kk, k2, rn, kp, bq, cum, e1, e2, e3, rk, d2 = [f32t(n) for n in
                ("lw", "ar", "kk", "k2", "rn", "kp", "bq", "cum", "e1", "e2", "e3", "rk", "d2")]
            t1 = k2
            AT = [sb(p1, "AT%d" % i, [128, 4, 128], BF16) for i in range(2)]
            BT = [sb(p1, "BT%d" % i, [128, 4, 128], BF16) for i in range(2)]
            KT = [sb(p1, "KT%d" % i, [128, 4, 128], BF16) for i in range(2)]
            RT = [sb(p1, "RT%d" % i, [128, 4, 128], BF16) for i in range(3)]
            VT = [sb(p1, "VT%d" % i, [128, 4, 128], BF16) for i in range(2)]
            ATm = [[sb(p1, "ATm%d_%d" % (i, q), [128, 4, 128], BF16) for q in range(2)] for i in range(2)]
            RTm = [[sb(p1, "RTm%d_%d" % (i, q), [128, 4, 128], BF16) for q in range(2)] for i in range(2)]
            cvm = sb(p1, "cvm", [128, 2], F32)
            S.op("pool", lambda e: e.memset(cvm[:], 0.0), writes=["cvm"])
            S.op("pool", lambda e: e.memset(cvm[0:64, 0:1], 1.0), writes=["cvm"])
            S.op("pool", lambda e: e.memset(cvm[64:128, 1:2], 1.0), writes=["cvm"])
            S.op("pool", lambda e: e.memset(a2s[0:64, :], 0.0), writes=["a2s"])
            gam = [sb(p1, "gam%d" % i, [64, 8, 2], F32) for i in range(3)]
            gT = [sb(p1, "gT%d" % i, [128, 4, 128], BF16) for i in range(3)]
            bon = [sb(p1, "bon%d" % i, [128, 4, 128], BF16) for i in range(3)]
            yv1 = f32t("yv0")
            yv = [yv1, yv1]
            TMA = [sb(p1, "TMA%d" % i, [128, 512], BF16) for i in range(2)]
            TMV = [sb(p1, "TMV%d" % i, [128, 512], BF16) for i in range(2)]
            TMBm = [[sb(p1, "TMBm%d_%d" % (i, c), [128, 512], BF16) for c in range(2)] for i in range(2)]
            TMKm = [[sb(p1, "TMKm%d_%d" % (i, c), [128, 512], BF16) for c in range(2)] for i in range(2)]
            Vpad = [sb(p1, "Vpad%d" % i, [128, 8, 128], BF16) for i in range(2)]
            Wpad1 = sb(p1, "Wpad0", [128, 8, 128], BF16)
            Wpad = [Wpad1, Wpad1]
            HpadA1 = sb(p1, "HpadA0", [64, 8, 128], BF16)
            HpadA = [HpadA1, HpadA1]
            HpadB1 = sb(p1, "HpadB0", [64, 8, 128], BF16)
            HpadB = [HpadB1, HpadB1]
            SC1 = [sb(p1, "SC1_%d" % i, [128, 8, 256], BF16) for i in range(2)]
            SC2 = [sb(p1, "SC2_%d" % i, [128, 8, 256], BF16) for i in range(2)]
            SC3 = [sb(p1, "SC3_%d" % i, [128, 8, 128], BF16) for i in range(2)]
            Zb = [sb(p1, "Zb%d" % i, [128, 8, 128], BF16) for i in range(2)]
            PP = [[sb(p1, "PP%d_%d" % (i, k), [128, 2, 8, 128], BF16) for k in range(2)] for i in range(2)]
            Ef = [sb(p1, "Ef%d" % i, [64, 2, 8, 64], BF16) for i in range(2)]
            Hb = sb(p1, "Hb", [64, 8, 64], BF16)
            S.op("pool", lambda e: e.memset(Hb[:], 0.0), writes=["Hb"])
            QTb1 = sb(p1, "QTb0", [64, 8, 128], BF16)
            QTb = [QTb1, QTb1]
            Hf = sb(p1, "Hf", [64, 8, 64], F32)
            for t_, n_ in ((Wpad1, "Wpad0"), (HpadA1, "HpadA0"), (HpadB1, "HpadB0")):
                S.op("pool", lambda e, t_=t_: e.memset(t_[:], 0.0), writes=[n_])
            for s_ in range(2):
                S.op("pool", lambda e, s_=s_: e.memset(Vpad[s_][:], 0.0), writes=["Vpad%d" % s_])
                for c_ in range(2):
                    S.op("pool", lambda e, s_=s_, c_=c_: e.memset(TMBm[s_][c_][:], 0.0), writes=["TMBm%d_%d" % (s_, c_)])
                    S.op("pool", lambda e, s_=s_, c_=c_: e.memset(TMKm[s_][c_][:], 0.0), writes=["TMKm%d_%d" % (s_, c_)])
            S.op("pool", lambda e: e.memset(Hf[:], 0.0), writes=["Hf"])

            def padview(t):
                return bass.AP(t[:].tensor, t[:].offset, [list(t[:].ap[0]), [256, 4], [192, 2], [1, 64]])

            colmap = [512 + c * 128 for c in range(4)] + [1024 + c * 128 for c in range(4)] + [1536] + \
                     [c * 128 for c in range(4)] + [1664]
            XB = (6, 7)

            def prep_gen(i):
                own = i >= NTW - NTO
                att = i >= NTW - NTA
                ia = i - (NTW - NTA)
                b = i % 2
                b3 = i % 3
                ntl = 14 if i >= NTW - NTO - 1 else 9
                S.dma("sp", lambda e: e.dma_start(out=xt[b][:], in_=xw[i * 128:(i + 1) * 128, :]), writes=["xt%d" % b])
                hsrc = hTr[b][:]
                hname = "hTr0"
                norm_transpose(xt[b][:], "xt%d" % b, C_G1, hsrc, hname, XB[0])
                emit_conv(1)
                if att:
                    S.dma("sp", lambda e: e.dma_start(out=hT_s[:, :, ia * 128:(ia + 1) * 128], in_=hTr[b][:]), reads=[hname], writes=["hT_s"])
                yield
                for gi, g0 in enumerate(range(0, ntl, 4)):
                    gn = min(4, ntl - g0)
                    pbk = XB[(gi + 1) % 2]
                    for c in range(gn):
                        co = colmap[g0 + c]
                        for k in range(8):
                            S.op("pe", lambda e, c=c, co=co, k=k: e.matmul(
                                banks[pbk][:, c * 128:(c + 1) * 128], wr[:, k, co:co + 128], hsrc[:, k, :],
                                start=(k == 0), stop=(k == 7)), reads=["wr", hname], writes=["bank%d" % pbk])
                    S.op("act", lambda e: e.copy(
                        PR[:, g0:g0 + gn, 1:129], banks[pbk][:, 0:gn * 128].rearrange("p (c t) -> p c t", t=128)),
                        reads=["bank%d" % pbk], writes=["PR"])
                    yield
                S.op("pool", lambda e: e.tensor_copy(PR[:, :, 0:1], prevcol[:]), reads=["prevcol"], writes=["PR"])
                S.op("pool", lambda e: e.tensor_tensor(PS[:, 0:ntl, :], PR[:, 0:ntl, 0:128], PR[:, 0:ntl, 1:129], ALU.subtract),
                     reads=["PR"], writes=["PS"])
                S.op("pool", lambda e: e.tensor_tensor(PS[:, 0:ntl, :], PS[:, 0:ntl, :],
                                                     cv[:, C_MU:C_MU + ntl].unsqueeze(2).broadcast_to([128, ntl, 128]), ALU.mult),
                     reads=["PS", "cv"], writes=["PS"])
                S.op("pool", lambda e: e.tensor_tensor(PS[:, 0:ntl, :], PS[:, 0:ntl, :], PR[:, 0:ntl, 1:129], ALU.add),
                     reads=["PS", "PR"], writes=["PS"])
                S.op("pool", lambda e: e.tensor_copy(prevcol[:], PR[:, :, 128:129]), reads=["PR"], writes=["prevcol"])
                kS, vS, rS, cgS = PS[:, 0:4, :], PS[:, 4:8, :], PS[:, 9:13, :], PS[:, 13, :]
                yield
                S.op("act", lambda e: e.activation(tcw[:], PS[0:64, 8, :], AF.Tanh), reads=["PS"], writes=["tcw"])
                for p in range(4):
                    S.op("pe", lambda e, p=p: e.matmul(banks[XB[0]][:, p * 128:(p + 1) * 128], w2s[:, p * 128:(p + 1) * 128], tcw[:],
                                                      start=True, stop=True), reads=["w2s", "tcw"], writes=["bank%d" % XB[0]])
                    S.op("pe", lambda e, p=p: e.matmul(banks[XB[1]][:, p * 128:(p + 1) * 128], a2s[:, p * 128:(p + 1) * 128], PS[:, 8, :],
                                                      start=True, stop=True), reads=["a2s", "PS"], writes=["bank%d" % XB[1]])
                for p in range(4):
                    S.op("act", lambda e, p=p: e.activation(lw[:, p, :], banks[XB[0]][:, p * 128:(p + 1) * 128], AF.Sigmoid,
                                                           bias=cv[:, C_W0 + p:C_W0 + p + 1]), reads=["bank%d" % XB[0], "cv"], writes=["lw"])
                    S.op("act", lambda e, p=p: e.activation(ar[:, p, :], banks[XB[1]][:, p * 128:(p + 1) * 128], AF.Sigmoid,
                                                           bias=cv[:, C_A0 + p:C_A0 + p + 1]), reads=["bank%d" % XB[1], "cv"], writes=["ar"])
                if own:
                    S.op("act", lambda e: e.activation(sgc[:], cgS, AF.Sigmoid), reads=["PS"], writes=["sgc"])
                yield
                S.op("dve", lambda e: e.tensor_scalar(lw[:], lw[:], -DEC_C, 0.0, ALU.mult, ALU.add), reads=["lw"], writes=["lw"])
                S.op("dve", lambda e: e.tensor_tensor_scan(cum[:].rearrange("p a t -> p (a t)"), cmask4[:],
                                                         lw[:].rearrange("p a t -> p (a t)"), 0.0, ALU.mult, ALU.add),
                     reads=["lw", "cmask4"], writes=["cum"])
                S.op("act", lambda e: e.activation(e1[:], cum[:], AF.Exp), reads=["cum"], writes=["e1"])
                S.op("act", lambda e: e.activation(e2[:], cum[:], AF.Exp, scale=-1.0), reads=["cum"], writes=["e2"])
                S.op("pool", lambda e: e.tensor_tensor(e3[:], cum[:], lw[:], ALU.subtract), reads=["cum", "lw"], writes=["e3"])
                S.op("act", lambda e: e.activation(e3[:], e3[:], AF.Exp), reads=["e3"], writes=["e3"])
                S.op("dve", lambda e: e.tensor_tensor(kk[:], kS, cv[:, C_KK:C_KK + 4].unsqueeze(2).broadcast_to([128, 4, 128]), ALU.mult),
                     reads=["PS", "cv"], writes=["kk"])
                S.op("pool", lambda e: e.tensor_tensor(k2[:], kk[:], kk[:], ALU.mult), reads=["kk"], writes=["k2"])
                S.op("pe", lambda e: e.matmul(banks[XB[0]][:], bones, k2[:].rearrange("p a t -> p (a t)"), start=True, stop=True),
                     reads=["cm", "k2"], writes=["bank%d" % XB[0]])
                S.op("act", lambda e: e.activation(rn[:].rearrange("p a t -> p (a t)"), banks[XB[0]][:], AF.Sqrt), reads=["bank%d" % XB[0]], writes=["rn"])
                yield
                S.op("dve", lambda e: e.tensor_scalar(rn[:], rn[:], 1e-12, 0.0, ALU.max, ALU.add), reads=["rn"], writes=["rn"])
                S.op("dve", lambda e: e.reciprocal(rn[:], rn[:]), reads=["rn"], writes=["rn"])
                S.op("dve", lambda e: e.tensor_tensor(kk[:], kk[:], rn[:], ALU.mult), reads=["kk", "rn"], writes=["kk"])
                S.op("pool", lambda e: e.tensor_tensor(t1[:], ar[:], cv[:, C_KA:C_KA + 4].unsqueeze(2).broadcast_to([128, 4, 128]), ALU.mult),
                     reads=["ar", "cv"], writes=["k2"])
                S.op("pool", lambda e: e.tensor_tensor(t1[:], t1[:], cv2[:, C2_OMKA:C2_OMKA + 4].unsqueeze(2).broadcast_to([128, 4, 128]), ALU.add),
                     reads=["k2", "cv2"], writes=["k2"])
                S.op("pool", lambda e: e.tensor_tensor(kp[:], kS, t1[:], ALU.mult), reads=["PS", "k2"], writes=["kp"])
                S.op("dve", lambda e: e.tensor_tensor(bq[:], kk[:], ar[:], ALU.mult), reads=["kk", "ar"], writes=["bq"])
                S.op("dve", lambda e: e.scalar_tensor_tensor(AT[b][:], kk[:], -1.0, e3[:], ALU.mult, ALU.mult),
                     reads=["kk", "e3"], writes=["AT%d" % b])
                for q in range(2):
                    S.op("pool", lambda e, q=q: e.tensor_scalar(ATm[b][q][:], AT[b][:], cvm[:, q:q + 1], 0.0, ALU.mult, ALU.add),
                         reads=["AT%d" % b, "cvm"], writes=["ATm%d_%d" % (b, q)])
                S.op("dve", lambda e: e.tensor_tensor(BT[b][:], bq[:], e2[:], ALU.mult), reads=["bq", "e2"], writes=["BT%d" % b])
                S.op("pool", lambda e: e.tensor_tensor(KT[b][:], kp[:], e2[:], ALU.mult), reads=["kp", "e2"], writes=["KT%d" % b])
                S.op("pool", lambda e: e.tensor_copy(VT[b][:], vS), reads=["PS"], writes=["VT%d" % b])
                if own:
                    S.op("dve", lambda e: e.tensor_tensor(RT[b3][:], rS, e1[:], ALU.mult), reads=["PS", "e1"], writes=["RT%d" % b3])
                    for q in range(2):
                        S.op("pool", lambda e, q=q: e.tensor_scalar(RTm[b][q][:], RT[b3][:], cvm[:, q:q + 1], 0.0, ALU.mult, ALU.add),
                             reads=["RT%d" % b3, "cvm"], writes=["RTm%d_%d" % (b, q)])
                for h in range(8):
                    p, q = h // 2, h % 2
                    S.op("pe", lambda e, h=h, p=p, q=q: e.matmul(
                        banks[XB[1]][0:64, h * 2:h * 2 + 2], cm[:, 64 * q:64 * q + 64],
                        e1[:, p, :].rearrange("p (c t) -> p c t", t=64)[:, :, 63], start=True, stop=True),
                        reads=["cm", "e1"], writes=["bank%d" % XB[1]])
                S.op("dve", lambda e: e.tensor_copy(gam[b3][:].rearrange("p h c -> p (h c)"), banks[XB[1]][0:64, 0:16]),
                     reads=["bank%d" % XB[1]], writes=["gam%d" % b3])
                yield
                if own:
                    for p in range(4):
                        S.op("pe", lambda e, p=p: e.matmul(banks[XB[0]][:, p * 128:(p + 1) * 128], g2s[:, p * 128:(p + 1) * 128], sgc[:],
                                                          start=True, stop=True), reads=["g2s", "sgc"], writes=["bank%d" % XB[0]])
                    S.op("act", lambda e: e.copy(gT[b3][:].rearrange("p a t -> p (a t)"), banks[XB[0]][:]), reads=["bank%d" % XB[0]], writes=["gT%d" % b3])
                    S.op("pool", lambda e: e.tensor_tensor(rk[:], rS, kp[:], ALU.mult), reads=["PS", "kp"], writes=["rk"])
                    S.op("pool", lambda e: e.tensor_tensor(rk[:], rk[:], cv2[:, C2_RK:C2_RK + 4].unsqueeze(2).broadcast_to([128, 4, 128]), ALU.mult),
                         reads=["rk", "cv2"], writes=["rk"])
                    S.op("pe", lambda e: e.matmul(banks[XB[1]][:], bones, rk[:].rearrange("p a t -> p (a t)"), start=True, stop=True),
                         reads=["cm", "rk"], writes=["bank%d" % XB[1]])
                    S.op("dve", lambda e: e.tensor_tensor(bon[b3][:], banks[XB[1]][:].rearrange("p (a t) -> p a t", t=128), vS, ALU.mult),
                         reads=["bank%d" % XB[1], "PS"], writes=["bon%d" % b3])
                    yield

            def tile_gen(i):
                s = i % 2
                b = i % 2
                b3 = i % 3
                own = i >= NTW - NTO
                nxt_own = (i + 1) >= NTW - NTO
                io = i - (NTW - NTO)
                Z0, Z1, WB = 3 * s, 3 * s + 1, 3 * s + 2
                zb_of = lambda h: (Z0 if h < 4 else Z1)
                bn = lambda k: "bank%d" % k
                ATn, BTn, KTn, RTn, VTn = "AT%d" % b, "BT%d" % b, "KT%d" % b, "RT%d" % b3, "VT%d" % b
                SC1n, SC2n, SC3n, Zbn = "SC1_%d" % s, "SC2_%d" % s, "SC3_%d" % s, "Zb%d" % s
                sc1, sc2, sc3, zbs = SC1[s], SC2[s], SC3[s], Zb[s]
                tmA, tmV = TMA[s], TMV[s]
                tmAn, tmVn = "TMA%d" % s, "TMV%d" % s
                tmB = TMBm[s]
                tmK = TMKm[s]
                tmBn = ["TMBm%d_%d" % (s, c) for c in range(2)]
                tmKn = ["TMKm%d_%d" % (s, c) for c in range(2)]
                hc = lambda h: slice(h * 64, (h + 1) * 64)
                for (src, sn, bk) in ((AT[b], ATn, Z0), (BT[b], BTn, Z1), (KT[b], KTn, WB)):
                    for p in range(4):
                        S.op("pe", lambda e, src=src, p=p, bk=bk: e.matmul(banks[bk][:, p * 128:(p + 1) * 128], src[:, p, :], idb[:], start=True, stop=True),
                             reads=[sn, "idb"], writes=[bn(bk)])
                S.op("act", lambda e: e.copy(tmA[:], banks[Z0][:]), reads=[bn(Z0)], writes=[tmAn])
                for c in range(2):
                    rs = slice(64 * c, 64 * c + 64)
                    S.op("dve", lambda e, c=c, rs=rs: e.tensor_copy(tmB[c][rs, :], banks[Z1][rs, :]), reads=[bn(Z1)], writes=[tmBn[c]])
                    S.op("act", lambda e, c=c, rs=rs: e.copy(tmK[c][rs, :], banks[WB][rs, :]), reads=[bn(WB)], writes=[tmKn[c]])
                yield
                for p in range(4):
                    S.op("pe", lambda e, p=p: e.matmul(banks[Z0][:, p * 128:(p + 1) * 128], VT[b][:, p, :], idb[:], start=True, stop=True),
                         reads=[VTn, "idb"], writes=[bn(Z0)])
                for h in range(8):
                    p, q = h // 2, h % 2
                    bk = Z1 if h < 4 else WB
                    S.op("pe", lambda e, h=h, p=p, q=q, bk=bk: e.matmul(banks[bk][:, (h % 4) * 128:(h % 4) * 128 + 128], ATm[b][q][:, p, :], BT[b][:, p, :],
                                                                       start=True, stop=True), reads=["ATm%d_%d" % (b, q), BTn], writes=[bn(bk)])
                S.op("act", lambda e: e.copy(tmV[:], banks[Z0][:]), reads=[bn(Z0)], writes=[tmVn])
                S.op("pool", lambda e: e.tensor_copy(padview(Vpad[s]), tmV[:].rearrange("p (a q d) -> p a q d", q=2, d=64)),
                     reads=[tmVn], writes=["Vpad%d" % s])
                for hg, bk in ((0, Z1), (1, WB)):
                    S.op("dve", lambda e, hg=hg, bk=bk: e.tensor_tensor(sc3[:, hg * 4:hg * 4 + 4, :], banks[bk][:].rearrange("p (h t) -> p h t", t=128),
                                                                       mskb[:, 2, :].unsqueeze(1).broadcast_to([128, 4, 128]), ALU.mult),
                         reads=[bn(bk), "mskb"], writes=[SC3n])
                yield
                for (lt, ltn, dst, dstn) in ((BT[b], BTn, sc1, SC1n), (KT[b], KTn, sc2, SC2n)):
                    for part in range(2 if own else 1):
                        for h in range(8):
                            p, q = h // 2, h % 2
                            bk = zb_of(h)
                            rhs = ATm[b][q] if part == 0 else RTm[b][q]
                            rn_ = ("ATm%d_%d" if part == 0 else "RTm%d_%d") % (b, q)
                            S.op("pe", lambda e, h=h, p=p, bk=bk, lt=lt, rhs=rhs: e.matmul(banks[bk][:, (h % 4) * 128:(h % 4) * 128 + 128],
                                                                                         lt[:, p, :], rhs[:, p, :], start=True, stop=True),
                                 reads=[ltn, rn_], writes=[bn(bk)])
                        for hg, bk in ((0, Z0), (1, Z1)):
                            S.op("dve", lambda e, hg=hg, bk=bk, dst=dst, part=part: e.tensor_tensor(
                                dst[:, hg * 4:hg * 4 + 4, part * 128:(part + 1) * 128], banks[bk][:].rearrange("p (h t) -> p h t", t=128),
                                mskb[:, part, :].unsqueeze(1).broadcast_to([128, 4, 128]), ALU.mult),
                                reads=[bn(bk), "mskb"], writes=[dstn])
                        yield
                for h in range(8):
                    S.op("pe", lambda e, h=h: e.matmul(banks[WB][:, hc(h)], sc2[:, h, 0:128], tmV[:, hc(h)], start=True, stop=True),
                         reads=[SC2n, tmVn], writes=[bn(WB)])
                S.op("act", lambda e: e.copy(zbs[:, :, 64:128], banks[WB][:].rearrange("p (h t) -> p h t", t=64)), reads=[bn(WB)], writes=[Zbn])
                S.op("pool", lambda e: e.tensor_copy(zbs[:, :, 0:64], tmA[:].rearrange("p (h t) -> p h t", t=64)), reads=[tmAn], writes=[Zbn])
                yield
                for half in range(2):
                    hs = range(half * 4, half * 4 + 4)
                    hsl = slice(half * 4, half * 4 + 4)
                    S.op("pe", lambda e: e.matmul(banks[Z0][:], idb[:], zbs[:, hsl, :].rearrange("p h t -> p (h t)"),
                                                  start=True, stop=False, skip_group_check=True),
                         reads=["idb", Zbn], writes=[bn(Z0)])
                    Pc, PTc, pn, ptn = sc3, sc1, SC3n, SC1n
                    for lv in range(6):
                        for h in hs:
                            lhs = PTc[:, h, 0:128]
                            S.op("pe", lambda e, h=h, lhs=lhs: e.matmul(banks[Z0][:, (h % 4) * 128:(h % 4) * 128 + 128], lhs, zbs[:, h, :],
                                                                        start=False, stop=(lv == 5), skip_group_check=True),
                                 reads=[ptn, Zbn], writes=[bn(Z0)])
                        if lv < 5:
                            ppb = PP[s][lv % 2]
                            npn = "PP%d_%d" % (s, lv % 2)
                            for h in hs:
                                S.op("pe", lambda e, h=h: e.matmul(banks[Z1][:, (h % 4) * 128:(h % 4) * 128 + 128], PTc[:, h, 0:128], Pc[:, h, 0:128],
                                                                  start=True, stop=True), reads=[pn, ptn], writes=[bn(Z1)])
                            for h in hs:
                                S.op("pe", lambda e, h=h: e.matmul(banks[WB][:, (h % 4) * 128:(h % 4) * 128 + 128], Pc[:, h, 0:128], PTc[:, h, 0:128],
                                                                  start=True, stop=True), reads=[pn, ptn], writes=[bn(WB)])
                        S.op("act", lambda e: e.copy(zbs[:, hsl, :], banks[Z0][:].rearrange("p (h t) -> p h t", t=128)), reads=[bn(Z0)], writes=[Zbn])
                        if lv < 5:
                            S.op("dve", lambda e: e.tensor_copy(ppb[:, 0, hsl, :], banks[Z1][:].rearrange("p (h t) -> p h t", t=128)),
                                 reads=[bn(Z1)], writes=[npn])
                            if lv % 2 == 0:
                                S.op("dve", lambda e: e.tensor_copy(ppb[:, 1, hsl, :], banks[WB][:].rearrange("p (h t) -> p h t", t=128)),
                                     reads=[bn(WB)], writes=[npn])
                            else:
                                S.op("act", lambda e: e.copy(ppb[:, 1, hsl, :], banks[WB][:].rearrange("p (h t) -> p h t", t=128)),
                                     reads=[bn(WB)], writes=[npn])
                            Pc, PTc, pn, ptn = ppb[:, 0], ppb[:, 1], npn, npn
                        yield
                if own:
                    S.op("pool", lambda e: e.tensor_copy(padview(Wpad[s]), zbs[:, :, 64:128].rearrange("p (a q) d -> p a q d", q=2)),
                         reads=[Zbn], writes=["Wpad0"])
                    for h in range(8):
                        p, q = h // 2, h % 2
                        rw = slice(64 * q, 64 * q + 64)
                        qo = banks[zb_of(h)][0:64, (h % 4) * 128:(h % 4) * 128 + 128]
                        S.op("pe", lambda e, h=h, qo=qo: e.matmul(qo, zbs[:, h, 0:64], sc1[:, h, 128:256], start=True, stop=False),
                             reads=[Zbn, SC1n], writes=[bn(zb_of(h))])
                        S.op("pe", lambda e, h=h, p=p, rw=rw, qo=qo: e.matmul(qo, idb[:, rw], RT[b3][:, p, :], start=False, stop=True),
                             reads=["idb", RTn], writes=[bn(zb_of(h))])
                    S.op("act", lambda e: e.copy(QTb[s][:, 0:4, :], banks[Z0][0:64, :].rearrange("p (h t) -> p h t", t=128)), reads=[bn(Z0)], writes=["QTb0"])
                    S.op("dve", lambda e: e.tensor_copy(QTb[s][:, 4:8, :], banks[Z1][0:64, :].rearrange("p (h t) -> p h t", t=128)), reads=[bn(Z1)], writes=["QTb0"])
                    yield
                for c, bk in ((0, Z0), (1, Z1)):
                    for h in range(8):
                        S.op("pe", lambda e, h=h, c=c, bk=bk: e.matmul(banks[bk][0:64, hc(h)], zbs[:, h, 0:64], tmB[c][:, hc(h)], start=True, stop=True),
                             reads=[Zbn, tmBn[c]], writes=[bn(bk)])
                S.op("act", lambda e: e.copy(Ef[s][:, 0], banks[Z0][0:64, :].rearrange("p (h t) -> p h t", t=64)), reads=[bn(Z0)], writes=["Ef%d" % s])
                S.op("dve", lambda e: e.tensor_copy(Ef[s][:, 1], banks[Z1][0:64, :].rearrange("p (h t) -> p h t", t=64)), reads=[bn(Z1)], writes=["Ef%d" % s])
                yield

                def update(c):
                    for h in range(8):
                        ho = banks[WB][0:64, hc(h)]
                        S.op("pe", lambda e, h=h, ho=ho: e.matmul(ho, tmB[c][:, hc(h)], zbs[:, h, 64:128], start=True, stop=False),
                             reads=[tmBn[c], Zbn], writes=[bn(WB)])
                        S.op("pe", lambda e, h=h, ho=ho: e.matmul(ho, tmK[c][:, hc(h)], tmV[:, hc(h)], start=False, stop=False),
                             reads=[tmKn[c], tmVn], writes=[bn(WB)])
                        S.op("pe", lambda e, h=h, ho=ho: e.matmul(ho, Ef[s][:, c, h, :], Hb[:, h, :], start=False, stop=True),
                             reads=["Ef%d" % s, "Hb"], writes=[bn(WB)])
                    S.op("dve", lambda e: e.tensor_tensor(Hf[:], banks[WB][0:64, :].rearrange("p (h t) -> p h t", t=64), Hf[:], ALU.add),
                         reads=[bn(WB), "Hf"], writes=["Hf"])
                    S.op("dve", lambda e: e.tensor_tensor(Hf[:], Hf[:], gam[b3][:, :, c:c + 1].broadcast_to([64, 8, 64]), ALU.mult),
                         reads=["Hf", "gam%d" % b3], writes=["Hf"])
                    S.op("pool", lambda e: e.tensor_copy(Hb[:], Hf[:]), reads=["Hf"], writes=["Hb"])
                update(0)
                if own:
                    S.op("pool", lambda e: e.tensor_copy(padview(HpadB[s]), Hf[:].rearrange("p (a q) d -> p a q d", q=2)),
                         reads=["Hf"], writes=["HpadB0"])
                yield
                if own:
                    for h in range(8):
                        p = h // 2
                        yo = banks[Z0][:, p * 128:(p + 1) * 128]
                        S.op("pe", lambda e, h=h, yo=yo: e.matmul(yo, Wpad[s][:, h, :], sc1[:, h, 128:256], start=(h % 2 == 0), stop=False, skip_group_check=True),
                             reads=["Wpad0", SC1n], writes=[bn(Z0)])
                        S.op("pe", lambda e, h=h, yo=yo: e.matmul(yo, Vpad[s][:, h, :], sc2[:, h, 128:256], start=False, stop=False, skip_group_check=True),
                             reads=["Vpad%d" % s, SC2n], writes=[bn(Z0)])
                        S.op("pe", lambda e, h=h, yo=yo: e.matmul(yo[:, 0:64], HpadA[s][:, h, :], QTb[s][:, h, 0:64], start=False, stop=False, skip_group_check=True),
                             reads=["HpadA0", "QTb0"], writes=[bn(Z0)])
                        S.op("pe", lambda e, h=h, yo=yo: e.matmul(yo[:, 64:128], HpadB[s][:, h, :], QTb[s][:, h, 64:128], start=False, stop=(h % 2 == 1), skip_group_check=True),
                             reads=["HpadB0", "QTb0"], writes=[bn(Z0)])
                    S.op("act", lambda e: e.copy(yv[b][:].rearrange("p a t -> p (a t)"), banks[Z0][:]), reads=[bn(Z0)], writes=["yv0"])
                    yield
                update(1)
                if nxt_own:
                    S.op("pool", lambda e: e.tensor_copy(padview(HpadA[1 - s]), Hf[:].rearrange("p (a q) d -> p a q d", q=2)),
                         reads=["Hf"], writes=["HpadA0"])
                yield
                if own:
                    yvb = yv[b]
                    yvn = "yv0"
                    S.op("pe", lambda e: e.matmul(banks[Z1][:], bones, yvb[:].rearrange("p a t -> p (a t)"), start=True, stop=True),
                         reads=["cm", yvn], writes=[bn(Z1)])
                    S.op("dve", lambda e: e.scalar_tensor_tensor(yvb[:].rearrange("p a t -> p (a t)"), banks[Z1][:], -1.0 / 64,
                                                                yvb[:].rearrange("p a t -> p (a t)"), ALU.mult, ALU.add),
                         reads=[bn(Z1), yvn], writes=[yvn])
                    S.op("pool", lambda e: e.tensor_tensor(d2[:], yvb[:], yvb[:], ALU.mult), reads=[yvn], writes=["d2"])
                    S.op("pe", lambda e: e.matmul(banks[Z0][:], bones, d2[:].rearrange("p a t -> p (a t)"), start=True, stop=True),
                         reads=["cm", "d2"], writes=[bn(Z0)])
                    S.op("act", lambda e: e.activation(d2[:].rearrange("p a t -> p (a t)"), banks[Z0][:], AF.Sqrt, bias=GN_EPS, scale=1.0 / 64),
                         reads=[bn(Z0)], writes=["d2"])
                    yield
                    S.op("dve", lambda e: e.reciprocal(d2[:], d2[:]), reads=["d2"], writes=["d2"])
                    S.op("dve", lambda e: e.tensor_tensor(yvb[:], yvb[:], d2[:], ALU.mult), reads=[yvn, "d2"], writes=[yvn])
                    S.op("pool", lambda e: e.tensor_tensor(yvb[:], yvb[:], cv2[:, C2_LW:C2_LW + 4].unsqueeze(2).broadcast_to([128, 4, 128]), ALU.mult),
                         reads=[yvn, "cv2"], writes=[yvn])
                    S.op("pool", lambda e: e.tensor_tensor(yvb[:], yvb[:], cv2[:, C2_LB:C2_LB + 4].unsqueeze(2).broadcast_to([128, 4, 128]), ALU.add),
                         reads=[yvn, "cv2"], writes=[yvn])
                    S.op("pool", lambda e: e.tensor_tensor(yvb[:], yvb[:], bon[b3][:], ALU.add), reads=[yvn, "bon%d" % b3], writes=[yvn])
                    S.op("pool", lambda e: e.tensor_tensor(y_bT[:, :, io * 128:(io + 1) * 128], yvb[:], gT[b3][:], ALU.mult),
                         reads=[yvn, "gT%d" % b3], writes=["y_bT"])
                    yield

            NTL_ = min(NTW, NTL)
            prep_done = -1
            next_prep = 0
            next_unit = 0
            prep_g = None
            active = []
            steps = {}
            unit_steps = {}
            EARLY = 7
            n_fin = 0
            while True:
                if prep_g is None and next_prep < NTL_ and n_fin >= next_prep - 2 and (
                        next_prep < 2 or n_fin >= next_prep - 1 or unit_steps.get(next_prep - 2, -1) >= EARLY):
                    prep_g = prep_gen(next_prep)
                if prep_g is not None:
                    try:
                        next(prep_g)
                    except StopIteration:
                        prep_done = next_prep
                        next_prep += 1
                        prep_g = None
                for ent in list(active):
                    try:
                        next(ent[1])
                        steps[ent[1]] += 1
                        unit_steps[ent[2]] = steps[ent[1]]
                    except StopIteration:
                        active.remove(ent)
                        n_fin += 1
                if (len(active) < 2 and next_unit < NTL_ and next_unit <= prep_done
                        and all(ent[0] != next_unit % 2 for ent in active)
                        and (not active or steps[active[-1][1]] >= LAG)):
                    g = tile_gen(next_unit)
                    steps[g] = 0
                    unit_steps[next_unit] = 0
                    active.append((next_unit % 2, g, next_unit))
                    next_unit += 1
                if prep_g is None and not active and next_prep >= NTL_ and next_unit >= NTL_:
                    break
        emit_conv(len(conv_tasks))
        S.barrier()
        y_aT = sb(st, "y_aT", [128, 4, OWN], BF16)
        grow = sb(st, "grow", [128, 2, D], F32)
        S.dma("sp", lambda e: e.dma_start(out=grow[:, 0, :], in_=rows[0:1, :].partition_broadcast(128)), writes=["grow"])
        S.dma("sp", lambda e: e.dma_start(out=grow[:, 1, :], in_=rows[1:2, :].partition_broadcast(128)), writes=["grow"])
        mo = sb(st, "mo", [128, D], F32)
        pm = ExitStack()
        hT_att = sb(pm, "hT_att", [128, 8, NTA * 128], BF16)
        S.dma("sp", lambda e: e.dma_start(out=hT_att[:], in_=hT_s), reads=["hT_s"], writes=["hT_att"])
        with ExitStack() as p2:
          if STAGE >= 2:
            wa = sb(p2, "wa", [128, 8, 1536], BF16)
            S.dma("sp", lambda e: e.dma_start(out=wa[:], in_=w_in_b.rearrange("(k p) n -> p k n", p=128)[:, :, 0:1536]),
                  reads=["w_in_b"], writes=["wa"])
            bT = sb(p2, "bT", [128, 5, 8, 128], F32)
            S.dma("sp", lambda e: e.dma_start(out=bT[:].rearrange("p a h t -> p (a h t)"), in_=biasd), writes=["bT"])
            vld = sb(p2, "vld", [128, NTA], F32)
            S.dma("sp", lambda e: e.dma_start(out=vld[:], in_=validd), writes=["vld"])
            qT = sb(p2, "qT", [128, 4, OWN], BF16)
            kTa = sb(p2, "kTa", [128, 4, NTA * 128], BF16)
            Va = sb(p2, "Va", [128, NTA, 8, 65], BF16)
            scf = [sb(p2, "scf%d" % i, [128, 512], F32) for i in range(2)]
            pTb = [sb(p2, "pTb%d" % i, [128, 512], BF16) for i in range(2)]
            rec = sb(p2, "rec", [128, 8], F32)
            ya = sb(p2, "ya", [128, 8, 64], BF16)
            for ia in range(NTA):
                ts_ = slice(ia * 128, (ia + 1) * 128)
                io = ia - 4
                pbk = ia % 2
                for c in range(4):
                    for k in range(8):
                        S.op("pe", lambda e, c=c, k=k, pbk=pbk, ts_=ts_: e.matmul(banks[pbk][:, c * 128:(c + 1) * 128], wa[:, k, 512 + c * 128:512 + (c + 1) * 128],
                                                                                 hT_att[:, k, ts_], start=(k == 0), stop=(k == 7)),
                             reads=["wa", "hT_att"], writes=["bank%d" % pbk])
                S.op("act", lambda e, pbk=pbk, ts_=ts_: e.copy(kTa[:, :, ts_], banks[pbk][:].rearrange("p (c t) -> p c t", t=128)),
                     reads=["bank%d" % pbk], writes=["kTa"])
                pbk2 = 2 + ia % 2
                for k in range(8):
                    S.op("pe", lambda e, k=k, pbk2=pbk2, ts_=ts_: e.matmul(banks[pbk2][:], hT_att[:, k, ts_], wa[:, k, 1024:1536],
                                                                          start=(k == 0), stop=(k == 7)), reads=["wa", "hT_att"], writes=["bank%d" % pbk2])
                S.op("dve", lambda e, pbk2=pbk2, ia=ia: e.tensor_copy(Va[:, ia, :, 0:64], banks[pbk2][:].rearrange("p (h d) -> p h d", d=64)),
                     reads=["bank%d" % pbk2], writes=["Va"])
                S.op("pool", lambda e, ia=ia: e.tensor_copy(Va[:, ia, :, 64:65], vld[:, ia:ia + 1].unsqueeze(1).broadcast_to([128, 8, 1])),
                     reads=["vld"], writes=["Va"])
                if io >= 0:
                    pbk3 = 4 + ia % 2
                    for c in range(4):
                        for k in range(8):
                            S.op("pe", lambda e, c=c, k=k, pbk3=pbk3, ts_=ts_: e.matmul(banks[pbk3][:, c * 128:(c + 1) * 128], wa[:, k, c * 128:(c + 1) * 128],
                                                                                       hT_att[:, k, ts_], start=(k == 0), stop=(k == 7)),
                                 reads=["wa", "hT_att"], writes=["bank%d" % pbk3])
                    S.op("act", lambda e, pbk3=pbk3, io=io: e.copy(qT[:, :, io * 128:(io + 1) * 128], banks[pbk3][:].rearrange("p (c t) -> p c t", t=128)),
                         reads=["bank%d" % pbk3], writes=["qT"])
            for io in range(NTO):
                ia = io + 4
                qs = slice(io * 128, (io + 1) * 128)
                step = 0
                for dl in range(5):
                    kt = ia - dl
                    ks = slice(kt * 128, (kt + 1) * 128)
                    for hg in range(2):
                        pb_ = (step % 2) * 2 + hg
                        sl = step % 2
                        for hh in (0, 2, "sep", 1, 3):
                            if hh == "sep":
                                S.op("pe", lambda e: e.matmul(banks[7][0:64, 0:64], idb[:, 0:64], idb[:, 0:64], start=True, stop=True),
                                     reads=["idb"], writes=["bank7"])
                                continue
                            h = hg * 4 + hh
                            p, q = h // 2, h % 2
                            rw = slice(64 * q, 64 * q + 64)
                            S.op("pe", lambda e, pb_=pb_, hh=hh, rw=rw, p=p, ks=ks: e.matmul(banks[pb_][:, hh * 128:(hh + 1) * 128], kTa[rw, p, ks], qT[rw, p, qs],
                                                                                          start=True, stop=True), reads=["kTa", "qT"], writes=["bank%d" % pb_])
                        S.op("dve", lambda e, pb_=pb_, hg=hg, dl=dl, sl=sl: e.scalar_tensor_tensor(
                            scf[hg][:], banks[pb_][:], 0.125, bT[:, dl, hg * 4:hg * 4 + 4, :].rearrange("p h t -> p (h t)"), ALU.mult, ALU.add),
                            reads=["bank%d" % pb_, "bT"], writes=["scf%d" % hg])
                        S.op("act", lambda e, hg=hg: e.activation(pTb[hg][:], scf[hg][:], AF.Exp), reads=["scf%d" % hg], writes=["pTb%d" % hg])
                        for hh in range(4):
                            h = hg * 4 + hh
                            S.op("pe", lambda e, hg=hg, hh=hh, h=h, kt=kt, dl=dl: e.matmul(banks[4 + hg][:, hh * 65:(hh + 1) * 65], pTb[hg][:, hh * 128:(hh + 1) * 128],
                                                                                       Va[:, kt, h, :], start=(dl == 0 and hh == 0), stop=(dl == 4), skip_group_check=True),
                                 reads=["pTb%d" % hg, "Va"], writes=["bank%d" % (4 + hg)])
                    step += 1
                for hg in range(2):
                    ov = banks[4 + hg][:, 0:260].rearrange("p (h d) -> p h d", d=65)
                    S.op("dve", lambda e, hg=hg, ov=ov: e.reciprocal(rec[:, hg * 4:hg * 4 + 4], ov[:, :, 64]), reads=["bank%d" % (4 + hg)], writes=["rec"])
                    S.op("dve", lambda e, hg=hg, ov=ov: e.tensor_tensor(ya[:, hg * 4:hg * 4 + 4, :], ov[:, :, 0:64],
                                                                       rec[:, hg * 4:hg * 4 + 4].unsqueeze(2).broadcast_to([128, 4, 64]), ALU.mult),
                         reads=["bank%d" % (4 + hg), "rec"], writes=["ya"])
                pb6 = bank_bf(6)
                for p in range(4):
                    S.op("pe", lambda e, p=p: e.transpose(pb6[:, p * 128:(p + 1) * 128], ya[:, 2 * p:2 * p + 2, :].rearrange("p h d -> p (h d)"), idb[:]),
                         reads=["ya", "idb"], writes=["bank6"])
                S.op("act", lambda e, qs=qs: e.copy(y_aT[:, :, qs], pb6[:, 0:512].rearrange("p (c t) -> p c t", t=128)), reads=["bank6"], writes=["y_aT"])
        S.barrier()
        with ExitStack() as p3:
          if STAGE >= 3:
            wg = sb(p3, "wg", [128, 8, 2048], BF16)
            pa = sb(p3, "pa", [128, 4, D], BF16)
            pb = sb(p3, "pb", [128, 4, D], BF16)
            wo = sb(p3, "wo", [128, 8, D], BF16)
            S.dma("sp", lambda e: e.dma_start(out=wg[:], in_=w_in_b.rearrange("(k p) n -> p k n", p=128)[:, :, 3328:5376]), reads=["w_in_b"], writes=["wg"])
            S.dma("sp", lambda e: e.dma_start(out=pa[:], in_=pa_b.rearrange("(k p) n -> p k n", p=128)), reads=["pa_b"], writes=["pa"])
            S.dma("sp", lambda e: e.dma_start(out=pb[:], in_=pb_b.rearrange("(k p) n -> p k n", p=128)), reads=["pb_b"], writes=["pb"])
            S.dma("sp", lambda e: e.dma_start(out=wo[:], in_=wo_b.rearrange("(k p) n -> p k n", p=128)), reads=["wo_b"], writes=["wo"])
            gat = sb(p3, "gat", [128, 16, 512], BF16)
            mT = sb(p3, "mT", [128, 8, 512], BF16)
            ta = sb(p3, "ta", [128, 512], F32)
            tb = sb(p3, "tb", [128, 512], F32)
            for g in range(NG):
                gs = slice(g * 512, (g + 1) * 512)
                hs = slice(512 + g * 512, 512 + (g + 1) * 512)
                for ct in range(16):
                    pbk = ct % 2
                    for k in range(8):
                        S.op("pe", lambda e, ct=ct, k=k, pbk=pbk: e.matmul(banks[pbk][:], wg[:, k, ct * 128:(ct + 1) * 128], hT_att[:, k, hs],
                                                                          start=(k == 0), stop=(k == 7)), reads=["wg", "hT_att"], writes=["bank%d" % pbk])
                    S.op("act", lambda e, ct=ct, pbk=pbk: e.activation(gat[:, ct, :], banks[pbk][:], AF.Sigmoid, bias=cv[:, C_GB + ct:C_GB + ct + 1]),
                         reads=["bank%d" % pbk, "cv"], writes=["gat"])
                for dt_ in range(8):
                    ba, bb = 2 + 2 * (dt_ % 2), 3 + 2 * (dt_ % 2)
                    for k in range(4):
                        S.op("pe", lambda e, dt_=dt_, k=k, ba=ba: e.matmul(banks[ba][:], pa[:, k, dt_ * 128:(dt_ + 1) * 128], y_aT[:, k, gs],
                                                                          start=(k == 0), stop=(k == 3)), reads=["pa", "y_aT"], writes=["bank%d" % ba])
                    for k in range(4):
                        S.op("pe", lambda e, dt_=dt_, k=k, bb=bb: e.matmul(banks[bb][:], pb[:, k, dt_ * 128:(dt_ + 1) * 128], y_bT[:, k, gs],
                                                                          start=(k == 0), stop=(k == 3)), reads=["pb", "y_bT"], writes=["bank%d" % bb])
                    S.op("dve", lambda e, dt_=dt_, ba=ba: e.tensor_tensor(ta[:], banks[ba][:], gat[:, dt_, :], ALU.mult),
                         reads=["bank%d" % ba, "gat"], writes=["ta"])
                    S.op("dve", lambda e, dt_=dt_, bb=bb: e.tensor_tensor(tb[:], banks[bb][:], gat[:, 8 + dt_, :], ALU.mult),
                         reads=["bank%d" % bb, "gat"], writes=["tb"])
                    S.op("pool", lambda e, dt_=dt_: e.tensor_tensor(mT[:, dt_, :], ta[:], tb[:], ALU.add), reads=["ta", "tb"], writes=["mT"])
                for tt in range(4):
                    it = g * 4 + tt
                    b = it % 2
                    S.dma("sp", lambda e, b=b, it=it: e.dma_start(out=xt[b][:], in_=xw[WIN - OWN + it * 128:WIN - OWN + (it + 1) * 128, :]),
                          writes=["xt%d" % b])
                    for half in range(2):
                        pbk = 6 + half
                        for k in range(8):
                            S.op("pe", lambda e, k=k, half=half, pbk=pbk, tt=tt: e.matmul(banks[pbk][:], mT[:, k, tt * 128:(tt + 1) * 128],
                                                                                         wo[:, k, half * 512:(half + 1) * 512], start=(k == 0), stop=(k == 7)),
                                 reads=["mT", "wo"], writes=["bank%d" % pbk])
                        S.op("act", lambda e, half=half, pbk=pbk: e.copy(mo[:, half * 512:(half + 1) * 512], banks[pbk][:]), reads=["bank%d" % pbk], writes=["mo"])
                    post_norm_res(S, nc, mo, "mo", xt[b], "xt%d" % b, grow[:, 0, :], junk, ssq, xt[b], "xt%d" % b)
                    S.dma("sp", lambda e, b=b, it=it: e.dma_start(out=x1s[it * 128:(it + 1) * 128, :], in_=xt[b][:]), reads=["xt%d" % b], writes=["x1s"])
        pm.close()
        S.barrier()
        with ExitStack() as p4:
          if STAGE >= 4:
            wus = [sb(p4, "wus%d" % i, [128, 8, 1024], BF16) for i in range(2)]
            wds = [sb(p4, "wds%d" % i, [128, 8, 1024], BF16) for i in range(2)]
            x1g = sb(p4, "x1g", [128, 4, D], F32)
            hfT = sb(p4, "hfT", [128, 8, 512], BF16)
            acc = sb(p4, "acc", [128, 4, D], F32)
            act_ = sb(p4, "act_", [128, 8, 512], BF16)
            rl = [sb(p4, "rl%d" % i, [128, 512], F32) for i in range(2)]
            sl_i = 0
            for g in range(NG):
                for tt in range(4):
                    it = g * 4 + tt
                    S.dma("sp", lambda e, tt=tt, it=it: e.dma_start(out=x1g[:, tt, :], in_=x1s[it * 128:(it + 1) * 128, :]), reads=["x1s"], writes=["x1g"])
                    norm_transpose(x1g[:, tt, :], "x1g", C_G3, hfT[:, :, tt * 128:(tt + 1) * 128], "hfT", 0)
                for s in range(4):
                    slot = sl_i % 2
                    sl_i += 1
                    S.dma("sp", lambda e, s=s, slot=slot: e.dma_start(out=wus[slot][:], in_=wu_b.rearrange("(k p) n -> p k n", p=128)[:, :, s * 1024:(s + 1) * 1024]),
                          reads=["wu_b"], writes=["wus%d" % slot])
                    S.dma("sp", lambda e, s=s, slot=slot: e.dma_start(out=wds[slot][:], in_=wd_b[s * 1024:(s + 1) * 1024, :].rearrange("(f p) n -> p f n", p=128)),
                          reads=["wd_b"], writes=["wds%d" % slot])
                    for f in range(8):
                        pbk = 1 + f % 2
                        for k in range(8):
                            S.op("pe", lambda e, f=f, k=k, pbk=pbk, slot=slot: e.matmul(banks[pbk][:], wus[slot][:, k, f * 128:(f + 1) * 128], hfT[:, k, :],
                                                                                       start=(k == 0), stop=(k == 7)), reads=["wus%d" % slot, "hfT"], writes=["bank%d" % pbk])
                        S.op("act", lambda e, f=f, pbk=pbk: e.activation(rl[f % 2][:], banks[pbk][:], AF.Relu), reads=["bank%d" % pbk], writes=["rl%d" % (f % 2)])
                        S.op("pool", lambda e, f=f: e.tensor_tensor(act_[:, f, :], rl[f % 2][:], rl[f % 2][:], ALU.mult), reads=["rl%d" % (f % 2)], writes=["act_"])
                    for tt in range(4):
                        for half in range(2):
                            pbk = 3 + (tt % 2) * 2 + half
                            for f in range(8):
                                S.op("pe", lambda e, f=f, tt=tt, half=half, pbk=pbk, slot=slot: e.matmul(
                                    banks[pbk][:], act_[:, f, tt * 128:(tt + 1) * 128], wds[slot][:, f, half * 512:(half + 1) * 512],
                                    start=(f == 0), stop=(f == 7)), reads=["act_", "wds%d" % slot], writes=["bank%d" % pbk])
                            if s == 0:
                                S.op("dve", lambda e, tt=tt, half=half, pbk=pbk: e.tensor_copy(acc[:, tt, half * 512:(half + 1) * 512], banks[pbk][:]),
                                     reads=["bank%d" % pbk], writes=["acc"])
                            else:
                                S.op("dve", lambda e, tt=tt, half=half, pbk=pbk: e.tensor_tensor(acc[:, tt, half * 512:(half + 1) * 512], banks[pbk][:],
                                                                                                acc[:, tt, half * 512:(half + 1) * 512], ALU.add),
                                     reads=["bank%d" % pbk, "acc"], writes=["acc"])
                for tt in range(4):
                    it = g * 4 + tt
                    post_norm_res(S, nc, acc[:, tt, :], "acc", x1g[:, tt, :], "x1g", grow[:, 1, :], junk, ssq, mo, "mo")
                    S.dma("sp", lambda e, it=it: e.dma_start(out=out[it * 128:(it + 1) * 128, :], in_=mo[:]), reads=["mo"], writes=["out"])
        if STAGE < 4:
            S.barrier()
            S.dma("sp", lambda e: e.dma_start(out=out[0:128, :], in_=xw[0:128, :]), reads=["w_in_b", "wd_b", "y_bT", "y_aT", "x1s"], writes=["out"])
        S.wait_all("sp", ["out"])
        S.emit()
    return nc


def post_norm_res(S, nc, u, uname, xres, xname, grow, junk, ssq, dst, dname):
    ua = u if isinstance(u, bass.AP) else u[:]
    xa = xres if isinstance(xres, bass.AP) else xres[:]
    da = dst if isinstance(dst, bass.AP) else dst[:]
    S.op("act", lambda e: e.activation(junk[:], ua, AF.Square, accum_out=ssq[:, 0:1]), reads=[uname], writes=["xs", "ssq"])
    S.op("act", lambda e: e.activation(ssq[:, 1:2], ssq[:, 0:1], AF.Sqrt, bias=RMS_EPS, scale=1.0 / D), reads=["ssq"], writes=["ssq"])
    S.op("dve", lambda e: e.reciprocal(ssq[:, 2:3], ssq[:, 1:2]), reads=["ssq"], writes=["ssq"])
    S.op("dve", lambda e: e.scalar_tensor_tensor(ua, ua, ssq[:, 2:3], grow, ALU.mult, ALU.mult), reads=[uname, "ssq", "grow"], writes=[uname])
    S.op("pool", lambda e: e.tensor_tensor(da, ua, xa, ALU.add), reads=[uname, xname], writes=[dname])


def _host_consts(inp):
    f = np.float32
    g = lambda n: np.asarray(inp[n], dtype=f)[0]
    cvec = np.zeros((128, 64), f)
    cvec[:, 0:8] = g("pre_mix_g").reshape(8, 128).T
    cvec[:, 8:16] = g("pre_ffn_g").reshape(8, 128).T
    cvec[:, 16:32] = g("gate_bias").reshape(16, 128).T
    mu = g("shift_mu")
    colmap = [512 + c * 128 for c in range(4)] + [1024 + c * 128 for c in range(4)] + [1536] + [c * 128 for c in range(4)] + [1664]
    for t, co in enumerate(colmap):
        cvec[:, 32 + t] = mu[co:co + 128]
    cvec[:, 46:50] = g("w0").reshape(4, 128).T
    cvec[:, 50:54] = g("a0").reshape(4, 128).T
    cvec[:, 54:58] = g("k_k").reshape(4, 128).T
    cvec[:, 58:62] = g("k_a").reshape(4, 128).T
    cvec2 = np.zeros((128, 16), f)
    cvec2[:, 0:4] = g("r_k").reshape(512).reshape(4, 128).T
    cvec2[:, 4:8] = g("ln_x_w").reshape(4, 128).T
    cvec2[:, 8:12] = g("ln_x_b").reshape(4, 128).T
    rows = np.stack([g("post_mix_g"), g("post_ffn_g")], 0)
    rb = np.concatenate([g("rel_bias"), np.full((8, 1), -1e30, f)], axis=1)
    kj = np.arange(128)[:, None]
    qi = np.arange(128)[None, :]
    bt = np.zeros((128, 5, 8, 128), f)
    for dl in range(5):
        dist = 128 * dl + qi - kj
        idx = np.clip(dist, -63, 256) + 63
        cd = 2 * dl + qi // 64 - kj // 64
        idx = np.where((cd >= 0) & (cd <= 8), idx, 320)
        bt[:, dl, :, :] = rb[:, idx].transpose(1, 0, 2)
    cmat = np.zeros((128, 768), f)
    cmat[:, 0:128] = np.eye(128, dtype=f)
    cmat[0:64, 128:192] = 1.0
    cmat[64:128, 192:256] = 1.0
    s_ = np.arange(64)[:, None]
    t_ = np.arange(64)[None, :]
    for c_ in range(2):
        r_ = slice(64 * c_, 64 * c_ + 64)
        cmat[r_, 256 + 64 * c_:256 + 64 * c_ + 64] = (s_ < t_)
        cmat[r_, 384 + 64 * c_:384 + 64 * c_ + 64] = (s_ <= t_)
        cmat[r_, 512 + 64 * c_:512 + 64 * c_ + 64] = (t_ < s_)
    return dict(cvec=cvec, cvec2=cvec2, rows=rows, biasT=bt.reshape(128, 5 * 8 * 128), cmat=cmat,
                w2=g("w2"), a2=g("a2"), g2=g("g2"), w_in=g("w_in"), proj_a=g("proj_a"), proj_b=g("proj_b"),
                w_out=g("w_out"), w_up=g("w_up"), w_down=g("w_down"))


_NC_CACHE = {}


def run(inputs, trace=False):
    x = np.asarray(inputs["x"], dtype=np.float32)
    B, SEQ, _ = x.shape
    OWN = SEQ // 4
    WIN = SEQ
    NTA = OWN // 128 + 4
    consts = _host_consts(inputs)
    in_maps = []
    for c in range(8):
        b, j = c // 4, c % 4
        end = (j + 1) * OWN
        xw = np.zeros((WIN, D), np.float32)
        xw[WIN - end:, :] = x[b, 0:end, :]
        pos = end - NTA * 128 + np.arange(NTA * 128)
        valid = (pos >= 0).astype(np.float32).reshape(NTA, 128).T.copy()
        m = dict(consts)
        m["xw"] = xw
        m["valid"] = valid
        in_maps.append(m)
    key = (WIN, OWN)
    if key not in _NC_CACHE:
        _NC_CACHE[key] = build_nc(WIN, OWN)
    nc = _NC_CACHE[key]
    res = run_bass_kernel_spmd(nc, in_maps, core_ids=list(range(8)))
    outp = np.zeros((B, SEQ, D), np.float32)
    for c in range(8):
        b, j = c // 4, c % 4
        outp[b, j * OWN:(j + 1) * OWN, :] = res.results[c]["out"]
    return outp


def kernel(**inputs):
    return run(inputs)
```

```python
import numpy as np
import ml_dtypes
import concourse.bass as bass
import concourse.mybir as mybir
from concourse.bass_utils import run_bass_kernel_spmd
from contextlib import ExitStack

F32 = mybir.dt.float32
BF16 = mybir.dt.bfloat16
AF = mybir.ActivationFunctionType
ALU = mybir.AluOpType

D = 1024
N_DMA_SEMS = 24
RMS_EPS = 1e-6
GN_EPS = 64 * 1e-5
DEC_C = float(np.exp(-0.5))


class _Rec:
    def __getattr__(self, name):
        def f(*a, **k):
            self.call = (name, a, k)
        return f


class Sched:
    ENGS = ("pe", "act", "dve", "pool", "sp")

    def __init__(self, nc, stack):
        self.nc = nc
        self.prog = {e: [] for e in self.ENGS}
        self.cnt = {e: 0 for e in self.ENGS}
        self.sems = {}
        for e in self.ENGS:
            self.sems["p_" + e] = stack.enter_context(nc.semaphore("prog_" + e))
        for i in range(N_DMA_SEMS):
            self.sems["d%d" % i] = stack.enter_context(nc.semaphore("dma%d" % i))
        self.dcnt = [0] * N_DMA_SEMS
        self.dnx = {"sp": 0, "pool": 0, "act": 0}
        self.seen = {e: {} for e in self.ENGS}
        self.lastw = {}
        self.reads = {}

    def _deps(self, eng, reads, writes):
        deps = {}
        for b in reads:
            for s, v in self.lastw.get(b, {}).items():
                if deps.get(s, 0) < v:
                    deps[s] = v
        for b in writes:
            for s, v in self.lastw.get(b, {}).items():
                if deps.get(s, 0) < v:
                    deps[s] = v
            for s, v in self.reads.get(b, {}).items():
                if deps.get(s, 0) < v:
                    deps[s] = v
        waits = []
        for s, v in deps.items():
            if s == "p_pe" and eng == "pe":
                continue
            if self.seen[eng].get(s, 0) < v:
                self.seen[eng][s] = v
                waits.append((s, v))
        return waits

    def _commit(self, tok, reads, writes):
        s, v = tok
        for b in writes:
            self.lastw.setdefault(b, {})[s] = v
        for b in reads:
            self.reads.setdefault(b, {})[s] = v

    @staticmethod
    def _rec(fn):
        r = _Rec()
        fn(r)
        return r.call

    def op(self, eng, fn, reads=(), writes=()):
        fn = self._rec(fn)
        waits = self._deps(eng, reads, writes)
        self.cnt[eng] += 1
        tok = ("p_" + eng, self.cnt[eng])
        self._commit(tok, reads, writes)
        self.prog[eng].append((waits, fn, ("p_" + eng, 1)))

    def dma(self, eng, fn, reads=(), writes=()):
        fn = self._rec(fn)
        half = N_DMA_SEMS // 2
        base = 0 if eng == "sp" else half
        i = base + self.dnx[eng]
        self.dnx[eng] = (self.dnx[eng] + 1) % half
        waits = self._deps(eng, reads, writes)
        s = "d%d" % i
        if self.dcnt[i] > 0 and self.seen[eng].get(s, 0) < self.dcnt[i]:
            self.seen[eng][s] = self.dcnt[i]
            waits.append((s, self.dcnt[i]))
        self.dcnt[i] += 16
        self._commit((s, self.dcnt[i]), reads, writes)
        self.prog[eng].append((waits, fn, (s, 16)))

    def barrier(self):
        cur = {"p_" + e: self.cnt[e] for e in self.ENGS}
        for i in range(N_DMA_SEMS):
            cur["d%d" % i] = self.dcnt[i]
        for eng in self.ENGS:
            waits = []
            for s_, v in cur.items():
                if s_ == "p_" + eng and eng == "pe":
                    continue
                if v > 0 and self.seen[eng].get(s_, 0) < v:
                    self.seen[eng][s_] = v
                    waits.append((s_, v))
            self.prog[eng].append((waits, None, None))

    def wait_all(self, eng, bufs):
        waits = self._deps(eng, bufs, ())
        self.prog[eng].append((waits, None, None))

    def emit(self):
        nc = self.nc
        with nc.Block() as block:
            def replay(name, e):
                for waits, fn, inc in self.prog[name]:
                    for s, v in waits:
                        e.wait_ge(self.sems[s], v)
                    if fn is None:
                        continue
                    getattr(e, fn[0])(*fn[1], **fn[2]).then_inc(self.sems[inc[0]], inc[1])

            @block.tensor
            def _(e):
                replay("pe", e)

            @block.scalar
            def _(e):
                replay("act", e)

            @block.vector
            def _(e):
                replay("dve", e)

            @block.gpsimd
            def _(e):
                replay("pool", e)

            @block.sync
            def _(e):
                replay("sp", e)


import os
STAGE = int(os.environ.get("STAGE", "9"))
SUB = int(os.environ.get("SUB", "99"))
NTL = int(os.environ.get("NTL", "9999"))
SUB2 = int(os.environ.get("SUB2", "99"))
LAG = int(os.environ.get("LAG", "7"))
CONV_EARLY = int(os.environ.get("CONV_EARLY", "1"))


def build_nc(WIN, OWN, dbg=False):
    NTW = WIN // 128
    NTO = OWN // 128
    NTA = NTO + 4
    NG = OWN // 512
    nc = bass.Bass("TRN2", target_bir_lowering=False)

    def din(name, shape, dt=F32):
        return nc.dram_tensor(name, list(shape), dt, kind="ExternalInput").ap()

    xw = din("xw", [WIN, D])
    w_in = din("w_in", [D, 5376])
    proj_a = din("proj_a", [512, D])
    proj_b = din("proj_b", [512, D])
    w_out = din("w_out", [D, D])
    w_up = din("w_up", [D, 4096])
    w_down = din("w_down", [4096, D])
    cvec = din("cvec", [128, 64])
    rows = din("rows", [2, D])
    w2d = din("w2", [64, 512])
    a2d = din("a2", [64, 512])
    g2d = din("g2", [128, 512])
    biasd = din("biasT", [128, 5 * 8 * 128])
    validd = din("valid", [128, NTA])
    cmat = din("cmat", [128, 6 * 128])
    out = nc.dram_tensor("out", [OWN, D], F32, kind="ExternalOutput").ap()
    dbg_o = {}

    w_in_b = nc.dram_tensor("w_in_b", [D, 5376], BF16).ap()
    pa_b = nc.dram_tensor("pa_b", [512, D], BF16).ap()
    pb_b = nc.dram_tensor("pb_b", [512, D], BF16).ap()
    wo_b = nc.dram_tensor("wo_b", [D, D], BF16).ap()
    wu_b = nc.dram_tensor("wu_b", [D, 4096], BF16).ap()
    wd_b = nc.dram_tensor("wd_b", [4096, D], BF16).ap()
    x1s = nc.dram_tensor("x1s", [OWN, D], F32).ap()
    hT_s = nc.dram_tensor("hT_s", [128, 8, NTA * 128], BF16).ap()

    C_G1, C_G3, C_GB, C_MU = 0, 8, 16, 32
    C_W0, C_A0, C_KK, C_KA = 46, 50, 54, 58
    cvec2 = din("cvec2", [128, 16])
    C2_RK, C2_LW, C2_LB, C2_OMKA = 0, 4, 8, 12

    with ExitStack() as st:
        S = Sched(nc, st)

        def sb(stack, name, shape, dt):
            return stack.enter_context(nc.sbuf_tensor(name, list(shape), dt))

        def psb(stack, name, shape, dt=F32):
            return stack.enter_context(nc.psum_tensor(name, list(shape), dt))

        cv = sb(st, "cv", [128, 64], F32)
        cv2 = sb(st, "cv2", [128, 16], F32)
        cm = sb(st, "cm", [128, 768], F32)
        idb = sb(st, "idb", [128, 128], BF16)
        mskb = sb(st, "mskb", [128, 3, 128], BF16)
        S.dma("sp", lambda e: e.dma_start(out=cv[:], in_=cvec), writes=["cv"])
        S.dma("sp", lambda e: e.dma_start(out=cv2[:], in_=cvec2), writes=["cv2"])
        S.dma("sp", lambda e: e.dma_start(out=cm[:], in_=cmat), writes=["cm"])
        ident = cm[:, 0:128]
        bones = cm[:, 128:256]
        scanm = cm[:, 640:768]
        S.op("dve", lambda e: e.tensor_copy(idb[:], ident), reads=["cm"], writes=["idb"])
        S.op("dve", lambda e: e.tensor_copy(mskb[:], cm[:, 256:640].rearrange("p (a b) -> p a b", b=128)),
             reads=["cm"], writes=["mskb"])

        def conv(dst, src, nrows, nm):
            for r in range(0, nrows, 128):
                S.dma("pool", lambda e, r=r: e.dma_start(out=dst[r:r + 128, :], in_=src[r:r + 128, :]), writes=[nm])
        conv(w_in_b, w_in, D, "w_in_b")
        conv_tasks = []
        for dst_, src_, nrows_, nm_ in ((pa_b, proj_a, 512, "pa_b"), (pb_b, proj_b, 512, "pb_b"), (wo_b, w_out, D, "wo_b"),
                                       (wu_b, w_up, D, "wu_b"), (wd_b, w_down, 4096, "wd_b")):
            for r_ in range(0, nrows_, 128):
                conv_tasks.append((dst_, src_, r_, nm_))

        def emit_conv(k):
            for _ in range(k):
                if conv_tasks:
                    dst_, src_, r_, nm_ = conv_tasks.pop(0)
                    S.dma("pool", lambda e: e.dma_start(out=dst_[r_:r_ + 128, :], in_=src_[r_:r_ + 128, :]), writes=[nm_])
        if CONV_EARLY:
            emit_conv(len(conv_tasks))

        y_bT = sb(st, "y_bT", [128, 4, OWN], BF16)
        xt = [sb(st, "xt%d" % i, [128, D], F32) for i in range(2)]
        xs = sb(st, "xs", [128, D], BF16)
        junk = xs
        ssq = sb(st, "ssq", [128, 4], F32)

        banks = [psb(st, "bank%d" % i, [128, 512]) for i in range(8)]

        def bank_bf(i):
            return banks[i][:].bitcast(BF16)

        def norm_transpose(src_tile, srcname, gcol, dst_ap, dstname, pbank):
            S.op("act", lambda e: e.activation(junk[:], src_tile, AF.Square, accum_out=ssq[:, 0:1]),
                 reads=[srcname], writes=["xs", "ssq"])
            S.op("act", lambda e: e.activation(ssq[:, 1:2], ssq[:, 0:1], AF.Sqrt, bias=RMS_EPS, scale=1.0 / D),
                 reads=["ssq"], writes=["ssq"])
            S.op("dve", lambda e: e.reciprocal(ssq[:, 2:3], ssq[:, 1:2]), reads=["ssq"], writes=["ssq"])
            S.op("dve", lambda e: e.tensor_scalar(xs[:], src_tile, ssq[:, 2:3], 0.0, ALU.mult, ALU.add),
                 reads=[srcname, "ssq"], writes=["xs"])
            pb = bank_bf(pbank)
            for k in range(8):
                S.op("pe", lambda e, k=k: e.transpose(pb[:, k * 128:(k + 1) * 128], xs[:, k * 128:(k + 1) * 128], idb[:]),
                     reads=["xs", "idb"], writes=["bank%d" % pbank])
            S.op("dve", lambda e: e.tensor_tensor(dst_ap, pb[:, 0:1024].rearrange("p (k t) -> p k t", t=128),
                                                 cv[:, gcol:gcol + 8].unsqueeze(2).broadcast_to([128, 8, 128]), ALU.mult),
                 reads=["bank%d" % pbank, "cv"], writes=[dstname])

        with ExitStack() as p1:
          if STAGE >= 1:
            wr = sb(p1, "wr", [128, 8, 1792], BF16)
            S.dma("sp", lambda e: e.dma_start(out=wr[:], in_=w_in_b.rearrange("(k p) n -> p k n", p=128)[:, :, 1536:3328]),
                  reads=["w_in_b"], writes=["wr"])
            w2s = sb(p1, "w2s", [64, 512], F32)
            a2s = sb(p1, "a2s", [128, 512], F32)
            g2s = sb(p1, "g2s", [128, 512], BF16)
            S.dma("sp", lambda e: e.dma_start(out=w2s[:], in_=w2d), writes=["w2s"])
            S.dma("sp", lambda e: e.dma_start(out=a2s[64:128, :], in_=a2d), writes=["a2s"])
            S.dma("pool", lambda e: e.dma_start(out=g2s[:], in_=g2d), writes=["g2s"])
            hTr1 = sb(p1, "hTr0", [128, 8, 128], BF16)
            hTr = [hTr1, hTr1]
            PR = sb(p1, "PR", [128, 14, 129], F32)
            prevcol = sb(p1, "prevcol", [128, 14, 1], F32)
            S.op("pool", lambda e: e.memset(PR[:], 0.0), writes=["PR"])
            S.op("pool", lambda e: e.memset(prevcol[:], 0.0), writes=["prevcol"])
            PS = sb(p1, "PS", [128, 14, 128], F32)

            def f32t(name):
                return sb(p1, name, [128, 4, 128], F32)
            tcw = sb(p1, "tcw", [64, 128], F32)
            cmask4 = sb(p1, "cmask4", [128, 512], F32)
            S.op("pool", lambda e: e.memset(cmask4[:], 1.0), writes=["cmask4"])
            S.op("pool", lambda e: e.memset(cmask4[:].rearrange("p (c t) -> p c t", t=64)[:, :, 0:1], 0.0), writes=["cmask4"])
            S.op("dve", lambda e: e.tensor_scalar(cv2[:, 12:16], cv[:, C_KA:C_KA + 4], -1.0, 1.0, ALU.mult, ALU.add), reads=["cv"], writes=["cv2"])
            sgc = sb(p1, "sgc", [128, 128], BF16)
            lw, ar, kk, k2, rn, kp, bq, cum, e1, e2, e3, rk, d2 = [f32t(n) for n in
                ("lw", "ar", "kk", "k2", "rn", "kp", "bq", "cum", "e1", "e2", "e3", "rk", "d2")]
            t1 = k2
            AT = [sb(p1, "AT%d" % i, [128, 4, 128], BF16) for i in range(2)]
            BT = [sb(p1, "BT%d" % i, [128, 4, 128], BF16) for i in range(2)]
            KT = [sb(p1, "KT%d" % i, [128, 4, 128], BF16) for i in range(2)]
            RT = [sb(p1, "RT%d" % i, [128, 4, 128], BF16) for i in range(3)]
            VT = [sb(p1, "VT%d" % i, [128, 4, 128], BF16) for i in range(2)]
            ATm = [[sb(p1, "ATm%d_%d" % (i, q), [128, 4, 128], BF16) for q in range(2)] for i in range(2)]
            RTm = [[sb(p1, "RTm%d_%d" % (i, q), [128, 4, 128], BF16) for q in range(2)] for i in range(2)]
            cvm = sb(p1, "cvm", [128, 2], F32)
            S.op("pool", lambda e: e.memset(cvm[:], 0.0), writes=["cvm"])
            S.op("pool", lambda e: e.memset(cvm[0:64, 0:1], 1.0), writes=["cvm"])
            S.op("pool", lambda e: e.memset(cvm[64:128, 1:2], 1.0), writes=["cvm"])
            S.op("pool", lambda e: e.memset(a2s[0:64, :], 0.0), writes=["a2s"])
            gam = [sb(p1, "gam%d" % i, [64, 8, 2], F32) for i in range(3)]
            gT = [sb(p1, "gT%d" % i, [128, 4, 128], BF16) for i in range(3)]
            bon = [sb(p1, "bon%d" % i, [128, 4, 128], BF16) for i in range(3)]
            yv1 = f32t("yv0")
            yv = [yv1, yv1]
            TMA = [sb(p1, "TMA%d" % i, [128, 512], BF16) for i in range(2)]
            TMV = [sb(p1, "TMV%d" % i, [128, 512], BF16) for i in range(2)]
            TMBm = [[sb(p1, "TMBm%d_%d" % (i, c), [128, 512], BF16) for c in range(2)] for i in range(2)]
            TMKm = [[sb(p1, "TMKm%d_%d" % (i, c), [128, 512], BF16) for c in range(2)] for i in range(2)]
            Vpad = [sb(p1, "Vpad%d" % i, [128, 8, 128], BF16) for i in range(2)]
            Wpad1 = sb(p1, "Wpad0", [128, 8, 128], BF16)
            Wpad = [Wpad1, Wpad1]
            HpadA1 = sb(p1, "HpadA0", [64, 8, 128], BF16)
            HpadA = [HpadA1, HpadA1]
            HpadB1 = sb(p1, "HpadB0", [64, 8, 128], BF16)
            HpadB = [HpadB1, HpadB1]
            SC1 = [sb(p1, "SC1_%d" % i, [128, 8, 256], BF16) for i in range(2)]
            SC2 = [sb(p1, "SC2_%d" % i, [128, 8, 256], BF16) for i in range(2)]
            SC3 = [sb(p1, "SC3_%d" % i, [128, 8, 128], BF16) for i in range(2)]
            Zb = [sb(p1, "Zb%d" % i, [128, 8, 128], BF16) for i in range(2)]
            PP = [[sb(p1, "PP%d_%d" % (i, k), [128, 2, 8, 128], BF16) for k in range(2)] for i in range(2)]
            Ef = [sb(p1, "Ef%d" % i, [64, 2, 8, 64], BF16) for i in range(2)]
            Hb = sb(p1, "Hb", [64, 8, 64], BF16)
            S.op("pool", lambda e: e.memset(Hb[:], 0.0), writes=["Hb"])
            QTb1 = sb(p1, "QTb0", [64, 8, 128], BF16)
            QTb = [QTb1, QTb1]
            Hf = sb(p1, "Hf", [64, 8, 64], F32)
            for t_, n_ in ((Wpad1, "Wpad0"), (HpadA1, "HpadA0"), (HpadB1, "HpadB0")):
                S.op("pool", lambda e, t_=t_: e.memset(t_[:], 0.0), writes=[n_])
            for s_ in range(2):
                S.op("pool", lambda e, s_=s_: e.memset(Vpad[s_][:], 0.0), writes=["Vpad%d" % s_])
                for c_ in range(2):
                    S.op("pool", lambda e, s_=s_, c_=c_: e.memset(TMBm[s_][c_][:], 0.0), writes=["TMBm%d_%d" % (s_, c_)])
                    S.op("pool", lambda e, s_=s_, c_=c_: e.memset(TMKm[s_][c_][:], 0.0), writes=["TMKm%d_%d" % (s_, c_)])
            S.op("pool", lambda e: e.memset(Hf[:], 0.0), writes=["Hf"])

            def padview(t):
                return bass.AP(t[:].tensor, t[:].offset, [list(t[:].ap[0]), [256, 4], [192, 2], [1, 64]])

            colmap = [512 + c * 128 for c in range(4)] + [1024 + c * 128 for c in range(4)] + [1536] + \
                     [c * 128 for c in range(4)] + [1664]
            XB = (6, 7)

            def prep_gen(i):
                own = i >= NTW - NTO
                att = i >= NTW - NTA
                ia = i - (NTW - NTA)
                b = i % 2
                b3 = i % 3
                ntl = 14 if i >= NTW - NTO - 1 else 9
                S.dma("sp", lambda e: e.dma_start(out=xt[b][:], in_=xw[i * 128:(i + 1) * 128, :]), writes=["xt%d" % b])
                hsrc = hTr[b][:]
                hname = "hTr0"
                norm_transpose(xt[b][:], "xt%d" % b, C_G1, hsrc, hname, XB[0])
                emit_conv(1)
                if att:
                    S.dma("sp", lambda e: e.dma_start(out=hT_s[:, :, ia * 128:(ia + 1) * 128], in_=hTr[b][:]), reads=[hname], writes=["hT_s"])
                yield
                for gi, g0 in enumerate(range(0, ntl, 4)):
                    gn = min(4, ntl - g0)
                    pbk = XB[(gi + 1) % 2]
                    for c in range(gn):
                        co = colmap[g0 + c]
                        for k in range(8):
                            S.op("pe", lambda e, c=c, co=co, k=k: e.matmul(
                                banks[pbk][:, c * 128:(c + 1) * 128], wr[:, k, co:co + 128], hsrc[:, k, :],
                                start=(k == 0), stop=(k == 7)), reads=["wr", hname], writes=["bank%d" % pbk])
                    S.op("act", lambda e: e.copy(
                        PR[:, g0:g0 + gn, 1:129], banks[pbk][:, 0:gn * 128].rearrange("p (c t) -> p c t", t=128)),
                        reads=["bank%d" % pbk], writes=["PR"])
                    yield
                S.op("pool", lambda e: e.tensor_copy(PR[:, :, 0:1], prevcol[:]), reads=["prevcol"], writes=["PR"])
                S.op("pool", lambda e: e.tensor_tensor(PS[:, 0:ntl, :], PR[:, 0:ntl, 0:128], PR[:, 0:ntl, 1:129], ALU.subtract),
                     reads=["PR"], writes=["PS"])
                S.op("pool", lambda e: e.tensor_tensor(PS[:, 0:ntl, :], PS[:, 0:ntl, :],
                                                     cv[:, C_MU:C_MU + ntl].unsqueeze(2).broadcast_to([128, ntl, 128]), ALU.mult),
                     reads=["PS", "cv"], writes=["PS"])
                S.op("pool", lambda e: e.tensor_tensor(PS[:, 0:ntl, :], PS[:, 0:ntl, :], PR[:, 0:ntl, 1:129], ALU.add),
                     reads=["PS", "PR"], writes=["PS"])
                S.op("pool", lambda e: e.tensor_copy(prevcol[:], PR[:, :, 128:129]), reads=["PR"], writes=["prevcol"])
                kS, vS, rS, cgS = PS[:, 0:4, :], PS[:, 4:8, :], PS[:, 9:13, :], PS[:, 13, :]
                yield
                S.op("act", lambda e: e.activation(tcw[:], PS[0:64, 8, :], AF.Tanh), reads=["PS"], writes=["tcw"])
                for p in range(4):
                    S.op("pe", lambda e, p=p: e.matmul(banks[XB[0]][:, p * 128:(p + 1) * 128], w2s[:, p * 128:(p + 1) * 128], tcw[:],
                                                      start=True, stop=True), reads=["w2s", "tcw"], writes=["bank%d" % XB[0]])
                    S.op("pe", lambda e, p=p: e.matmul(banks[XB[1]][:, p * 128:(p + 1) * 128], a2s[:, p * 128:(p + 1) * 128], PS[:, 8, :],
                                                      start=True, stop=True), reads=["a2s", "PS"], writes=["bank%d" % XB[1]])
                for p in range(4):
                    S.op("act", lambda e, p=p: e.activation(lw[:, p, :], banks[XB[0]][:, p * 128:(p + 1) * 128], AF.Sigmoid,
                                                           bias=cv[:, C_W0 + p:C_W0 + p + 1]), reads=["bank%d" % XB[0], "cv"], writes=["lw"])
                    S.op("act", lambda e, p=p: e.activation(ar[:, p, :], banks[XB[1]][:, p * 128:(p + 1) * 128], AF.Sigmoid,
                                                           bias=cv[:, C_A0 + p:C_A0 + p + 1]), reads=["bank%d" % XB[1], "cv"], writes=["ar"])
                if own:
                    S.op("act", lambda e: e.activation(sgc[:], cgS, AF.Sigmoid), reads=["PS"], writes=["sgc"])
                yield
                S.op("dve", lambda e: e.tensor_scalar(lw[:], lw[:], -DEC_C, 0.0, ALU.mult, ALU.add), reads=["lw"], writes=["lw"])
                S.op("dve", lambda e: e.tensor_tensor_scan(cum[:].rearrange("p a t -> p (a t)"), cmask4[:],
                                                         lw[:].rearrange("p a t -> p (a t)"), 0.0, ALU.mult, ALU.add),
                     reads=["lw", "cmask4"], writes=["cum"])
                S.op("act", lambda e: e.activation(e1[:], cum[:], AF.Exp), reads=["cum"], writes=["e1"])
                S.op("act", lambda e: e.activation(e2[:], cum[:], AF.Exp, scale=-1.0), reads=["cum"], writes=["e2"])
                S.op("pool", lambda e: e.tensor_tensor(e3[:], cum[:], lw[:], ALU.subtract), reads=["cum", "lw"], writes=["e3"])
                S.op("act", lambda e: e.activation(e3[:], e3[:], AF.Exp), reads=["e3"], writes=["e3"])
                S.op("dve", lambda e: e.tensor_tensor(kk[:], kS, cv[:, C_KK:C_KK + 4].unsqueeze(2).broadcast_to([128, 4, 128]), ALU.mult),
                     reads=["PS", "cv"], writes=["kk"])
                S.op("pool", lambda e: e.tensor_tensor(k2[:], kk[:], kk[:], ALU.mult), reads=["kk"], writes=["k2"])
                S.op("pe", lambda e: e.matmul(banks[XB[0]][:], bones, k2[:].rearrange("p a t -> p (a t)"), start=True, stop=True),
                     reads=["cm", "k2"], writes=["bank%d" % XB[0]])
                S.op("act", lambda e: e.activation(rn[:].rearrange("p a t -> p (a t)"), banks[XB[0]][:], AF.Sqrt), reads=["bank%d" % XB[0]], writes=["rn"])
                yield
                S.op("dve", lambda e: e.tensor_scalar(rn[:], rn[:], 1e-12, 0.0, ALU.max, ALU.add), reads=["rn"], writes=["rn"])
                S.op("dve", lambda e: e.reciprocal(rn[:], rn[:]), reads=["rn"], writes=["rn"])
                S.op("dve", lambda e: e.tensor_tensor(kk[:], kk[:], rn[:], ALU.mult), reads=["kk", "rn"], writes=["kk"])
                S.op("pool", lambda e: e.tensor_tensor(t1[:], ar[:], cv[:, C_KA:C_KA + 4].unsqueeze(2).broadcast_to([128, 4, 128]), ALU.mult),
                     reads=["ar", "cv"], writes=["k2"])
                S.op("pool", lambda e: e.tensor_tensor(t1[:], t1[:], cv2[:, C2_OMKA:C2_OMKA + 4].unsqueeze(2).broadcast_to([128, 4, 128]), ALU.add),
                     reads=["k2", "cv2"], writes=["k2"])
                S.op("pool", lambda e: e.tensor_tensor(kp[:], kS, t1[:], ALU.mult), reads=["PS", "k2"], writes=["kp"])
                S.op("dve", lambda e: e.tensor_tensor(bq[:], kk[:], ar[:], ALU.mult), reads=["kk", "ar"], writes=["bq"])
                S.op("dve", lambda e: e.scalar_tensor_tensor(AT[b][:], kk[:], -1.0, e3[:], ALU.mult, ALU.mult),
                     reads=["kk", "e3"], writes=["AT%d" % b])
                for q in range(2):
                    S.op("pool", lambda e, q=q: e.tensor_scalar(ATm[b][q][:], AT[b][:], cvm[:, q:q + 1], 0.0, ALU.mult, ALU.add),
                         reads=["AT%d" % b, "cvm"], writes=["ATm%d_%d" % (b, q)])
                S.op("dve", lambda e: e.tensor_tensor(BT[b][:], bq[:], e2[:], ALU.mult), reads=["bq", "e2"], writes=["BT%d" % b])
                S.op("pool", lambda e: e.tensor_tensor(KT[b][:], kp[:], e2[:], ALU.mult), reads=["kp", "e2"], writes=["KT%d" % b])
                S.op("pool", lambda e: e.tensor_copy(VT[b][:], vS), reads=["PS"], writes=["VT%d" % b])
                if own:
                    S.op("dve", lambda e: e.tensor_tensor(RT[b3][:], rS, e1[:], ALU.mult), reads=["PS", "e1"], writes=["RT%d" % b3])
                    for q in range(2):
                        S.op("pool", lambda e, q=q: e.tensor_scalar(RTm[b][q][:], RT[b3][:], cvm[:, q:q + 1], 0.0, ALU.mult, ALU.add),
                             reads=["RT%d" % b3, "cvm"], writes=["RTm%d_%d" % (b, q)])
                for h in range(8):
                    p, q = h // 2, h % 2
                    S.op("pe", lambda e, h=h, p=p, q=q: e.matmul(
                        banks[XB[1]][0:64, h * 2:h * 2 + 2], cm[:, 64 * q:64 * q + 64],
                        e1[:, p, :].rearrange("p (c t) -> p c t", t=64)[:, :, 63], start=True, stop=True),
                        reads=["cm", "e1"], writes=["bank%d" % XB[1]])
                S.op("dve", lambda e: e.tensor_copy(gam[b3][:].rearrange("p h c -> p (h c)"), banks[XB[1]][0:64, 0:16]),
                     reads=["bank%d" % XB[1]], writes=["gam%d" % b3])
                yield
                if own:
                    for p in range(4):
                        S.op("pe", lambda e, p=p: e.matmul(banks[XB[0]][:, p * 128:(p + 1) * 128], g2s[:, p * 128:(p + 1) * 128], sgc[:],
                                                          start=True, stop=True), reads=["g2s", "sgc"], writes=["bank%d" % XB[0]])
                    S.op("act", lambda e: e.copy(gT[b3][:].rearrange("p a t -> p (a t)"), banks[XB[0]][:]), reads=["bank%d" % XB[0]], writes=["gT%d" % b3])
                    S.op("pool", lambda e: e.tensor_tensor(rk[:], rS, kp[:], ALU.mult), reads=["PS", "kp"], writes=["rk"])
                    S.op("pool", lambda e: e.tensor_tensor(rk[:], rk[:], cv2[:, C2_RK:C2_RK + 4].unsqueeze(2).broadcast_to([128, 4, 128]), ALU.mult),
                         reads=["rk", "cv2"], writes=["rk"])
                    S.op("pe", lambda e: e.matmul(banks[XB[1]][:], bones, rk[:].rearrange("p a t -> p (a t)"), start=True, stop=True),
                         reads=["cm", "rk"], writes=["bank%d" % XB[1]])
                    S.op("dve", lambda e: e.tensor_tensor(bon[b3][:], banks[XB[1]][:].rearrange("p (a t) -> p a t", t=128), vS, ALU.mult),
                         reads=["bank%d" % XB[1], "PS"], writes=["bon%d" % b3])
                    yield

            def tile_gen(i):
                s = i % 2
                b = i % 2
                b3 = i % 3
                own = i >= NTW - NTO
                nxt_own = (i + 1) >= NTW - NTO
                io = i - (NTW - NTO)
                Z0, Z1, WB = 3 * s, 3 * s + 1, 3 * s + 2
                zb_of = lambda h: (Z0 if h < 4 else Z1)
                bn = lambda k: "bank%d" % k
                ATn, BTn, KTn, RTn, VTn = "AT%d" % b, "BT%d" % b, "KT%d" % b, "RT%d" % b3, "VT%d" % b
                SC1n, SC2n, SC3n, Zbn = "SC1_%d" % s, "SC2_%d" % s, "SC3_%d" % s, "Zb%d" % s
                sc1, sc2, sc3, zbs = SC1[s], SC2[s], SC3[s], Zb[s]
                tmA, tmV = TMA[s], TMV[s]
                tmAn, tmVn = "TMA%d" % s, "TMV%d" % s
                tmB = TMBm[s]
                tmK = TMKm[s]
                tmBn = ["TMBm%d_%d" % (s, c) for c in range(2)]
                tmKn = ["TMKm%d_%d" % (s, c) for c in range(2)]
                hc = lambda h: slice(h * 64, (h + 1) * 64)
                for (src, sn, bk) in ((AT[b], ATn, Z0), (BT[b], BTn, Z1), (KT[b], KTn, WB)):
                    for p in range(4):
                        S.op("pe", lambda e, src=src, p=p, bk=bk: e.matmul(banks[bk][:, p * 128:(p + 1) * 128], src[:, p, :], idb[:], start=True, stop=True),
                             reads=[sn, "idb"], writes=[bn(bk)])
                S.op("act", lambda e: e.copy(tmA[:], banks[Z0][:]), reads=[bn(Z0)], writes=[tmAn])
                for c in range(2):
                    rs = slice(64 * c, 64 * c + 64)
                    S.op("dve", lambda e, c=c, rs=rs: e.tensor_copy(tmB[c][rs, :], banks[Z1][rs, :]), reads=[bn(Z1)], writes=[tmBn[c]])
                    S.op("act", lambda e, c=c, rs=rs: e.copy(tmK[c][rs, :], banks[WB][rs, :]), reads=[bn(WB)], writes=[tmKn[c]])
                yield
                for p in range(4):
                    S.op("pe", lambda e, p=p: e.matmul(banks[Z0][:, p * 128:(p + 1) * 128], VT[b][:, p, :], idb[:], start=True, stop=True),
                         reads=[VTn, "idb"], writes=[bn(Z0)])
                for h in range(8):
                    p, q = h // 2, h % 2
                    bk = Z1 if h < 4 else WB
                    S.op("pe", lambda e, h=h, p=p, q=q, bk=bk: e.matmul(banks[bk][:, (h % 4) * 128:(h % 4) * 128 + 128], ATm[b][q][:, p, :], BT[b][:, p, :],
                                                                       start=True, stop=True), reads=["ATm%d_%d" % (b, q), BTn], writes=[bn(bk)])
                S.op("act", lambda e: e.copy(tmV[:], banks[Z0][:]), reads=[bn(Z0)], writes=[tmVn])
                S.op("pool", lambda e: e.tensor_copy(padview(Vpad[s]), tmV[:].rearrange("p (a q d) -> p a q d", q=2, d=64)),
                     reads=[tmVn], writes=["Vpad%d" % s])
                for hg, bk in ((0, Z1), (1, WB)):
                    S.op("dve", lambda e, hg=hg, bk=bk: e.tensor_tensor(sc3[:, hg * 4:hg * 4 + 4, :], banks[bk][:].rearrange("p (h t) -> p h t", t=128),
                                                                       mskb[:, 2, :].unsqueeze(1).broadcast_to([128, 4, 128]), ALU.mult),
                         reads=[bn(bk), "mskb"], writes=[SC3n])
                yield
                for (lt, ltn, dst, dstn) in ((BT[b], BTn, sc1, SC1n), (KT[b], KTn, sc2, SC2n)):
                    for part in range(2 if own else 1):
                        for h in range(8):
                            p, q = h // 2, h % 2
                            bk = zb_of(h)
                            rhs = ATm[b][q] if part == 0 else RTm[b][q]
                            rn_ = ("ATm%d_%d" if part == 0 else "RTm%d_%d") % (b, q)
                            S.op("pe", lambda e, h=h, p=p, bk=bk, lt=lt, rhs=rhs: e.matmul(banks[bk][:, (h % 4) * 128:(h % 4) * 128 + 128],
                                                                                         lt[:, p, :], rhs[:, p, :], start=True, stop=True),
                                 reads=[ltn, rn_], writes=[bn(bk)])
                        for hg, bk in ((0, Z0), (1, Z1)):
                            S.op("dve", lambda e, hg=hg, bk=bk, dst=dst, part=part: e.tensor_tensor(
                                dst[:, hg * 4:hg * 4 + 4, part * 128:(part + 1) * 128], banks[bk][:].rearrange("p (h t) -> p h t", t=128),
                                mskb[:, part, :].unsqueeze(1).broadcast_to([128, 4, 128]), ALU.mult),
                                reads=[bn(bk), "mskb"], writes=[dstn])
                        yield
                for h in range(8):
                    S.op("pe", lambda e, h=h: e.matmul(banks[WB][:, hc(h)], sc2[:, h, 0:128], tmV[:, hc(h)], start=True, stop=True),
                         reads=[SC2n, tmVn], writes=[bn(WB)])
                S.op("act", lambda e: e.copy(zbs[:, :, 64:128], banks[WB][:].rearrange("p (h t) -> p h t", t=64)), reads=[bn(WB)], writes=[Zbn])
                S.op("pool", lambda e: e.tensor_copy(zbs[:, :, 0:64], tmA[:].rearrange("p (h t) -> p h t", t=64)), reads=[tmAn], writes=[Zbn])
                yield
                for half in range(2):
                    hs = range(half * 4, half * 4 + 4)
                    hsl = slice(half * 4, half * 4 + 4)
                    S.op("pe", lambda e: e.matmul(banks[Z0][:], idb[:], zbs[:, hsl, :].rearrange("p h t -> p (h t)"),
                                                  start=True, stop=False, skip_group_check=True),
                         reads=["idb", Zbn], writes=[bn(Z0)])
                    Pc, PTc, pn, ptn = sc3, sc1, SC3n, SC1n
                    for lv in range(6):
                        for h in hs:
                            lhs = PTc[:, h, 0:128]
                            S.op("pe", lambda e, h=h, lhs=lhs: e.matmul(banks[Z0][:, (h % 4) * 128:(h % 4) * 128 + 128], lhs, zbs[:, h, :],
                                                                        start=False, stop=(lv == 5), skip_group_check=True),
                                 reads=[ptn, Zbn], writes=[bn(Z0)])
                        if lv < 5:
                            ppb = PP[s][lv % 2]
                            npn = "PP%d_%d" % (s, lv % 2)
                            for h in hs:
                                S.op("pe", lambda e, h=h: e.matmul(banks[Z1][:, (h % 4) * 128:(h % 4) * 128 + 128], PTc[:, h, 0:128], Pc[:, h, 0:128],
                                                                  start=True, stop=True), reads=[pn, ptn], writes=[bn(Z1)])
                            for h in hs:
                                S.op("pe", lambda e, h=h: e.matmul(banks[WB][:, (h % 4) * 128:(h % 4) * 128 + 128], Pc[:, h, 0:128], PTc[:, h, 0:128],
                                                                  start=True, stop=True), reads=[pn, ptn], writes=[bn(WB)])
                        S.op("act", lambda e: e.copy(zbs[:, hsl, :], banks[Z0][:].rearrange("p (h t) -> p h t", t=128)), reads=[bn(Z0)], writes=[Zbn])
                        if lv < 5:
                            S.op("dve", lambda e: e.tensor_copy(ppb[:, 0, hsl, :], banks[Z1][:].rearrange("p (h t) -> p h t", t=128)),
                                 reads=[bn(Z1)], writes=[npn])
                            if lv % 2 == 0:
                                S.op("dve", lambda e: e.tensor_copy(ppb[:, 1, hsl, :], banks[WB][:].rearrange("p (h t) -> p h t", t=128)),
                                     reads=[bn(WB)], writes=[npn])
                            else:
                                S.op("act", lambda e: e.copy(ppb[:, 1, hsl, :], banks[WB][:].rearrange("p (h t) -> p h t", t=128)),
                                     reads=[bn(WB)], writes=[npn])
                            Pc, PTc, pn, ptn = ppb[:, 0], ppb[:, 1], npn, npn
                        yield
                if own:
                    S.op("pool", lambda e: e.tensor_copy(padview(Wpad[s]), zbs[:, :, 64:128].rearrange("p (a q) d -> p a q d", q=2)),
                         reads=[Zbn], writes=["Wpad0"])
                    for h in range(8):
                        p, q = h // 2, h % 2
                        rw = slice(64 * q, 64 * q + 64)
                        qo = banks[zb_of(h)][0:64, (h % 4) * 128:(h % 4) * 128 + 128]
                        S.op("pe", lambda e, h=h, qo=qo: e.matmul(qo, zbs[:, h, 0:64], sc1[:, h, 128:256], start=True, stop=False),
                             reads=[Zbn, SC1n], writes=[bn(zb_of(h))])
                        S.op("pe", lambda e, h=h, p=p, rw=rw, qo=qo: e.matmul(qo, idb[:, rw], RT[b3][:, p, :], start=False, stop=True),
                             reads=["idb", RTn], writes=[bn(zb_of(h))])
                    S.op("act", lambda e: e.copy(QTb[s][:, 0:4, :], banks[Z0][0:64, :].rearrange("p (h t) -> p h t", t=128)), reads=[bn(Z0)], writes=["QTb0"])
                    S.op("dve", lambda e: e.tensor_copy(QTb[s][:, 4:8, :], banks[Z1][0:64, :].rearrange("p (h t) -> p h t", t=128)), reads=[bn(Z1)], writes=["QTb0"])
                    yield
                for c, bk in ((0, Z0), (1, Z1)):
                    for h in range(8):
                        S.op("pe", lambda e, h=h, c=c, bk=bk: e.matmul(banks[bk][0:64, hc(h)], zbs[:, h, 0:64], tmB[c][:, hc(h)], start=True, stop=True),
                             reads=[Zbn, tmBn[c]], writes=[bn(bk)])
                S.op("act", lambda e: e.copy(Ef[s][:, 0], banks[Z0][0:64, :].rearrange("p (h t) -> p h t", t=64)), reads=[bn(Z0)], writes=["Ef%d" % s])
                S.op("dve", lambda e: e.tensor_copy(Ef[s][:, 1], banks[Z1][0:64, :].rearrange("p (h t) -> p h t", t=64)), reads=[bn(Z1)], writes=["Ef%d" % s])
                yield

                def update(c):
                    for h in range(8):
                        ho = banks[WB][0:64, hc(h)]
                        S.op("pe", lambda e, h=h, ho=ho: e.matmul(ho, tmB[c][:, hc(h)], zbs[:, h, 64:128], start=True, stop=False),
                             reads=[tmBn[c], Zbn], writes=[bn(WB)])
                        S.op("pe", lambda e, h=h, ho=ho: e.matmul(ho, tmK[c][:, hc(h)], tmV[:, hc(h)], start=False, stop=False),
                             reads=[tmKn[c], tmVn], writes=[bn(WB)])
                        S.op("pe", lambda e, h=h, ho=ho: e.matmul(ho, Ef[s][:, c, h, :], Hb[:, h, :], start=False, stop=True),
                             reads=["Ef%d" % s, "Hb"], writes=[bn(WB)])
                    S.op("dve", lambda e: e.tensor_tensor(Hf[:], banks[WB][0:64, :].rearrange("p (h t) -> p h t", t=64), Hf[:], ALU.add),
                         reads=[bn(WB), "Hf"], writes=["Hf"])
                    S.op("dve", lambda e: e.tensor_tensor(Hf[:], Hf[:], gam[b3][:, :, c:c + 1].broadcast_to([64, 8, 64]), ALU.mult),
                         reads=["Hf", "gam%d" % b3], writes=["Hf"])
                    S.op("pool", lambda e: e.tensor_copy(Hb[:], Hf[:]), reads=["Hf"], writes=["Hb"])
                update(0)
                if own:
                    S.op("pool", lambda e: e.tensor_copy(padview(HpadB[s]), Hf[:].rearrange("p (a q) d -> p a q d", q=2)),
                         reads=["Hf"], writes=["HpadB0"])
                yield
                if own:
                    for h in range(8):
                        p = h // 2
                        yo = banks[Z0][:, p * 128:(p + 1) * 128]
                        S.op("pe", lambda e, h=h, yo=yo: e.matmul(yo, Wpad[s][:, h, :], sc1[:, h, 128:256], start=(h % 2 == 0), stop=False, skip_group_check=True),
                             reads=["Wpad0", SC1n], writes=[bn(Z0)])
                        S.op("pe", lambda e, h=h, yo=yo: e.matmul(yo, Vpad[s][:, h, :], sc2[:, h, 128:256], start=False, stop=False, skip_group_check=True),
                             reads=["Vpad%d" % s, SC2n], writes=[bn(Z0)])
                        S.op("pe", lambda e, h=h, yo=yo: e.matmul(yo[:, 0:64], HpadA[s][:, h, :], QTb[s][:, h, 0:64], start=False, stop=False, skip_group_check=True),
                             reads=["HpadA0", "QTb0"], writes=[bn(Z0)])
                        S.op("pe", lambda e, h=h, yo=yo: e.matmul(yo[:, 64:128], HpadB[s][:, h, :], QTb[s][:, h, 64:128], start=False, stop=(h % 2 == 1), skip_group_check=True),
                             reads=["HpadB0", "QTb0"], writes=[bn(Z0)])
                    S.op("act", lambda e: e.copy(yv[b][:].rearrange("p a t -> p (a t)"), banks[Z0][:]), reads=[bn(Z0)], writes=["yv0"])
                    yield
                update(1)
                if nxt_own:
                    S.op("pool", lambda e: e.tensor_copy(padview(HpadA[1 - s]), Hf[:].rearrange("p (a q) d -> p a q d", q=2)),
                         reads=["Hf"], writes=["HpadA0"])
                yield
                if own:
                    yvb = yv[b]
                    yvn = "yv0"
                    S.op("pe", lambda e: e.matmul(banks[Z1][:], bones, yvb[:].rearrange("p a t -> p (a t)"), start=True, stop=True),
                         reads=["cm", yvn], writes=[bn(Z1)])
                    S.op("dve", lambda e: e.scalar_tensor_tensor(yvb[:].rearrange("p a t -> p (a t)"), banks[Z1][:], -1.0 / 64,
                                                                yvb[:].rearrange("p a t -> p (a t)"), ALU.mult, ALU.add),
                         reads=[bn(Z1), yvn], writes=[yvn])
                    S.op("pool", lambda e: e.tensor_tensor(d2[:], yvb[:], yvb[:], ALU.mult), reads=[yvn], writes=["d2"])
                    S.op("pe", lambda e: e.matmul(banks[Z0][:], bones, d2[:].rearrange("p a t -> p (a t)"), start=True, stop=True),
                         reads=["cm", "d2"], writes=[bn(Z0)])
                    S.op("act", lambda e: e.activation(d2[:].rearrange("p a t -> p (a t)"), banks[Z0][:], AF.Sqrt, bias=GN_EPS, scale=1.0 / 64),
                         reads=[bn(Z0)], writes=["d2"])
                    yield
                    S.op("dve", lambda e: e.reciprocal(d2[:], d2[:]), reads=["d2"], writes=["d2"])
                    S.op("dve", lambda e: e.tensor_tensor(yvb[:], yvb[:], d2[:], ALU.mult), reads=[yvn, "d2"], writes=[yvn])
                    S.op("pool", lambda e: e.tensor_tensor(yvb[:], yvb[:], cv2[:, C2_LW:C2_LW + 4].unsqueeze(2).broadcast_to([128, 4, 128]), ALU.mult),
                         reads=[yvn, "cv2"], writes=[yvn])
                    S.op("pool", lambda e: e.tensor_tensor(yvb[:], yvb[:], cv2[:, C2_LB:C2_LB + 4].unsqueeze(2).broadcast_to([128, 4, 128]), ALU.add),
                         reads=[yvn, "cv2"], writes=[yvn])
                    S.op("pool", lambda e: e.tensor_tensor(yvb[:], yvb[:], bon[b3][:], ALU.add), reads=[yvn, "bon%d" % b3], writes=[yvn])
                    S.op("pool", lambda e: e.tensor_tensor(y_bT[:, :, io * 128:(io + 1) * 128], yvb[:], gT[b3][:], ALU.mult),
                         reads=[yvn, "gT%d" % b3], writes=["y_bT"])
                    yield

            NTL_ = min(NTW, NTL)
            prep_done = -1
            next_prep = 0
            next_unit = 0
            prep_g = None
            active = []
            steps = {}
            unit_steps = {}
            EARLY = 7
            n_fin = 0
            while True:
                if prep_g is None and next_prep < NTL_ and n_fin >= next_prep - 2 and (
                        next_prep < 2 or n_fin >= next_prep - 1 or unit_steps.get(next_prep - 2, -1) >= EARLY):
                    prep_g = prep_gen(next_prep)
                if prep_g is not None:
                    try:
                        next(prep_g)
                    except StopIteration:
                        prep_done = next_prep
                        next_prep += 1
                        prep_g = None
                for ent in list(active):
                    try:
                        next(ent[1])
                        steps[ent[1]] += 1
                        unit_steps[ent[2]] = steps[ent[1]]
                    except StopIteration:
                        active.remove(ent)
                        n_fin += 1
                if (len(active) < 2 and next_unit < NTL_ and next_unit <= prep_done
                        and all(ent[0] != next_unit % 2 for ent in active)
                        and (not active or steps[active[-1][1]] >= LAG)):
                    g = tile_gen(next_unit)
                    steps[g] = 0
                    unit_steps[next_unit] = 0
                    active.append((next_unit % 2, g, next_unit))
                    next_unit += 1
                if prep_g is None and not active and next_prep >= NTL_ and next_unit >= NTL_:
                    break
        emit_conv(len(conv_tasks))
        S.barrier()
        y_aT = sb(st, "y_aT", [128, 4, OWN], BF16)
        grow = sb(st, "grow", [128, 2, D], F32)
        S.dma("sp", lambda e: e.dma_start(out=grow[:, 0, :], in_=rows[0:1, :].partition_broadcast(128)), writes=["grow"])
        S.dma("sp", lambda e: e.dma_start(out=grow[:, 1, :], in_=rows[1:2, :].partition_broadcast(128)), writes=["grow"])
        mo = sb(st, "mo", [128, D], F32)
        pm = ExitStack()
        hT_att = sb(pm, "hT_att", [128, 8, NTA * 128], BF16)
        S.dma("sp", lambda e: e.dma_start(out=hT_att[:], in_=hT_s), reads=["hT_s"], writes=["hT_att"])
        with ExitStack() as p2:
          if STAGE >= 2:
            wa = sb(p2, "wa", [128, 8, 1536], BF16)
            S.dma("sp", lambda e: e.dma_start(out=wa[:], in_=w_in_b.rearrange("(k p) n -> p k n", p=128)[:, :, 0:1536]),
                  reads=["w_in_b"], writes=["wa"])
            bT = sb(p2, "bT", [128, 5, 8, 128], F32)
            S.dma("sp", lambda e: e.dma_start(out=bT[:].rearrange("p a h t -> p (a h t)"), in_=biasd), writes=["bT"])
            vld = sb(p2, "vld", [128, NTA], F32)
            S.dma("sp", lambda e: e.dma_start(out=vld[:], in_=validd), writes=["vld"])
            qT = sb(p2, "qT", [128, 4, OWN], BF16)
            kTa = sb(p2, "kTa", [128, 4, NTA * 128], BF16)
            Va = sb(p2, "Va", [128, NTA, 8, 65], BF16)
            scf = [sb(p2, "scf%d" % i, [128, 512], F32) for i in range(2)]
            pTb = [sb(p2, "pTb%d" % i, [128, 512], BF16) for i in range(2)]
            rec = sb(p2, "rec", [128, 8], F32)
            ya = sb(p2, "ya", [128, 8, 64], BF16)
            for ia in range(NTA):
                ts_ = slice(ia * 128, (ia + 1) * 128)
                io = ia - 4
                pbk = ia % 2
                for c in range(4):
                    for k in range(8):
                        S.op("pe", lambda e, c=c, k=k, pbk=pbk, ts_=ts_: e.matmul(banks[pbk][:, c * 128:(c + 1) * 128], wa[:, k, 512 + c * 128:512 + (c + 1) * 128],
                                                                                 hT_att[:, k, ts_], start=(k == 0), stop=(k == 7)),
                             reads=["wa", "hT_att"], writes=["bank%d" % pbk])
                S.op("act", lambda e, pbk=pbk, ts_=ts_: e.copy(kTa[:, :, ts_], banks[pbk][:].rearrange("p (c t) -> p c t", t=128)),
                     reads=["bank%d" % pbk], writes=["kTa"])
                pbk2 = 2 + ia % 2
                for k in range(8):
                    S.op("pe", lambda e, k=k, pbk2=pbk2, ts_=ts_: e.matmul(banks[pbk2][:], hT_att[:, k, ts_], wa[:, k, 1024:1536],
                                                                          start=(k == 0), stop=(k == 7)), reads=["wa", "hT_att"], writes=["bank%d" % pbk2])
                S.op("dve", lambda e, pbk2=pbk2, ia=ia: e.tensor_copy(Va[:, ia, :, 0:64], banks[pbk2][:].rearrange("p (h d) -> p h d", d=64)),
                     reads=["bank%d" % pbk2], writes=["Va"])
                S.op("pool", lambda e, ia=ia: e.tensor_copy(Va[:, ia, :, 64:65], vld[:, ia:ia + 1].unsqueeze(1).broadcast_to([128, 8, 1])),
                     reads=["vld"], writes=["Va"])
                if io >= 0:
                    pbk3 = 4 + ia % 2
                    for c in range(4):
                        for k in range(8):
                            S.op("pe", lambda e, c=c, k=k, pbk3=pbk3, ts_=ts_: e.matmul(banks[pbk3][:, c * 128:(c + 1) * 128], wa[:, k, c * 128:(c + 1) * 128],
                                                                                       hT_att[:, k, ts_], start=(k == 0), stop=(k == 7)),
                                 reads=["wa", "hT_att"], writes=["bank%d" % pbk3])
                    S.op("act", lambda e, pbk3=pbk3, io=io: e.copy(qT[:, :, io * 128:(io + 1) * 128], banks[pbk3][:].rearrange("p (c t) -> p c t", t=128)),
                         reads=["bank%d" % pbk3], writes=["qT"])
            for io in range(NTO):
                ia = io + 4
                qs = slice(io * 128, (io + 1) * 128)
                step = 0
                for dl in range(5):
                    kt = ia - dl
                    ks = slice(kt * 128, (kt + 1) * 128)
                    for hg in range(2):
                        pb_ = (step % 2) * 2 + hg
                        sl = step % 2
                        for hh in (0, 2, "sep", 1, 3):
                            if hh == "sep":
                                S.op("pe", lambda e: e.matmul(banks[7][0:64, 0:64], idb[:, 0:64], idb[:, 0:64], start=True, stop=True),
                                     reads=["idb"], writes=["bank7"])
                                continue
                            h = hg * 4 + hh
                            p, q = h // 2, h % 2
                            rw = slice(64 * q, 64 * q + 64)
                            S.op("pe", lambda e, pb_=pb_, hh=hh, rw=rw, p=p, ks=ks: e.matmul(banks[pb_][:, hh * 128:(hh + 1) * 128], kTa[rw, p, ks], qT[rw, p, qs],
                                                                                          start=True, stop=True), reads=["kTa", "qT"], writes=["bank%d" % pb_])
                        S.op("dve", lambda e, pb_=pb_, hg=hg, dl=dl, sl=sl: e.scalar_tensor_tensor(
                            scf[hg][:], banks[pb_][:], 0.125, bT[:, dl, hg * 4:hg * 4 + 4, :].rearrange("p h t -> p (h t)"), ALU.mult, ALU.add),
                            reads=["bank%d" % pb_, "bT"], writes=["scf%d" % hg])
                        S.op("act", lambda e, hg=hg: e.activation(pTb[hg][:], scf[hg][:], AF.Exp), reads=["scf%d" % hg], writes=["pTb%d" % hg])
                        for hh in range(4):
                            h = hg * 4 + hh
                            S.op("pe", lambda e, hg=hg, hh=hh, h=h, kt=kt, dl=dl: e.matmul(banks[4 + hg][:, hh * 65:(hh + 1) * 65], pTb[hg][:, hh * 128:(hh + 1) * 128],
                                                                                       Va[:, kt, h, :], start=(dl == 0 and hh == 0), stop=(dl == 4), skip_group_check=True),
                                 reads=["pTb%d" % hg, "Va"], writes=["bank%d" % (4 + hg)])
                    step += 1
                for hg in range(2):
                    ov = banks[4 + hg][:, 0:260].rearrange("p (h d) -> p h d", d=65)
                    S.op("dve", lambda e, hg=hg, ov=ov: e.reciprocal(rec[:, hg * 4:hg * 4 + 4], ov[:, :, 64]), reads=["bank%d" % (4 + hg)], writes=["rec"])
                    S.op("dve", lambda e, hg=hg, ov=ov: e.tensor_tensor(ya[:, hg * 4:hg * 4 + 4, :], ov[:, :, 0:64],
                                                                       rec[:, hg * 4:hg * 4 + 4].unsqueeze(2).broadcast_to([128, 4, 64]), ALU.mult),
                         reads=["bank%d" % (4 + hg), "rec"], writes=["ya"])
                pb6 = bank_bf(6)
                for p in range(4):
                    S.op("pe", lambda e, p=p: e.transpose(pb6[:, p * 128:(p + 1) * 128], ya[:, 2 * p:2 * p + 2, :].rearrange("p h d -> p (h d)"), idb[:]),
                         reads=["ya", "idb"], writes=["bank6"])
                S.op("act", lambda e, qs=qs: e.copy(y_aT[:, :, qs], pb6[:, 0:512].rearrange("p (c t) -> p c t", t=128)), reads=["bank6"], writes=["y_aT"])
        S.barrier()
        with ExitStack() as p3:
          if STAGE >= 3:
            wg = sb(p3, "wg", [128, 8, 2048], BF16)
            pa = sb(p3, "pa", [128, 4, D], BF16)
            pb = sb(p3, "pb", [128, 4, D], BF16)
            wo = sb(p3, "wo", [128, 8, D], BF16)
            S.dma("sp", lambda e: e.dma_start(out=wg[:], in_=w_in_b.rearrange("(k p) n -> p k n", p=128)[:, :, 3328:5376]), reads=["w_in_b"], writes=["wg"])
            S.dma("sp", lambda e: e.dma_start(out=pa[:], in_=pa_b.rearrange("(k p) n -> p k n", p=128)), reads=["pa_b"], writes=["pa"])
            S.dma("sp", lambda e: e.dma_start(out=pb[:], in_=pb_b.rearrange("(k p) n -> p k n", p=128)), reads=["pb_b"], writes=["pb"])
            S.dma("sp", lambda e: e.dma_start(out=wo[:], in_=wo_b.rearrange("(k p) n -> p k n", p=128)), reads=["wo_b"], writes=["wo"])
            gat = sb(p3, "gat", [128, 16, 512], BF16)
            mT = sb(p3, "mT", [128, 8, 512], BF16)
            ta = sb(p3, "ta", [128, 512], F32)
            tb = sb(p3, "tb", [128, 512], F32)
            for g in range(NG):
                gs = slice(g * 512, (g + 1) * 512)
                hs = slice(512 + g * 512, 512 + (g + 1) * 512)
                for ct in range(16):
                    pbk = ct % 2
                    for k in range(8):
                        S.op("pe", lambda e, ct=ct, k=k, pbk=pbk: e.matmul(banks[pbk][:], wg[:, k, ct * 128:(ct + 1) * 128], hT_att[:, k, hs],
                                                                          start=(k == 0), stop=(k == 7)), reads=["wg", "hT_att"], writes=["bank%d" % pbk])
                    S.op("act", lambda e, ct=ct, pbk=pbk: e.activation(gat[:, ct, :], banks[pbk][:], AF.Sigmoid, bias=cv[:, C_GB + ct:C_GB + ct + 1]),
                         reads=["bank%d" % pbk, "cv"], writes=["gat"])
                for dt_ in range(8):
                    ba, bb = 2 + 2 * (dt_ % 2), 3 + 2 * (dt_ % 2)
                    for k in range(4):
                        S.op("pe", lambda e, dt_=dt_, k=k, ba=ba: e.matmul(banks[ba][:], pa[:, k, dt_ * 128:(dt_ + 1) * 128], y_aT[:, k, gs],
                                                                          start=(k == 0), stop=(k == 3)), reads=["pa", "y_aT"], writes=["bank%d" % ba])
                    for k in range(4):
                        S.op("pe", lambda e, dt_=dt_, k=k, bb=bb: e.matmul(banks[bb][:], pb[:, k, dt_ * 128:(dt_ + 1) * 128], y_bT[:, k, gs],
                                                                          start=(k == 0), stop=(k == 3)), reads=["pb", "y_bT"], writes=["bank%d" % bb])
                    S.op("dve", lambda e, dt_=dt_, ba=ba: e.tensor_tensor(ta[:], banks[ba][:], gat[:, dt_, :], ALU.mult),
                         reads=["bank%d" % ba, "gat"], writes=["ta"])
                    S.op("dve", lambda e, dt_=dt_, bb=bb: e.tensor_tensor(tb[:], banks[bb][:], gat[:, 8 + dt_, :], ALU.mult),
                         reads=["bank%d" % bb, "gat"], writes=["tb"])
                    S.op("pool", lambda e, dt_=dt_: e.tensor_tensor(mT[:, dt_, :], ta[:], tb[:], ALU.add), reads=["ta", "tb"], writes=["mT"])
                for tt in range(4):
                    it = g * 4 + tt
                    b = it % 2
                    S.dma("sp", lambda e, b=b, it=it: e.dma_start(out=xt[b][:], in_=xw[WIN - OWN + it * 128:WIN - OWN + (it + 1) * 128, :]),
                          writes=["xt%d" % b])
                    for half in range(2):
                        pbk = 6 + half
                        for k in range(8):
                            S.op("pe", lambda e, k=k, half=half, pbk=pbk, tt=tt: e.matmul(banks[pbk][:], mT[:, k, tt * 128:(tt + 1) * 128],
                                                                                         wo[:, k, half * 512:(half + 1) * 512], start=(k == 0), stop=(k == 7)),
                                 reads=["mT", "wo"], writes=["bank%d" % pbk])
                        S.op("act", lambda e, half=half, pbk=pbk: e.copy(mo[:, half * 512:(half + 1) * 512], banks[pbk][:]), reads=["bank%d" % pbk], writes=["mo"])
                    post_norm_res(S, nc, mo, "mo", xt[b], "xt%d" % b, grow[:, 0, :], junk, ssq, xt[b], "xt%d" % b)
                    S.dma("sp", lambda e, b=b, it=it: e.dma_start(out=x1s[it * 128:(it + 1) * 128, :], in_=xt[b][:]), reads=["xt%d" % b], writes=["x1s"])
        pm.close()
        S.barrier()
        with ExitStack() as p4:
          if STAGE >= 4:
            wus = [sb(p4, "wus%d" % i, [128, 8, 1024], BF16) for i in range(2)]
            wds = [sb(p4, "wds%d" % i, [128, 8, 1024], BF16) for i in range(2)]
            x1g = sb(p4, "x1g", [128, 4, D], F32)
            hfT = sb(p4, "hfT", [128, 8, 512], BF16)
            acc = sb(p4, "acc", [128, 4, D], F32)
            act_ = sb(p4, "act_", [128, 8, 512], BF16)
            rl = [sb(p4, "rl%d" % i, [128, 512], F32) for i in range(2)]
            sl_i = 0
            for g in range(NG):
                for tt in range(4):
                    it = g * 4 + tt
                    S.dma("sp", lambda e, tt=tt, it=it: e.dma_start(out=x1g[:, tt, :], in_=x1s[it * 128:(it + 1) * 128, :]), reads=["x1s"], writes=["x1g"])
                    norm_transpose(x1g[:, tt, :], "x1g", C_G3, hfT[:, :, tt * 128:(tt + 1) * 128], "hfT", 0)
                for s in range(4):
                    slot = sl_i % 2
                    sl_i += 1
                    S.dma("sp", lambda e, s=s, slot=slot: e.dma_start(out=wus[slot][:], in_=wu_b.rearrange("(k p) n -> p k n", p=128)[:, :, s * 1024:(s + 1) * 1024]),
                          reads=["wu_b"], writes=["wus%d" % slot])
                    S.dma("sp", lambda e, s=s, slot=slot: e.dma_start(out=wds[slot][:], in_=wd_b[s * 1024:(s + 1) * 1024, :].rearrange("(f p) n -> p f n", p=128)),
                          reads=["wd_b"], writes=["wds%d" % slot])
                    for f in range(8):
                        pbk = 1 + f % 2
                        for k in range(8):
                            S.op("pe", lambda e, f=f, k=k, pbk=pbk, slot=slot: e.matmul(banks[pbk][:], wus[slot][:, k, f * 128:(f + 1) * 128], hfT[:, k, :],
                                                                                       start=(k == 0), stop=(k == 7)), reads=["wus%d" % slot, "hfT"], writes=["bank%d" % pbk])
                        S.op("act", lambda e, f=f, pbk=pbk: e.activation(rl[f % 2][:], banks[pbk][:], AF.Relu), reads=["bank%d" % pbk], writes=["rl%d" % (f % 2)])
                        S.op("pool", lambda e, f=f: e.tensor_tensor(act_[:, f, :], rl[f % 2][:], rl[f % 2][:], ALU.mult), reads=["rl%d" % (f % 2)], writes=["act_"])
                    for tt in range(4):
                        for half in range(2):
                            pbk = 3 + (tt % 2) * 2 + half
                            for f in range(8):
                                S.op("pe", lambda e, f=f, tt=tt, half=half, pbk=pbk, slot=slot: e.matmul(
                                    banks[pbk][:], act_[:, f, tt * 128:(tt + 1) * 128], wds[slot][:, f, half * 512:(half + 1) * 512],
                                    start=(f == 0), stop=(f == 7)), reads=["act_", "wds%d" % slot], writes=["bank%d" % pbk])
                            if s == 0:
                                S.op("dve", lambda e, tt=tt, half=half, pbk=pbk: e.tensor_copy(acc[:, tt, half * 512:(half + 1) * 512], banks[pbk][:]),
                                     reads=["bank%d" % pbk], writes=["acc"])
                            else:
                                S.op("dve", lambda e, tt=tt, half=half, pbk=pbk: e.tensor_tensor(acc[:, tt, half * 512:(half + 1) * 512], banks[pbk][:],
                                                                                                acc[:, tt, half * 512:(half + 1) * 512], ALU.add),
                                     reads=["bank%d" % pbk, "acc"], writes=["acc"])
                for tt in range(4):
                    it = g * 4 + tt
                    post_norm_res(S, nc, acc[:, tt, :], "acc", x1g[:, tt, :], "x1g", grow[:, 1, :], junk, ssq, mo, "mo")
                    S.dma("sp", lambda e, it=it: e.dma_start(out=out[it * 128:(it + 1) * 128, :], in_=mo[:]), reads=["mo"], writes=["out"])
        if STAGE < 4:
            S.barrier()
            S.dma("sp", lambda e: e.dma_start(out=out[0:128, :], in_=xw[0:128, :]), reads=["w_in_b", "wd_b", "y_bT", "y_aT", "x1s"], writes=["out"])
        S.wait_all("sp", ["out"])
        S.emit()
    return nc


def post_norm_res(S, nc, u, uname, xres, xname, grow, junk, ssq, dst, dname):
    ua = u if isinstance(u, bass.AP) else u[:]
    xa = xres if isinstance(xres, bass.AP) else xres[:]
    da = dst if isinstance(dst, bass.AP) else dst[:]
    S.op("act", lambda e: e.activation(junk[:], ua, AF.Square, accum_out=ssq[:, 0:1]), reads=[uname], writes=["xs", "ssq"])
    S.op("act", lambda e: e.activation(ssq[:, 1:2], ssq[:, 0:1], AF.Sqrt, bias=RMS_EPS, scale=1.0 / D), reads=["ssq"], writes=["ssq"])
    S.op("dve", lambda e: e.reciprocal(ssq[:, 2:3], ssq[:, 1:2]), reads=["ssq"], writes=["ssq"])
    S.op("dve", lambda e: e.scalar_tensor_tensor(ua, ua, ssq[:, 2:3], grow, ALU.mult, ALU.mult), reads=[uname, "ssq", "grow"], writes=[uname])
    S.op("pool", lambda e: e.tensor_tensor(da, ua, xa, ALU.add), reads=[uname, xname], writes=[dname])


def _host_consts(inp):
    f = np.float32
    g = lambda n: np.asarray(inp[n], dtype=f)[0]
    cvec = np.zeros((128, 64), f)
    cvec[:, 0:8] = g("pre_mix_g").reshape(8, 128).T
    cvec[:, 8:16] = g("pre_ffn_g").reshape(8, 128).T
    cvec[:, 16:32] = g("gate_bias").reshape(16, 128).T
    mu = g("shift_mu")
    colmap = [512 + c * 128 for c in range(4)] + [1024 + c * 128 for c in range(4)] + [1536] + [c * 128 for c in range(4)] + [1664]
    for t, co in enumerate(colmap):
        cvec[:, 32 + t] = mu[co:co + 128]
    cvec[:, 46:50] = g("w0").reshape(4, 128).T
    cvec[:, 50:54] = g("a0").reshape(4, 128).T
    cvec[:, 54:58] = g("k_k").reshape(4, 128).T
    cvec[:, 58:62] = g("k_a").reshape(4, 128).T
    cvec2 = np.zeros((128, 16), f)
    cvec2[:, 0:4] = g("r_k").reshape(512).reshape(4, 128).T
    cvec2[:, 4:8] = g("ln_x_w").reshape(4, 128).T
    cvec2[:, 8:12] = g("ln_x_b").reshape(4, 128).T
    rows = np.stack([g("post_mix_g"), g("post_ffn_g")], 0)
    rb = np.concatenate([g("rel_bias"), np.full((8, 1), -1e30, f)], axis=1)
    kj = np.arange(128)[:, None]
    qi = np.arange(128)[None, :]
    bt = np.zeros((128, 5, 8, 128), f)
    for dl in range(5):
        dist = 128 * dl + qi - kj
        idx = np.clip(dist, -63, 256) + 63
        cd = 2 * dl + qi // 64 - kj // 64
        idx = np.where((cd >= 0) & (cd <= 8), idx, 320)
        bt[:, dl, :, :] = rb[:, idx].transpose(1, 0, 2)
    cmat = np.zeros((128, 768), f)
    cmat[:, 0:128] = np.eye(128, dtype=f)
    cmat[0:64, 128:192] = 1.0
    cmat[64:128, 192:256] = 1.0
    s_ = np.arange(64)[:, None]
    t_ = np.arange(64)[None, :]
    for c_ in range(2):
        r_ = slice(64 * c_, 64 * c_ + 64)
        cmat[r_, 256 + 64 * c_:256 + 64 * c_ + 64] = (s_ < t_)
        cmat[r_, 384 + 64 * c_:384 + 64 * c_ + 64] = (s_ <= t_)
        cmat[r_, 512 + 64 * c_:512 + 64 * c_ + 64] = (t_ < s_)
    return dict(cvec=cvec, cvec2=cvec2, rows=rows, biasT=bt.reshape(128, 5 * 8 * 128), cmat=cmat,
                w2=g("w2"), a2=g("a2"), g2=g("g2"), w_in=g("w_in"), proj_a=g("proj_a"), proj_b=g("proj_b"),
                w_out=g("w_out"), w_up=g("w_up"), w_down=g("w_down"))


_NC_CACHE = {}


def run(inputs, trace=False):
    x = np.asarray(inputs["x"], dtype=np.float32)
    B, SEQ, _ = x.shape
    OWN = SEQ // 4
    WIN = SEQ
    NTA = OWN // 128 + 4
    consts = _host_consts(inputs)
    in_maps = []
    for c in range(8):
        b, j = c // 4, c % 4
        end = (j + 1) * OWN
        xw = np.zeros((WIN, D), np.float32)
        xw[WIN - end:, :] = x[b, 0:end, :]
        pos = end - NTA * 128 + np.arange(NTA * 128)
        valid = (pos >= 0).astype(np.float32).reshape(NTA, 128).T.copy()
        m = dict(consts)
        m["xw"] = xw
        m["valid"] = valid
        in_maps.append(m)
    key = (WIN, OWN)
    if key not in _NC_CACHE:
        _NC_CACHE[key] = build_nc(WIN, OWN)
    nc = _NC_CACHE[key]
    res = run_bass_kernel_spmd(nc, in_maps, core_ids=list(range(8)))
    outp = np.zeros((B, SEQ, D), np.float32)
    for c in range(8):
        b, j = c // 4, c % 4
        outp[b, j * OWN:(j + 1) * OWN, :] = res.results[c]["out"]
    return outp


def kernel(**inputs):
    return run(inputs)
```
